# Optimizing a Trainium2 kernel written in Bass

```python
import math
import jax, jax.numpy as jnp
from jax import lax
import numpy as np

D_MODEL = 2048
BATCH = 4
SEQ = 2048
DEPTH = 4
DEC_BATCH = 8
DEC_SEQ = 4
PAST_LEN = 16384
PAGE_SIZE = 128

N_MIXERS = 3
N_SSD_LAYERS = (DEPTH + 2) // 3
N_MOBA_LAYERS = (DEPTH + 1) // 3
N_GMLP_LAYERS = DEPTH // 3
NORM_EPS = 1e-6

SSD_EXPAND = 2
SSD_INNER = SSD_EXPAND * D_MODEL
SSD_HEAD_DIM = 64
SSD_HEADS = SSD_INNER // SSD_HEAD_DIM
SSD_GROUPS = 8
SSD_HPG = SSD_HEADS // SSD_GROUPS
SSD_STATE = 128
SSD_CONV = 4
SSD_CHUNK = 128
SSD_CONV_DIM = SSD_INNER + 2 * SSD_GROUPS * SSD_STATE
SSD_IN_DIM = SSD_INNER + SSD_CONV_DIM + SSD_HEADS

MOBA_HEADS = 16
MOBA_HEAD_DIM = D_MODEL // MOBA_HEADS
MOBA_BLOCK = 256
MOBA_TOPK = 3
MOBA_Q_BLOCK = 8
ROPE_THETA = 500000.0
ROPE_DIM = MOBA_HEAD_DIM // 4

GMLP_DIM = D_MODEL
GMLP_GROUPS = 16
GMLP_GROUP_DIM = GMLP_DIM // GMLP_GROUPS
GMLP_CHUNK = 128

FFN_HIDDEN = ((8 * D_MODEL + 3 * 256 - 1) // (3 * 256)) * 256

kernel_name = 'hybrid_ssd_moba_gmlp_decode_step'


def rms_norm(x, g):
    xf = x.astype(jnp.float32)
    y = xf * lax.rsqrt(jnp.mean(xf * xf, axis=-1, keepdims=True) + NORM_EPS)
    return (y * g.astype(jnp.float32)).astype(x.dtype)


def swiglu_ffn(h, w_gate, w_up, w_down):
    return (jax.nn.silu(h @ w_gate) * (h @ w_up)) @ w_down


def partial_rotary(x, pos):
    half = ROPE_DIM // 2
    inv_freq = ROPE_THETA ** (-jnp.arange(half, dtype=jnp.float32) * 2.0 / ROPE_DIM)
    ang = pos.astype(jnp.float32)[:, None] * inv_freq[None, :]
    cos = jnp.cos(ang)[None, :, None, :]
    sin = jnp.sin(ang)[None, :, None, :]
    xf = x.astype(jnp.float32)
    x1 = xf[..., :half]
    x2 = xf[..., half:ROPE_DIM]
    out = jnp.concatenate([x1 * cos - x2 * sin, x1 * sin + x2 * cos, xf[..., ROPE_DIM:]], axis=-1)
    return out.astype(x.dtype)


def causal_depthwise_conv(xbc_ext, w, b):
    c = xbc_ext.shape[-1]
    out = lax.conv_general_dilated(xbc_ext, w[:, None, :].astype(xbc_ext.dtype), window_strides=(1,), padding='VALID', dimension_numbers=('NWC', 'WIO', 'NWC'), feature_group_count=c)
    return out + b


def ssd_scan(x, dt, a, bm, cm, h0):
    b_, l_, g_, r_, p_ = x.shape
    n_ = bm.shape[-1]
    q_ = math.gcd(l_, SSD_CHUNK)
    nc = l_ // q_
    x = x.reshape(b_, nc, q_, g_, r_, p_)
    dt = dt.reshape(b_, nc, q_, g_, r_)
    bm = bm.reshape(b_, nc, q_, g_, n_)
    cm = cm.reshape(b_, nc, q_, g_, n_)
    acum = jnp.cumsum(dt * a, axis=2)
    causal = jnp.tril(jnp.ones((q_, q_), bool))[None, None, :, :, None, None]
    seg = acum[:, :, :, None] - acum[:, :, None, :]
    decay = jnp.exp(jnp.where(causal, seg, -jnp.inf))
    cb = jnp.einsum('bcign,bcjgn->bcijg', cm, bm)
    y_diag = jnp.einsum('bcijg,bcijgr,bcjgr,bcjgrp->bcigrp', cb, decay, dt, x)
    decay_end = jnp.exp(acum[:, :, -1:] - acum)
    states = jnp.einsum('bcjgn,bcjgr,bcjgrp->bcgrpn', bm, decay_end * dt, x)
    chunk_decay = jnp.exp(acum[:, :, -1])

    def step(h, inp):
        dec, st = inp
        return dec[..., None, None] * h + st, h

    h_final, h_start = lax.scan(step, h0, (jnp.moveaxis(chunk_decay, 1, 0), jnp.moveaxis(states, 1, 0)))
    h_start = jnp.moveaxis(h_start, 0, 1)
    y_off = jnp.einsum('bcign,bcigr,bcgrpn->bcigrp', cm, jnp.exp(acum), h_start)
    return (y_diag + y_off).reshape(b_, l_, g_, r_, p_), h_final


def ssd_mixer(h, conv_buf, ssm_state, w_in, conv_w, conv_b, dt_bias, a_log, d_skip, gate_norm, w_out):
    b_, l_, _ = h.shape
    f32 = jnp.float32
    proj = h @ w_in
    z = proj[..., :SSD_INNER]
    xbc = proj[..., SSD_INNER:SSD_INNER + SSD_CONV_DIM]
    dt_raw = proj[..., SSD_INNER + SSD_CONV_DIM:]
    xbc_ext = jnp.concatenate([conv_buf.astype(xbc.dtype), xbc], axis=1)
    new_conv = xbc_ext[:, -(SSD_CONV - 1):]
    xbc = jax.nn.silu(causal_depthwise_conv(xbc_ext, conv_w, conv_b))
    gn = SSD_GROUPS * SSD_STATE
    xs = xbc[..., :SSD_INNER].reshape(b_, l_, SSD_GROUPS, SSD_HPG, SSD_HEAD_DIM).astype(f32)
    bm = xbc[..., SSD_INNER:SSD_INNER + gn].reshape(b_, l_, SSD_GROUPS, SSD_STATE).astype(f32)
    cm = xbc[..., SSD_INNER + gn:].reshape(b_, l_, SSD_GROUPS, SSD_STATE).astype(f32)
    dt = jax.nn.softplus(dt_raw.astype(f32) + dt_bias.astype(f32)).reshape(b_, l_, SSD_GROUPS, SSD_HPG)
    a = -jnp.exp(a_log.astype(f32)).reshape(SSD_GROUPS, SSD_HPG)
    h0 = ssm_state.astype(f32).reshape(b_, SSD_GROUPS, SSD_HPG, SSD_HEAD_DIM, SSD_STATE)
    y, h_final = ssd_scan(xs, dt, a, bm, cm, h0)
    y = y + d_skip.astype(f32).reshape(SSD_GROUPS, SSD_HPG)[:, :, None] * xs
    g = y.reshape(b_, l_, SSD_GROUPS, SSD_HPG * SSD_HEAD_DIM) * jax.nn.silu(z.astype(f32)).reshape(b_, l_, SSD_GROUPS, SSD_HPG * SSD_HEAD_DIM)
    g = g * lax.rsqrt(jnp.mean(g * g, axis=-1, keepdims=True) + NORM_EPS)
    g = g.reshape(b_, l_, SSD_INNER) * gate_norm.astype(f32)
    out = g.astype(h.dtype) @ w_out
    return out, new_conv, h_final.reshape(b_, SSD_HEADS, SSD_HEAD_DIM, SSD_STATE).astype(ssm_state.dtype)


def moba_project(h, pos, w_qkv):
    b_, l_, _ = h.shape
    qkv = (h @ w_qkv).reshape(b_, l_, 3, MOBA_HEADS, MOBA_HEAD_DIM)
    q = partial_rotary(qkv[:, :, 0], pos)
    k = partial_rotary(qkv[:, :, 1], pos)
    return q, k, qkv[:, :, 2]


def key_blocks(k_all, v_all):
    b_, lk, h_, d_ = k_all.shape
    kb = k_all.reshape(b_, lk // MOBA_BLOCK, MOBA_BLOCK, h_, d_)
    vb = v_all.reshape(b_, lk // MOBA_BLOCK, MOBA_BLOCK, h_, d_)
    kmean = jnp.mean(kb, axis=2, dtype=jnp.float32)
    return kb, vb, kmean


def moba_select_attend(q, q_pos, kb, vb, kmean):
    b_, t_, h_, d_ = q.shape
    nb = kb.shape[1]
    cur = q_pos // MOBA_BLOCK
    gate = jnp.einsum('bthd,bnhd->bhtn', q.astype(jnp.float32), kmean)
    fully_past = jnp.arange(nb)[None, :] < cur[:, None]
    gate = jnp.where(fully_past[None, None], gate, -jnp.inf)
    _, top = lax.top_k(gate, min(MOBA_TOPK, nb))
    own = jnp.broadcast_to(cur[None, None, :, None], (b_, h_, t_, 1)).astype(jnp.int32)
    sel = jnp.concatenate([top.astype(jnp.int32), own], axis=-1)
    valid = jnp.concatenate([top < cur[None, None, :, None], jnp.ones(own.shape, bool)], axis=-1)
    kpos = sel[..., None] * MOBA_BLOCK + jnp.arange(MOBA_BLOCK)
    mask = valid[..., None] & (kpos <= q_pos[None, None, :, None, None])
    bi = jnp.arange(b_)[:, None, None, None]
    hi = jnp.arange(h_)[None, :, None, None]
    k_sel = kb[bi, sel, :, hi]
    v_sel = vb[bi, sel, :, hi]
    s = jnp.einsum('bthd,bhtnkd->bhtnk', q, k_sel, preferred_element_type=jnp.float32) * (d_ ** -0.5)
    s = jnp.where(mask, s, -jnp.inf).reshape(b_, h_, t_, -1)
    p = jax.nn.softmax(s, axis=-1).reshape(mask.shape)
    o = jnp.einsum('bhtnk,bhtnkd->bthd', p.astype(v_sel.dtype), v_sel, preferred_element_type=jnp.float32)
    return o.astype(q.dtype)


def moba_prompt(h, w_qkv, w_o):
    b_, l_, _ = h.shape
    pos = jnp.arange(l_, dtype=jnp.int32)
    q, k, v = moba_project(h, pos, w_qkv)
    pad = (-l_) % MOBA_BLOCK
    padw = ((0, 0), (0, pad), (0, 0), (0, 0))
    kb, vb, kmean = key_blocks(jnp.pad(k, padw), jnp.pad(v, padw))
    nq = l_ // MOBA_Q_BLOCK
    q_blocks = jnp.moveaxis(q.reshape(b_, nq, MOBA_Q_BLOCK, MOBA_HEADS, MOBA_HEAD_DIM), 1, 0)
    pos_blocks = pos.reshape(nq, MOBA_Q_BLOCK)
    o = lax.map(lambda qp: moba_select_attend(qp[0], qp[1], kb, vb, kmean), (q_blocks, pos_blocks))
    o = jnp.moveaxis(o, 0, 1).reshape(b_, l_, MOBA_HEADS * MOBA_HEAD_DIM)
    return o @ w_o, k, v


def moba_sample(h, cache_k, cache_v, page_table, w_qkv, w_o):
    b_, t_, _ = h.shape
    past = page_table.shape[1] * PAGE_SIZE
    pos = past + jnp.arange(t_, dtype=jnp.int32)
    q, k, v = moba_project(h, pos, w_qkv)
    past_k = cache_k[page_table].reshape(b_, past, MOBA_HEADS, MOBA_HEAD_DIM).astype(k.dtype)
    past_v = cache_v[page_table].reshape(b_, past, MOBA_HEADS, MOBA_HEAD_DIM).astype(v.dtype)
    pad = (-(past + t_)) % MOBA_BLOCK
    zeros = jnp.zeros((b_, pad, MOBA_HEADS, MOBA_HEAD_DIM), k.dtype)
    kb, vb, kmean = key_blocks(jnp.concatenate([past_k, k, zeros], axis=1), jnp.concatenate([past_v, v, zeros], axis=1))
    o = moba_select_attend(q, pos, kb, vb, kmean).reshape(b_, t_, MOBA_HEADS * MOBA_HEAD_DIM)
    return o @ w_o, k, v


def gmlp_mixer(h, w_in, ln_g, ln_b, w_s, b_s, w_out):
    b_, l_, _ = h.shape
    uv = jax.nn.gelu(h @ w_in, approximate=False)
    u = uv[..., :GMLP_DIM]
    vf = uv[..., GMLP_DIM:].astype(jnp.float32)
    mu = jnp.mean(vf, axis=-1, keepdims=True)
    var = jnp.mean(jnp.square(vf - mu), axis=-1, keepdims=True)
    v = ((vf - mu) * lax.rsqrt(var + NORM_EPS) * ln_g.astype(jnp.float32) + ln_b.astype(jnp.float32)).astype(h.dtype)
    n = min(l_, GMLP_CHUNK)
    nc = l_ // n
    w = jnp.tril(w_s[:, :n, :n])
    vc = v.reshape(b_, nc, n, GMLP_GROUPS, GMLP_GROUP_DIM)
    s = jnp.einsum('gij,bcjgd->bcigd', w, vc) + b_s[:, :n].T[None, None, :, :, None]
    gated = u.reshape(b_, nc, n, GMLP_GROUPS, GMLP_GROUP_DIM) * s
    return gated.reshape(b_, l_, GMLP_DIM) @ w_out, v


def setup_inputs(seed: int = 0) -> dict:
    key = jax.random.key(seed)
    ks = list(jax.random.split(key, 32))
    f32 = jnp.float32

    def nrm(k, shape, scale):
        return jax.random.normal(k, shape, f32) * scale

    n_pages = PAST_LEN // PAGE_SIZE
    n_pool = (5 * DEC_BATCH * n_pages) // 4
    page_table = jax.random.permutation(ks[0], n_pool)[:DEC_BATCH * n_pages].reshape(DEC_BATCH, n_pages).astype(jnp.int32)
    dt0 = jnp.exp(jax.random.uniform(ks[1], (N_SSD_LAYERS, SSD_HEADS), f32, math.log(1e-3), math.log(1e-1)))
    ssd_dt_bias = dt0 + jnp.log(-jnp.expm1(-dt0))
    ssd_a_log = jnp.log(jax.random.uniform(ks[2], (N_SSD_LAYERS, SSD_HEADS), f32, 1.0, 16.0))
    kv_shape = (N_MOBA_LAYERS, n_pool, PAGE_SIZE, MOBA_HEADS, MOBA_HEAD_DIM)
    attn_w = MOBA_HEADS * MOBA_HEAD_DIM
    return {
        'x_prompt': nrm(ks[3], (BATCH, SEQ, D_MODEL), 1.0),
        'x_sample': nrm(ks[4], (DEC_BATCH, DEC_SEQ, D_MODEL), 1.0),
        'state_ssm': nrm(ks[5], (N_SSD_LAYERS, DEC_BATCH, SSD_HEADS, SSD_HEAD_DIM, SSD_STATE), 0.1),
        'state_conv': nrm(ks[6], (N_SSD_LAYERS, DEC_BATCH, SSD_CONV - 1, SSD_CONV_DIM), 1.0),
        'cache_k': nrm(ks[7], kv_shape, 1.0),
        'cache_v': nrm(ks[8], kv_shape, 1.0),
        'page_table': page_table,
        'norm_mix': 1.0 + nrm(ks[9], (DEPTH, D_MODEL), 0.05),
        'norm_ffn': 1.0 + nrm(ks[10], (DEPTH, D_MODEL), 0.05),
        'norm_final': 1.0 + nrm(ks[11], (D_MODEL,), 0.05),
        'ssd_w_in': nrm(ks[12], (N_SSD_LAYERS, D_MODEL, SSD_IN_DIM), D_MODEL ** -0.5),
        'ssd_conv_w': nrm(ks[13], (N_SSD_LAYERS, SSD_CONV, SSD_CONV_DIM), SSD_CONV ** -0.5),
        'ssd_conv_b': nrm(ks[14], (N_SSD_LAYERS, SSD_CONV_DIM), 0.02),
        'ssd_dt_bias': ssd_dt_bias,
        'ssd_a_log': ssd_a_log,
        'ssd_d': 1.0 + nrm(ks[15], (N_SSD_LAYERS, SSD_HEADS), 0.1),
        'ssd_gate_norm': 1.0 + nrm(ks[16], (N_SSD_LAYERS, SSD_INNER), 0.05),
        'ssd_w_out': nrm(ks[17], (N_SSD_LAYERS, SSD_INNER, D_MODEL), SSD_INNER ** -0.5),
        'moba_w_qkv': nrm(ks[18], (N_MOBA_LAYERS, D_MODEL, 3 * attn_w), D_MODEL ** -0.5),
        'moba_w_o': nrm(ks[19], (N_MOBA_LAYERS, attn_w, D_MODEL), attn_w ** -0.5),
        'gmlp_w_in': nrm(ks[20], (N_GMLP_LAYERS, D_MODEL, 2 * GMLP_DIM), D_MODEL ** -0.5),
        'gmlp_ln_g': 1.0 + nrm(ks[21], (N_GMLP_LAYERS, GMLP_DIM), 0.05),
        'gmlp_ln_b': nrm(ks[22], (N_GMLP_LAYERS, GMLP_DIM), 0.02),
        'gmlp_w_s': nrm(ks[23], (N_GMLP_LAYERS, GMLP_GROUPS, GMLP_CHUNK, GMLP_CHUNK), GMLP_CHUNK ** -0.5),
        'gmlp_b_s': 1.0 + nrm(ks[24], (N_GMLP_LAYERS, GMLP_GROUPS, GMLP_CHUNK), 0.1),
        'gmlp_w_out': nrm(ks[25], (N_GMLP_LAYERS, GMLP_DIM, D_MODEL), GMLP_DIM ** -0.5),
        'ffn_w_gate': nrm(ks[26], (DEPTH, D_MODEL, FFN_HIDDEN), D_MODEL ** -0.5),
        'ffn_w_up': nrm(ks[27], (DEPTH, D_MODEL, FFN_HIDDEN), D_MODEL ** -0.5),
        'ffn_w_down': nrm(ks[28], (DEPTH, FFN_HIDDEN, D_MODEL), FFN_HIDDEN ** -0.5),
    }


def reference(x_prompt, x_sample, state_ssm, state_conv, cache_k, cache_v, page_table, norm_mix, norm_ffn, norm_final, ssd_w_in, ssd_conv_w, ssd_conv_b, ssd_dt_bias, ssd_a_log, ssd_d, ssd_gate_norm, ssd_w_out, moba_w_qkv, moba_w_o, gmlp_w_in, gmlp_ln_g, gmlp_ln_b, gmlp_w_s, gmlp_b_s, gmlp_w_out, ffn_w_gate, ffn_w_up, ffn_w_down):
    hp = x_prompt
    hs = x_sample
    bp = x_prompt.shape[0]
    ssm_p, ssm_s, conv_p, conv_s = [], [], [], []
    k_p, v_p, k_s, v_s, gv_s = [], [], [], [], []
    for i in range(DEPTH):
        kind = i % N_MIXERS
        j = i // N_MIXERS
        ap = rms_norm(hp, norm_mix[i])
        as_ = rms_norm(hs, norm_mix[i])
        if kind == 0:
            w = (ssd_w_in[j], ssd_conv_w[j], ssd_conv_b[j], ssd_dt_bias[j], ssd_a_log[j], ssd_d[j], ssd_gate_norm[j], ssd_w_out[j])
            zero_conv = jnp.zeros((bp, SSD_CONV - 1, SSD_CONV_DIM), state_conv.dtype)
            zero_ssm = jnp.zeros((bp, SSD_HEADS, SSD_HEAD_DIM, SSD_STATE), state_ssm.dtype)
            mp, cp_, sp_ = ssd_mixer(ap, zero_conv, zero_ssm, *w)
            ms, cs_, ss_ = ssd_mixer(as_, state_conv[j], state_ssm[j], *w)
            conv_p.append(cp_)
            ssm_p.append(sp_)
            conv_s.append(cs_)
            ssm_s.append(ss_)
        elif kind == 1:
            mp, kp_, vp_ = moba_prompt(ap, moba_w_qkv[j], moba_w_o[j])
            ms, ks_, vs_ = moba_sample(as_, cache_k[j], cache_v[j], page_table, moba_w_qkv[j], moba_w_o[j])
            k_p.append(kp_)
            v_p.append(vp_)
            k_s.append(ks_)
            v_s.append(vs_)
        else:
            w = (gmlp_w_in[j], gmlp_ln_g[j], gmlp_ln_b[j], gmlp_w_s[j], gmlp_b_s[j], gmlp_w_out[j])
            mp, _ = gmlp_mixer(ap, *w)
            ms, vrows = gmlp_mixer(as_, *w)
            gv_s.append(vrows)
        hp = hp + mp
        hs = hs + ms
        hp = hp + swiglu_ffn(rms_norm(hp, norm_ffn[i]), ffn_w_gate[i], ffn_w_up[i], ffn_w_down[i])
        hs = hs + swiglu_ffn(rms_norm(hs, norm_ffn[i]), ffn_w_gate[i], ffn_w_up[i], ffn_w_down[i])
    y_prompt = rms_norm(hp, norm_final)
    y_sample = rms_norm(hs, norm_final)
    return (y_prompt, y_sample, jnp.stack(ssm_p), jnp.stack(ssm_s), jnp.stack(conv_p), jnp.stack(conv_s), jnp.stack(k_p), jnp.stack(v_p), jnp.stack(k_s), jnp.stack(v_s), jnp.stack(gv_s))
```

```python
import contextlib
import math
import numpy as np
import concourse.bass as bass
import concourse.mybir as mybir
from concourse.bass_utils import run_bass_kernel_spmd

F32 = mybir.dt.float32
BF16 = mybir.dt.bfloat16
I32 = mybir.dt.int32
AF = mybir.ActivationFunctionType
ALU = mybir.AluOpType
AX = mybir.AxisListType

ENGS = ("pe", "dve", "act", "pool", "sp")
SEM_CAP = 20000
DMA_CAP = 1000
DMA_POOL = 8

D = 2048
KC = 16
SEQ = 2048
TP = 1024
DEPTH = 4
SSD_INNER = 4096
SSD_CONV_DIM = 6144
SSD_IN = 10304
FFN_H = 5632
EPS = 1e-6
NCORES = 8
NPOOL = 1280


class Op:
    __slots__ = ("eng", "fn", "deps", "needed", "is_dma", "event")

    def __init__(self, eng, fn, is_dma):
        self.eng = eng
        self.fn = fn
        self.deps = []
        self.needed = False
        self.is_dma = is_dma
        self.event = None


class Prog:
    def __init__(self, nc):
        self.nc = nc
        self.ops = {e: [] for e in ENGS}
        self.last_w = {}
        self.readers = {}
        self.stack = contextlib.ExitStack()
        self.nsem = 0
        self.dmas_since = []

    def new_sem(self):
        self.nsem += 1
        return self.stack.enter_context(self.nc.semaphore("s%d" % self.nsem))

    def op(self, eng, fn, reads=(), writes=(), dma=False, pe_acc=False, extra=()):
        o = Op(eng, fn, dma)
        deps = list(extra)
        for k in reads:
            w = self.last_w.get(k)
            if w is not None:
                deps.append(w)
        for k in writes:
            w = self.last_w.get(k)
            if w is not None and not (pe_acc and w.eng == "pe" and eng == "pe" and not w.is_dma):
                deps.append(w)
            deps.extend(self.readers.get(k, ()))
        seen = set()
        for d in deps:
            if d is o or id(d) in seen:
                continue
            seen.add(id(d))
            d.needed = True
            o.deps.append(d)
        for k in reads:
            self.readers.setdefault(k, []).append(o)
        for k in writes:
            self.last_w[k] = o
            self.readers[k] = []
        self.ops[eng].append(o)
        if dma:
            self.dmas_since.append(o)
        return o

    def barrier(self):
        prev = [self.ops[e][-1] for e in ENGS if self.ops[e]] + self.dmas_since
        self.dmas_since = []
        for e in ENGS:
            self.op(e, lambda eh: eh.nop(), extra=prev)
        self.last_w = {}
        self.readers = {}

    def finish(self, final_ops):
        nc = self.nc
        for o in final_ops:
            o.needed = True
        for e in ENGS:
            sem = None
            cnt = 0
            for o in self.ops[e]:
                if o.is_dma or not o.needed:
                    continue
                if sem is None or cnt >= SEM_CAP:
                    sem = self.new_sem()
                    cnt = 0
                cnt += 1
                o.event = (sem, cnt)
        dma_prev = {}
        for e in ENGS:
            pool = []
            i = 0
            for o in self.ops[e]:
                if not o.is_dma:
                    continue
                if len(pool) < DMA_POOL:
                    pool.append([self.new_sem(), 0])
                slot = pool[i % len(pool)]
                if slot[1] >= DMA_CAP:
                    slot[0] = self.new_sem()
                    slot[1] = 0
                dma_prev[id(o)] = (slot[0], slot[1] * 16) if slot[1] > 0 else None
                slot[1] += 1
                o.event = (slot[0], slot[1] * 16)
                i += 1
        handles = {"pe": "tensor", "dve": "vector", "act": "scalar", "pool": "gpsimd", "sp": "sync"}
        with nc.Block() as block:
            for e in ENGS:

                def body(eh, e=e):
                    known = {}

                    def wait(ev):
                        if ev is None:
                            return
                        s, v = ev
                        if known.get(id(s), 0) >= v:
                            return
                        known[id(s)] = v
                        eh.wait_ge(s, v)

                    for o in self.ops[e]:
                        for d in o.deps:
                            wait(d.event)
                        if o.is_dma:
                            wait(dma_prev[id(o)])
                        ins = o.fn(eh)
                        if o.is_dma:
                            ins.then_inc(o.event[0], 16)
                        elif o.needed:
                            ins.then_inc(o.event[0], 1)
                    if e == "sp":
                        for o in final_ops:
                            wait(o.event)

                getattr(block, handles[e])(body)


def bc(ap, axis, n):
    l = [list(x) for x in ap.ap]
    l.insert(axis, [0, n])
    return bass.AP(ap.tensor, ap.offset, l)


def build(cfg):
    nc = bass.Bass("TRN2", target_bir_lowering=False)

    def din(name, shape, dt=F32):
        if cfg.get("stub_cache") and name.startswith("cache"):
            shape = [1, 128, 128]
        if cfg.get("stub_w") and name.split("_")[0] in ("ssd", "ffn", "moba", "gmlp", "cache") and int(np.prod(shape)) > (1 << 22):
            shape = [1, 128, 128]
        return nc.dram_tensor(name, list(shape), dt, kind="ExternalInput").ap()

    def dout(name, shape, dt=F32):
        return nc.dram_tensor(name, list(shape), dt, kind="ExternalOutput").ap()

    def dscr(name, shape, dt=F32):
        return nc.dram_tensor(name, list(shape), dt).ap()

    n_layers = cfg.get("n_layers", DEPTH)
    do_mixer = cfg.get("mixer", True)
    do_ffn = cfg.get("ffn", True)

    I = {}
    I["xp"] = din("xp", [SEQ, D])
    I["xs"] = din("xs", [4, D])
    I["state_ssm"] = din("state_ssm", [2, 4096, 128])
    I["state_conv"] = din("state_conv", [2, 3, SSD_CONV_DIM])
    I["norm_mix"] = din("norm_mix", [DEPTH, D])
    I["norm_ffn"] = din("norm_ffn", [DEPTH, D])
    I["norm_final"] = din("norm_final", [1, D])
    I["ssd_w_in"] = din("ssd_w_in", [2, D, SSD_IN])
    I["ssd_conv_w"] = din("ssd_conv_w", [2, 4, SSD_CONV_DIM])
    I["ssd_conv_b"] = din("ssd_conv_b", [2, SSD_CONV_DIM])
    I["ssd_dt_bias"] = din("ssd_dt_bias", [2, 64])
    I["ssd_a_log"] = din("ssd_a_log", [2, 64])
    I["ssd_d"] = din("ssd_d", [2, 64])
    I["ssd_gate_norm"] = din("ssd_gate_norm", [2, SSD_INNER])
    I["ssd_w_out"] = din("ssd_w_out", [2, SSD_INNER, D])
    I["moba_w_qkv"] = din("moba_w_qkv", [1, D, 3 * D])
    I["moba_w_o"] = din("moba_w_o", [1, D, D])
    I["rope_p"] = din("rope_p", [SEQ, 32])
    I["rope_s"] = din("rope_s", [4, 32])
    I["valid_p"] = din("valid_p", [SEQ, 8])
    I["own_p"] = din("own_p", [SEQ, 8])
    npool = cfg.get("npool", NPOOL)
    I["cache_k"] = din("cache_k", [npool * 128, D])
    I["cache_v"] = din("cache_v", [npool * 128, D])
    I["page_table"] = din("page_table", [1, 128], I32)
    I["own_s"] = din("own_s", [64, 4])
    I["gmlp_w_in"] = din("gmlp_w_in", [1, D, 2 * D])
    I["gmlp_ln_g"] = din("gmlp_ln_g", [1, D])
    I["gmlp_ln_b"] = din("gmlp_ln_b", [1, D])
    I["gmlp_w_s"] = din("gmlp_w_s", [1, 16, 128, 128])
    I["gmlp_b_s"] = din("gmlp_b_s", [1, 16, 128])
    I["gmlp_w_out"] = din("gmlp_w_out", [1, D, D])
    I["ffn_w_gate"] = din("ffn_w_gate", [DEPTH, D, FFN_H])
    I["ffn_w_up"] = din("ffn_w_up", [DEPTH, D, FFN_H])
    I["ffn_w_down"] = din("ffn_w_down", [DEPTH, FFN_H, D])
    O = {}
    O["y_p"] = dout("y_p", [SEQ, D])
    O["y_s"] = dout("y_s", [4, D])
    O["ssm_p"] = dout("ssm_p", [2, 4096, 128])
    O["ssm_s"] = dout("ssm_s", [2, 4096, 128])
    O["conv_p"] = dout("conv_p", [2, 3, SSD_CONV_DIM])
    O["conv_s"] = dout("conv_s", [2, 3, SSD_CONV_DIM])
    O["k_p"] = dout("k_p", [SEQ, D])
    O["v_p"] = dout("v_p", [SEQ, D])
    O["k_s"] = dout("k_s", [4, D])
    O["v_s"] = dout("v_s", [4, D])
    O["gv_s"] = dout("gv_s", [4, D])
    S = {}
    S["kT"] = dscr("kT_scr", [16, 128, 1024], BF16)
    S["v"] = dscr("v_scr", [16, 128, 8, 128], BF16)
    S["ksum"] = dscr("ksum_scr", [16, 128, 4])
    S["hsave"] = dscr("hsave", [2, 128, 4096])

    P = Prog(nc)
    finals = []
    with contextlib.ExitStack() as es:
        es.enter_context(P.stack)
        es.enter_context(nc.allow_non_contiguous_dma(reason="small strided constant/boundary loads"))

        def sb(name, shape, dt):
            return es.enter_context(nc.sbuf_tensor(name, list(shape), dt))

        def psum(name, shape, dt):
            return es.enter_context(nc.psum_tensor(name, list(shape), dt))

        R = sb("R", [128, 9, D], F32)
        xnT = sb("xnT", [128, KC, 1152], BF16)
        ident = sb("ident", [128, 128], F32)
        identb = sb("identb", [128, 128], BF16)
        gcol = sb("gcol", [128, 9, KC], F32)
        small = sb("small", [128, 64], F32)
        ARN = 23000
        AR = sb("AR", [128, ARN], F32)
        pm = [psum("pm%d" % i, [128, 512], F32) for i in range(4)]
        pyb = psum("pyb", [128, 512], F32)
        ptb = [psum("ptb%d" % i, [128, 1024], BF16) for i in range(2)]
        pf = psum("pf", [128, 512], F32)

        rot = {"pm": 0, "pt": 0}

        def next_pm():
            i = rot["pm"] % 4
            rot["pm"] += 1
            return pm[i], "pm%d" % i

        def next_pt():
            i = rot["pt"] % 2
            rot["pt"] += 1
            return ptb[i], "pt%d" % i

        class Carver:
            def __init__(self):
                self.off = 0

            def f32(self, n):
                a = AR[:, self.off:self.off + n]
                self.off += n
                assert self.off <= ARN, self.off
                return a

            def bf16(self, n):
                m = (n + 1) // 2
                a = AR[:, self.off:self.off + m].bitcast(BF16)
                self.off += m
                assert self.off <= ARN, self.off
                return a

        P.op("pool", lambda e: e.memset(ident[:], 0.0), writes=["ident"])
        P.op("pool", lambda e: e.affine_select(out=ident[:], in_=ident[:], pattern=[[-1, 128]], compare_op=ALU.not_equal, fill=1.0, base=0, channel_multiplier=1), reads=["ident"], writes=["ident"])
        P.op("dve", lambda e: e.tensor_copy(out=identb[:], in_=ident[:]), reads=["ident"], writes=["identb"])
        P.op("sp", lambda e: e.dma_start(out=gcol[:, 0:4, :], in_=I["norm_mix"].rearrange("v (kc p) -> p v kc", p=128)), writes=["gcol0"], dma=True)
        P.op("sp", lambda e: e.dma_start(out=gcol[:, 4:8, :], in_=I["norm_ffn"].rearrange("v (kc p) -> p v kc", p=128)), writes=["gcol1"], dma=True)
        P.op("sp", lambda e: e.dma_start(out=gcol[:, 8:9, :], in_=I["norm_final"].rearrange("v (kc p) -> p v kc", p=128)), writes=["gcol2"], dma=True)

        def rms_stats(tt):
            def f(xs_scr):
                P.op("act", lambda e: e.activation(out=xs_scr, in_=R[:, tt, :], func=AF.Square, accum_out=small[:, 0:1]), reads=[("R", tt)], writes=["xs_scr", "ss"])
                P.op("act", lambda e: e.activation(out=small[:, 1:2], in_=small[:, 0:1], func=AF.Sqrt, scale=1.0 / D, bias=small[:, 8:9]), reads=["ss", "epsc"], writes=["std"])
                P.op("dve", lambda e: e.reciprocal(out=small[:, 2:3], in_=small[:, 1:2]), reads=["std"], writes=["rstd"])
            return f

        P.op("dve", lambda e: e.memset(small[:, 8:9], EPS), writes=["epsc"])
        Tri = sb("Tri", [128, 128], F32)
        SL = sb("SL", [128, 128], F32)
        NEG = sb("NEG", [128, 128], F32)
        ones = sb("ones", [128, 128], F32)
        vmask = sb("vmask", [128, 1], F32)
        ctail = sb("ctail", [128, 2, 48, 3], F32)
        P.op("pool", lambda e: e.memset(ones[:], 1.0), writes=["ones"])
        P.op("pool", lambda e: e.memset(Tri[:], 1.0), writes=["Tri"])
        P.op("pool", lambda e: e.affine_select(out=Tri[:], in_=Tri[:], pattern=[[1, 128]], compare_op=ALU.is_ge, fill=0.0, base=0, channel_multiplier=-1), writes=["Tri"])
        P.op("pool", lambda e: e.memset(SL[:], 1.0), writes=["SL"])
        P.op("pool", lambda e: e.affine_select(out=SL[:], in_=SL[:], pattern=[[-1, 128]], compare_op=ALU.is_gt, fill=0.0, base=0, channel_multiplier=1), writes=["SL"])
        P.op("pool", lambda e: e.memset(NEG[:], 0.0), writes=["NEG"])
        P.op("pool", lambda e: e.affine_select(out=NEG[:], in_=NEG[:], pattern=[[1, 128]], compare_op=ALU.is_ge, fill=-30000.0, base=0, channel_multiplier=-1), writes=["NEG"])
        TriLf = sb("TriLf", [128, 128], F32)
        P.op("pool", lambda e: e.memset(TriLf[:], 1.0), writes=["TriLf"])
        P.op("pool", lambda e: e.affine_select(out=TriLf[:], in_=TriLf[:], pattern=[[-1, 128]], compare_op=ALU.is_ge, fill=0.0, base=0, channel_multiplier=1), writes=["TriLf"])
        TriL = sb("TriL", [128, 128], BF16)
        P.op("pool", lambda e: e.memset(TriL[:], 1.0), writes=["TriL"])
        P.op("pool", lambda e: e.affine_select(out=TriL[:], in_=TriL[:], pattern=[[-1, 128]], compare_op=ALU.is_ge, fill=0.0, base=0, channel_multiplier=1), writes=["TriL"])
        P.op("pool", lambda e: e.memset(vmask[:], 1.0), writes=["vmask"])
        P.op("pool", lambda e: e.affine_select(out=vmask[:], in_=vmask[:], pattern=[[0, 1]], compare_op=ALU.is_ge, fill=0.0, base=3, channel_multiplier=-1), writes=["vmask"])

        def make_xnT(gidx, NT, xs_scr):
            for tt in range(NT):
                rms_stats(tt)(xs_scr)
                P.op("act", lambda e, tt=tt: e.activation(out=xs_scr, in_=R[:, tt, :], func=AF.Identity, scale=small[:, 2:3]), reads=[("R", tt), "rstd"], writes=["xs_scr"])
                for q in range(2):
                    pt, ptk = next_pt()
                    for j in range(8):
                        kc = q * 8 + j
                        P.op("pe", lambda e, pt=pt, j=j, kc=kc: e.transpose(out=pt[:, j * 128:(j + 1) * 128], in_=xs_scr[:, kc * 128:(kc + 1) * 128], identity=identb[:]), reads=["xs_scr", "identb"], writes=[ptk], pe_acc=True)
                    P.op("dve", lambda e, pt=pt, q=q, tt=tt: e.tensor_tensor(out=xnT[:, q * 8:q * 8 + 8, tt * 128:(tt + 1) * 128], in0=pt.rearrange("p (a b) -> p a b", a=8), in1=bc(gcol[:, gidx, q * 8:q * 8 + 8], 2, 128), op=ALU.mult), reads=["gcol0", "gcol1", "gcol2"], writes=[ptk, ("xnT", tt)])

        def load_w(dst, w2d, row0, nkc, col0, ncols, key):
            src = w2d[row0:row0 + nkc * 128, col0:col0 + ncols].rearrange("(kc p) n -> p kc n", p=128)
            return P.op("pool", lambda e: e.dma_start(out=dst, in_=src), writes=[key], dma=True)

        def mts(NT):
            T = NT * 128
            return [(c, min(512, T - c)) for c in range(0, T, 512)]

        def ffn(li, NT):
            P.barrier()
            cv = Carver()
            xs_scr = cv.bf16(D)
            wg = [cv.bf16(KC * 256).rearrange("p (k n) -> p k n", k=KC) for _ in range(2)]
            wu = [cv.bf16(KC * 256).rearrange("p (k n) -> p k n", k=KC) for _ in range(2)]
            wd = [cv.bf16(2 * D).rearrange("p (k n) -> p k n", k=2) for _ in range(2)]
            hid = [cv.bf16(2 * 1152).rearrange("p (k n) -> p k n", k=2) for _ in range(2)]
            sg = [cv.f32(512) for _ in range(2)]
            make_xnT(4 + li, NT, xs_scr)
            T = NT * 128
            wgd, wud, wdd = I["ffn_w_gate"][li], I["ffn_w_up"][li], I["ffn_w_down"][li]
            cnt = 0
            if cfg.get('ffn_stage', 3) < 2:
                return
            for grp in range(cfg.get('ffn_groups', FFN_H // 256)):
                b = grp % 2
                load_w(wg[b], wgd, 0, KC, grp * 256, 256, ("wg", b))
                load_w(wu[b], wud, 0, KC, grp * 256, 256, ("wu", b))
                load_w(wd[b], wdd, grp * 256, 2, 0, D, ("wd", b))
                for hc in range(2):
                    for (c0, n) in mts(NT):
                        pg, pgk = next_pm()
                        pu, puk = next_pm()
                        rk = [("xnT", t) for t in range(c0 // 128, (c0 + n) // 128)]
                        for kc in range(KC):
                            P.op("pe", lambda e, pg=pg, b=b, kc=kc, hc=hc, c0=c0, n=n: e.matmul(pg[:, 0:n], lhsT=wg[b][:, kc, hc * 128:(hc + 1) * 128], rhs=xnT[:, kc, c0:c0 + n], start=(kc == 0), stop=(kc == KC - 1)), reads=[("wg", b)] + rk, writes=[pgk], pe_acc=True)
                        for kc in range(KC):
                            P.op("pe", lambda e, pu=pu, b=b, kc=kc, hc=hc, c0=c0, n=n: e.matmul(pu[:, 0:n], lhsT=wu[b][:, kc, hc * 128:(hc + 1) * 128], rhs=xnT[:, kc, c0:c0 + n], start=(kc == 0), stop=(kc == KC - 1)), reads=[("wu", b)] + rk, writes=[puk], pe_acc=True)
                        s = sg[cnt % 2]
                        sk = ("sg", cnt % 2)
                        cnt += 1
                        P.op("act", lambda e, pg=pg, s=s, n=n: e.activation(out=s[:, 0:n], in_=pg[:, 0:n], func=AF.Silu), writes=[pgk, sk])
                        P.op("dve", lambda e, pu=pu, s=s, n=n, b=b, hc=hc, c0=c0: e.tensor_tensor(out=hid[b][:, hc, c0:c0 + n], in0=pu[:, 0:n], in1=s[:, 0:n], op=ALU.mult), reads=[sk], writes=[puk, ("hid", b, hc, c0)])
                for tt in range(NT if cfg.get('ffn_stage', 3) >= 3 else 0):
                    for cg in range(4):
                        po, pok = next_pm()
                        for hc in range(2):
                            P.op("pe", lambda e, po=po, b=b, hc=hc, tt=tt, cg=cg: e.matmul(po[:], lhsT=hid[b][:, hc, tt * 128:(tt + 1) * 128], rhs=wd[b][:, hc, cg * 512:(cg + 1) * 512], start=(hc == 0), stop=(hc == 1)), reads=[("wd", b), ("hid", b, hc, (tt // 4) * 512)], writes=[pok], pe_acc=True)
                        P.op("dve", lambda e, po=po, tt=tt, cg=cg: e.tensor_tensor(out=R[:, tt, cg * 512:(cg + 1) * 512], in0=po[:], in1=R[:, tt, cg * 512:(cg + 1) * 512], op=ALU.add), writes=[pok, ("R", tt)])

        def ssd(li, ph, NT):
            j = li // 3
            P.barrier()
            cv = Carver()
            xs_scr = cv.bf16(D)
            make_xnT(li, NT, xs_scr)
            P.barrier()
            cv = Carver()
            T = NT * 128
            XL = 3 + 1024 + (3 + 128 if NT == 9 else 0)
            SOFF = 1027
            wb = [cv.bf16(KC * 264).rearrange("p (k n) -> p k n", k=KC) for _ in range(2)]
            zs = cv.bf16(NT * 512).rearrange("p (t n) -> p t n", t=NT)
            xtm = cv.f32(NT * 512).rearrange("p (t n) -> p t n", t=NT)
            btm = cv.bf16(NT * 128).rearrange("p (t n) -> p t n", t=NT)
            bT = cv.bf16(T)
            cT = cv.bf16(T)
            dtt = cv.f32(NT * 8).rearrange("p (t n) -> p t n", t=NT)
            att = cv.f32(NT * 8).rearrange("p (t n) -> p t n", t=NT)
            xpre = cv.f32(XL)
            xcv = cv.f32(T)
            xfT = xcv
            gnT = cv.bf16(4 * T).rearrange("p (k n) -> p k n", k=4)
            wo1 = cv.bf16(4 * 512).rearrange("p (k n) -> p k n", k=4)
            wo = [wo1, wo1]
            hT = cv.f32(512)
            hTb = cv.bf16(512)
            hS = hT
            cw = cv.f32(48 * 4).rearrange("p (c k) -> p c k", c=48)
            cbias = cv.f32(48)
            dtb_bc = cv.f32(64)
            A_bc = cv.f32(64)
            D_bc = cv.f32(64)
            gnw_bc = cv.f32(512)
            sc = cv.f32(64)
            lT = cv.f32(128)
            ex = cv.f32(128)
            cbs = cv.f32(128)
            mT = cv.bf16(128)
            xdt = cv.bf16(512)
            xw = cv.bf16(512)
            yy = cv.f32(512)
            gg = cv.f32(512)
            gnb = cv.bf16(512)
            st_in = xpre[:, 0:512]
            print('ssd arena', cv.off)
            w_in = I["ssd_w_in"][j]
            for k in range(4):
                P.op("sp", lambda e, k=k: e.dma_start(out=cw[:, :, k], in_=I["ssd_conv_w"][j][k:k + 1, :].rearrange("o (c p) -> p (o c)", p=128)), writes=["cw%d" % k], dma=True)
            P.op("sp", lambda e: e.dma_start(out=cbias, in_=I["ssd_conv_b"][j:j + 1, :].rearrange("o (c p) -> p (o c)", p=128)), writes=["cbias"], dma=True)
            for (dst, src, k) in ((dtb_bc, "ssd_dt_bias", "dtb"), (A_bc, "ssd_a_log", "Abc"), (D_bc, "ssd_d", "Dbc")):
                P.op("sp", lambda e, dst=dst, src=src: e.dma_start(out=dst, in_=bass.AP(I[src].tensor, j * 64, [[0, 128], [1, 64]])), writes=[k], dma=True)
            P.op("act", lambda e: e.activation(out=A_bc, in_=A_bc, func=AF.Exp), writes=["Abc"])
            P.op("dve", lambda e: e.tensor_scalar(out=A_bc, in0=A_bc, scalar1=-1.0, scalar2=None, op0=ALU.mult), writes=["Abc"])

            def fm_proj(col0, dst_fn, wt, wcol):
                for (c0, n) in mts(NT):
                    pp, ppk = next_pm()
                    rk = [("xnT", t) for t in range(c0 // 128, (c0 + n) // 128)]
                    for kc in range(KC):
                        P.op("pe", lambda e, pp=pp, kc=kc, c0=c0, n=n: e.matmul(pp[:, 0:n], lhsT=wt[:, kc, wcol:wcol + 128], rhs=xnT[:, kc, c0:c0 + n], start=(kc == 0), stop=(kc == KC - 1)), reads=["wb"] + rk, writes=[ppk], pe_acc=True)
                    dst_fn(pp, ppk, c0, n)

            def conv_chunk(ch, out_ap, out_key, zero_tail):
                segs = [(0, 1024, 0)] + ([(SOFF, 128, 1024)] if NT == 9 else [])
                for (off, L, o0) in segs:
                    P.op("dve", lambda e, off=off, L=L, o0=o0: e.tensor_scalar(out=xcv[:, o0:o0 + L], in0=xpre[:, off:off + L], scalar1=cw[:, ch, 0:1], scalar2=None, op0=ALU.mult), reads=["xpre", "cw0", "cw1", "cw2", "cw3"], writes=["xcv"])
                    for k in range(1, 4):
                        P.op("dve", lambda e, off=off, L=L, o0=o0, k=k: e.scalar_tensor_tensor(out=xcv[:, o0:o0 + L], in0=xpre[:, off + k:off + k + L], scalar=cw[:, ch, k:k + 1], in1=xcv[:, o0:o0 + L], op0=ALU.mult, op1=ALU.add), reads=["xpre", "cw0", "cw1", "cw2", "cw3"], writes=["xcv"])
                P.op("act", lambda e: e.activation(out=out_ap[:, 0:T], in_=xcv[:, 0:T], func=AF.Silu, bias=cbias[:, ch:ch + 1]), reads=["xcv", "cbias"], writes=[out_key])
                if zero_tail and NT == 9:
                    P.op("pool", lambda e: e.memset(out_ap[:, 1024 + 4:T], 0.0), writes=[out_key])

            for g in range(8):
                P.op("sp", lambda e, g=g: e.dma_start(out=gnw_bc, in_=bass.AP(I["ssd_gate_norm"].tensor, j * SSD_INNER + g * 512, [[0, 128], [1, 512]])), writes=["gnw"], dma=True)
                for half in range(2):
                    w = wb[half]
                    load_w(w[:, :, 0:256], w_in, 0, KC, g * 512 + half * 256, 256, "wb")
                    for tt in range(NT):
                        pp, ppk = next_pm()
                        for kc in range(KC):
                            P.op("pe", lambda e, pp=pp, kc=kc, tt=tt, w=w: e.matmul(pp[:, 0:256], lhsT=xnT[:, kc, tt * 128:(tt + 1) * 128], rhs=w[:, kc, 0:256], start=(kc == 0), stop=(kc == KC - 1)), reads=["wb", ("xnT", tt)], writes=[ppk], pe_acc=True)
                        P.op("act", lambda e, pp=pp, tt=tt, half=half: e.activation(out=zs[:, tt, half * 256:(half + 1) * 256], in_=pp[:, 0:256], func=AF.Silu), writes=[ppk, ("zs", tt)])
                w = wb[0]
                load_w(w[:, :, 0:128], w_in, 0, KC, 8192 + g * 128, 128, "wb")
                load_w(w[:, :, 128:256], w_in, 0, KC, 9216 + g * 128, 128, "wb")
                load_w(w[:, :, 256:264], w_in, 0, KC, 10240 + g * 8, 8, "wb")
                for tt in range(NT):
                    pp, ppk = next_pm()
                    for kc in range(KC):
                        P.op("pe", lambda e, pp=pp, kc=kc, tt=tt, w=w: e.matmul(pp[:, 0:8], lhsT=xnT[:, kc, tt * 128:(tt + 1) * 128], rhs=w[:, kc, 256:264], start=(kc == 0), stop=(kc == KC - 1)), reads=["wb", ("xnT", tt)], writes=[ppk], pe_acc=True)
                    P.op("dve", lambda e, pp=pp, tt=tt, g=g: e.tensor_tensor(out=dtt[:, tt, :], in0=pp[:, 0:8], in1=dtb_bc[:, g * 8:(g + 1) * 8], op=ALU.add), reads=["dtb"], writes=[ppk, "dtt"])
                P.op("act", lambda e: e.activation(out=dtt, in_=dtt, func=AF.Exp), writes=["dtt"])
                P.op("act", lambda e: e.activation(out=dtt, in_=dtt, func=AF.Ln, bias=1.0), writes=["dtt"])
                if NT == 9:
                    P.op("dve", lambda e: e.tensor_scalar(out=dtt[:, 8, :], in0=dtt[:, 8, :], scalar1=vmask[:, 0:1], scalar2=None, op0=ALU.mult), reads=["vmask"], writes=["dtt"])
                P.op("dve", lambda e, g=g: e.tensor_tensor(out=att, in0=dtt, in1=bc(A_bc[:, g * 8:(g + 1) * 8], 1, NT), op=ALU.mult), reads=["dtt", "Abc"], writes=["att"])

                def run_chunk(ch, wt, wcol, kind, ci):
                    if ph == 0:
                        P.op("pool", lambda e: e.memset(xpre[:, 0:3], 0.0), writes=["xpre"])
                    else:
                        P.op("pool", lambda e: e.tensor_copy(out=xpre[:, 0:3], in_=ctail[:, j, ch, :]), reads=["ctail"], writes=["xpre"])
                    if NT == 9:
                        P.op("sp", lambda e: e.dma_start(out=xpre[:, SOFF:SOFF + 3], in_=I["state_conv"][j][:, ch * 128:(ch + 1) * 128].rearrange("r p -> p r")), writes=["xpre"], dma=True)

                    def dst_fn(pp, ppk, c0, n):
                        o = 3 + c0 if c0 < 1024 else SOFF + 3
                        P.op("act", lambda e: e.copy(out=xpre[:, o:o + n], in_=pp[:, 0:n]), writes=[ppk, "xpre"])
                    fm_proj(0, dst_fn, wt, wcol)
                    if ph == 0:
                        P.op("pool", lambda e: e.tensor_copy(out=ctail[:, j, ch, :], in_=xpre[:, 1024:1027]), reads=["xpre"], writes=["ctail"])
                        finals.append(P.op("sp", lambda e: e.dma_start(out=O["conv_s"][j][:, ch * 128:(ch + 1) * 128].rearrange("r p -> p r"), in_=xpre[:, SOFF + 4:SOFF + 7]), reads=["xpre"], dma=True))
                    else:
                        finals.append(P.op("sp", lambda e: e.dma_start(out=O["conv_p"][j][:, ch * 128:(ch + 1) * 128].rearrange("r p -> p r"), in_=xpre[:, 1024:1027]), reads=["xpre"], dma=True))
                    if kind == "C":
                        conv_chunk(ch, cT, "cT", False)
                        return
                    conv_chunk(ch, xfT, "xcv", True)
                    if kind == "B":
                        P.op("dve", lambda e: e.tensor_copy(out=bT, in_=xfT), reads=["xcv"], writes=["bT"])
                    for t0 in range(0, NT, 4):
                        nt = min(4, NT - t0)
                        P.op("pe", lambda e: e.nop(), writes=["pf"])
                        for q in range(nt):
                            P.op("pe", lambda e, q=q, t0=t0: e.transpose(out=pf[:, q * 128:(q + 1) * 128], in_=xfT[:, (t0 + q) * 128:(t0 + q + 1) * 128], identity=ident[:]), reads=["xcv", "ident"], writes=["pf"], pe_acc=True)
                        if kind == "B":
                            P.op("act", lambda e, t0=t0, nt=nt: e.copy(out=btm[:, t0:t0 + nt, :], in_=pf[:, 0:nt * 128].rearrange("p (a b) -> p a b", a=nt)), writes=["pf", "btm"])
                        else:
                            P.op("act", lambda e, t0=t0, nt=nt, ci=ci: e.copy(out=xtm[:, t0:t0 + nt, ci * 128:(ci + 1) * 128], in_=pf[:, 0:nt * 128].rearrange("p (a b) -> p a b", a=nt)), writes=["pf", "xtm"])

                run_chunk(32 + g, w, 0, "B", 0)
                run_chunk(40 + g, w, 128, "C", 0)
                for half in range(2):
                    w = wb[1 - half]
                    load_w(w[:, :, 0:256], w_in, 0, KC, 4096 + g * 512 + half * 256, 256, "wb")
                    for q in range(2):
                        run_chunk(g * 4 + half * 2 + q, w, q * 128, "x", half * 2 + q)

                def chunk(tt, st, stb, stk):
                    a_t = att[:, tt, :]
                    d_t = dtt[:, tt, :]
                    P.op("pe", lambda e: e.nop(), writes=["pf"])
                    P.op("pe", lambda e: e.matmul(pf[:, 0:8], lhsT=Tri[:], rhs=a_t, start=True, stop=True), reads=["att", "Tri"], writes=["pf"], pe_acc=True)
                    P.op("pe", lambda e: e.matmul(pf[:, 8:16], lhsT=ones[:], rhs=a_t, start=True, stop=True), reads=["att", "ones"], writes=["pf"], pe_acc=True)
                    P.op("act", lambda e: e.activation(out=sc[:, 0:16], in_=pf[:, 0:16], func=AF.Exp), writes=["pf", "sc"])
                    P.op("dve", lambda e: e.tensor_tensor(out=sc[:, 24:32], in0=pf[:, 8:16], in1=pf[:, 0:8], op=ALU.subtract) if False else e.tensor_copy(out=sc[:, 32:48], in_=pf[:, 0:16]), writes=["pf", "sc2"])
                    P.op("dve", lambda e: e.tensor_tensor(out=sc[:, 24:32], in0=sc[:, 40:48], in1=sc[:, 32:40], op=ALU.subtract), reads=["sc2"], writes=["sc3"])
                    P.op("act", lambda e: e.activation(out=sc[:, 24:32], in_=sc[:, 24:32], func=AF.Exp), writes=["sc3"])
                    P.op("dve", lambda e: e.tensor_tensor(out=sc[:, 16:24], in0=sc[:, 24:32], in1=d_t, op=ALU.mult), reads=["sc3", "dtt"], writes=["sc4"])
                    P.op("dve", lambda e: e.tensor_tensor(out=xdt.rearrange("p (h d) -> p h d", h=8), in0=xtm[:, tt, :].rearrange("p (h d) -> p h d", h=8), in1=bc(d_t, 2, 64), op=ALU.mult), reads=["xtm", "dtt"], writes=["xdt"])
                    P.op("dve", lambda e: e.tensor_tensor(out=xw.rearrange("p (h d) -> p h d", h=8), in0=xtm[:, tt, :].rearrange("p (h d) -> p h d", h=8), in1=bc(sc[:, 16:24], 2, 64), op=ALU.mult), reads=["xtm", "sc4"], writes=["xw"])
                    pc, pck = next_pm()
                    P.op("pe", lambda e: e.matmul(pc[:, 0:128], lhsT=bT[:, tt * 128:(tt + 1) * 128], rhs=cT[:, tt * 128:(tt + 1) * 128], start=True, stop=True), reads=["bT", "cT"], writes=[pck])
                    P.op("act", lambda e: e.copy(out=cbs, in_=pc[:, 0:128]), writes=[pck, "cbs"])
                    py, pyk = pyb, "pyb"
                    for r in range(8):
                        P.op("dve", lambda e, r=r: e.tensor_scalar(out=lT, in0=SL[:], scalar1=att[:, tt, r:r + 1], scalar2=None, op0=ALU.mult), reads=["SL", "att"], writes=["lT"])
                        pg_, pgk_ = next_pm()
                        P.op("pe", lambda e, pg_=pg_: e.matmul(pg_[:, 0:128], lhsT=lT, rhs=Tri[:], start=True, stop=False), reads=["lT", "Tri"], writes=[pgk_], pe_acc=True)
                        P.op("pe", lambda e, pg_=pg_: e.matmul(pg_[:, 0:128], lhsT=ident[:], rhs=NEG[:], start=False, stop=True), reads=["ident", "NEG"], writes=[pgk_], pe_acc=True)
                        P.op("act", lambda e, pg_=pg_: e.activation(out=ex, in_=pg_[:, 0:128], func=AF.Exp), writes=[pgk_, "ex"])
                        P.op("dve", lambda e: e.tensor_tensor(out=mT, in0=ex, in1=cbs, op=ALU.mult), reads=["ex", "cbs"], writes=["mT"])
                        P.op("pe", lambda e, r=r, py=py: e.matmul(py[:, r * 64:(r + 1) * 64], lhsT=mT, rhs=xdt[:, r * 64:(r + 1) * 64], start=True, stop=True), reads=["mT", "xdt"], writes=[pyk], pe_acc=True)
                    po_, pok_ = next_pm()
                    P.op("pe", lambda e, po_=po_: e.matmul(po_[:], lhsT=cT[:, tt * 128:(tt + 1) * 128], rhs=stb, start=True, stop=True), reads=["cT", stk + "b"], writes=[pok_])
                    P.op("dve", lambda e, po_=po_: e.tensor_tensor(out=yy.rearrange("p (h d) -> p h d", h=8), in0=po_[:].rearrange("p (h d) -> p h d", h=8), in1=bc(sc[:, 0:8], 2, 64), op=ALU.mult), reads=["sc"], writes=[pok_, "yy"])
                    P.op("dve", lambda e, py=py: e.tensor_tensor(out=yy, in0=py[:], in1=yy, op=ALU.add), writes=[pyk, "yy"])
                    P.op("dve", lambda e, g=g: e.tensor_tensor(out=gg.rearrange("p (h d) -> p h d", h=8), in0=xtm[:, tt, :].rearrange("p (h d) -> p h d", h=8), in1=bc(D_bc[:, g * 8:(g + 1) * 8], 2, 64), op=ALU.mult), reads=["xtm", "Dbc"], writes=["gg"])
                    P.op("dve", lambda e: e.tensor_tensor(out=yy, in0=yy, in1=gg, op=ALU.add), reads=["gg"], writes=["yy"])
                    P.op("dve", lambda e: e.tensor_tensor(out=gg, in0=yy, in1=zs[:, tt, :], op=ALU.mult), reads=["yy", ("zs", tt)], writes=["gg"])
                    P.op("act", lambda e: e.activation(out=yy, in_=gg, func=AF.Square, accum_out=small[:, 16:17]), reads=["gg"], writes=["yy", "gss"])
                    P.op("act", lambda e: e.activation(out=small[:, 17:18], in_=small[:, 16:17], func=AF.Sqrt, scale=1.0 / 512, bias=small[:, 8:9]), reads=["gss", "epsc"], writes=["gstd"])
                    P.op("dve", lambda e: e.reciprocal(out=small[:, 18:19], in_=small[:, 17:18]), reads=["gstd"], writes=["grstd"])
                    P.op("dve", lambda e: e.scalar_tensor_tensor(out=gnb, in0=gg, scalar=small[:, 18:19], in1=gnw_bc, op0=ALU.mult, op1=ALU.mult), reads=["gg", "grstd", "gnw"], writes=["gnb"])
                    pt, ptk = next_pt()
                    for q in range(4):
                        P.op("pe", lambda e, q=q, pt=pt: e.transpose(out=pt[:, q * 128:(q + 1) * 128], in_=gnb[:, q * 128:(q + 1) * 128], identity=identb[:]), reads=["gnb", "identb"], writes=[ptk], pe_acc=True)
                    P.op("act", lambda e, pt=pt: e.copy(out=gnT[:, :, tt * 128:(tt + 1) * 128], in_=pt[:, 0:512].rearrange("p (a b) -> p a b", a=4)), writes=[ptk, ("gnT", tt)])
                    ps_, psk_ = next_pm()
                    P.op("pe", lambda e, ps_=ps_: e.matmul(ps_[:], lhsT=btm[:, tt, :], rhs=xw, start=True, stop=True), reads=["btm", "xw"], writes=[psk_])
                    P.op("dve", lambda e: e.tensor_tensor(out=st.rearrange("p (h d) -> p h d", h=8), in0=st.rearrange("p (h d) -> p h d", h=8), in1=bc(sc[:, 8:16], 2, 64), op=ALU.mult), reads=["sc"], writes=[stk])
                    P.op("dve", lambda e, ps_=ps_: e.tensor_tensor(out=st, in0=ps_[:], in1=st, op=ALU.add), writes=[psk_, stk])
                    P.op("act", lambda e: e.copy(out=stb, in_=st), reads=[stk], writes=[stk + "b"])

                def store_state(st, stk, dst):
                    P.op("pe", lambda e: e.nop(), writes=["pf"])
                    for q in range(4):
                        P.op("pe", lambda e, q=q: e.transpose(out=pf[:, q * 128:(q + 1) * 128], in_=st[:, q * 128:(q + 1) * 128], identity=ident[:]), reads=[stk, "ident"], writes=["pf"], pe_acc=True)
                    P.op("act", lambda e: e.copy(out=st_in, in_=pf[:]), writes=["pf", "xpre"])
                    finals.append(P.op("sp", lambda e, g=g: e.dma_start(out=dst[g * 512:(g + 1) * 512, :].rearrange("(q p) n -> p q n", p=128), in_=st_in.rearrange("p (q n) -> p q n", q=4)), reads=["xpre"], dma=True))

                if ph == 0:
                    P.op("pool", lambda e: e.memset(hT, 0.0), writes=["hT"])
                else:
                    P.op("sp", lambda e, g=g: e.dma_start(out=hT, in_=S["hsave"][j][:, g * 512:(g + 1) * 512]), writes=["hT"], dma=True)
                P.op("act", lambda e: e.copy(out=hTb, in_=hT), reads=["hT"], writes=["hTb"])
                for tt in range(8):
                    chunk(tt, hT, hTb, "hT")
                if ph == 0:
                    P.op("sp", lambda e, g=g: e.dma_start(out=S["hsave"][j][:, g * 512:(g + 1) * 512], in_=hT), reads=["hT"], writes=["hsave"], dma=True)
                else:
                    store_state(hT, "hT", O["ssm_p"][j])
                if NT == 9:
                    P.op("sp", lambda e, g=g: e.dma_start(out=st_in.rearrange("p (q n) -> p q n", q=4), in_=I["state_ssm"][j][g * 512:(g + 1) * 512, :].rearrange("(q p) n -> p q n", p=128)), writes=["xpre"], dma=True)
                    P.op("pe", lambda e: e.nop(), writes=["pf"])
                    for q in range(4):
                        P.op("pe", lambda e, q=q: e.transpose(out=pf[:, q * 128:(q + 1) * 128], in_=st_in[:, q * 128:(q + 1) * 128], identity=ident[:]), reads=["xpre", "ident"], writes=["pf"], pe_acc=True)
                    P.op("act", lambda e: e.copy(out=hS, in_=pf[:]), writes=["pf", "hT"])
                    P.op("act", lambda e: e.copy(out=hTb, in_=hS), reads=["hT"], writes=["hTb"])
                    chunk(8, hS, hTb, "hT")
                    store_state(hS, "hT", O["ssm_s"][j])

                for cg in range(4):
                    w_o = wo[cg % 2]
                    load_w(w_o, I["ssd_w_out"][j], g * 512, 4, cg * 512, 512, ("wo", 0))
                    for tt in range(NT):
                        pp, ppk = next_pm()
                        for ci in range(4):
                            P.op("pe", lambda e, pp=pp, ci=ci, tt=tt, w_o=w_o: e.matmul(pp[:], lhsT=gnT[:, ci, tt * 128:(tt + 1) * 128], rhs=w_o[:, ci, :], start=(ci == 0), stop=(ci == 3)), reads=[("wo", 0), ("gnT", tt)], writes=[ppk], pe_acc=True)
                        P.op("dve", lambda e, pp=pp, tt=tt, cg=cg: e.tensor_tensor(out=R[:, tt, cg * 512:(cg + 1) * 512], in0=pp[:], in1=R[:, tt, cg * 512:(cg + 1) * 512], op=ALU.add), writes=[ppk, ("R", tt)])

        def moba_sample(QsTf, KsT, Vs, cv0):
            P.barrier()
            cv = Carver()
            cv.off = cv0.keep
            SCL = 1.0 / math.sqrt(128.0)
            pti = cv.f32(128).bitcast(I32)
            ptf = cv.f32(128)
            idx = cv.f32(128).bitcast(I32)
            pio = cv.f32(1)
            Qbdf = cv.f32(16 * 64).rearrange("p (h n) -> p h n", h=16)
            Qbd = cv.bf16(16 * 64).rearrange("p (h n) -> p h n", h=16)
            BD = cv.bf16(16 * 64).rearrange("p (h n) -> p h n", h=16)
            cm = cv.f32(4)
            kpage = cv.f32(2 * D).rearrange("p (i n) -> p i n", i=2)
            vpage = cv.bf16(2 * D).rearrange("p (i n) -> p i n", i=2)
            KTs = cv.bf16(16 * 256).rearrange("p (h n) -> p h n", h=16)
            ksumS = cv.f32(1024)
            gs = cv.f32(64)
            top8 = cv.f32(8)
            als = cv.f32(64)
            lacc = cv.f32(66)
            Pfs = cv.f32(256)
            Pbs = cv.bf16(256)
            PTs = cv.bf16(2 * 64).rearrange("p (i n) -> p i n", i=2)
            PTx = cv.bf16(2 * 16 * 64).rearrange("p (i h n) -> p i h n", i=2, h=16)
            o2 = cv.bf16(128)
            oTs = cv.bf16(64)
            wos = cv.bf16(16 * 512).rearrange("p (k n) -> p k n", k=16)
            P.op("sp", lambda e: e.dma_start(out=pti, in_=bass.AP(I["page_table"].tensor, 0, [[0, 128], [1, 128]])), writes=["pti"], dma=True)
            P.op("pool", lambda e: e.iota(pio.bitcast(I32), [[0, 1]], base=0, channel_multiplier=1), writes=["pio"])
            P.op("dve", lambda e: e.tensor_copy(out=ptf, in_=pti), reads=["pti"], writes=["ptf"])
            P.op("dve", lambda e: e.tensor_copy(out=pio, in_=pio.bitcast(I32)), writes=["pio"])
            P.op("dve", lambda e: e.tensor_scalar(out=ptf, in0=ptf, scalar1=128.0, scalar2=pio[:, 0:1], op0=ALU.mult, op1=ALU.add), reads=["pio"], writes=["ptf"])
            P.op("dve", lambda e: e.tensor_copy(out=idx, in_=ptf), reads=["ptf"], writes=["idx"])
            P.op("sp", lambda e: e.dma_start(out=cm[0:64, :], in_=I["own_s"]), writes=["cm"], dma=True)
            P.op("pool", lambda e: e.memset(Qbdf, 0.0), writes=["Qbdf"])
            P.op("pool", lambda e: e.memset(BD, 0.0), writes=["BD"])
            for h in range(16):
                P.op("pool", lambda e, h=h: e.tensor_copy(out=Qbdf[:, h, 4 * h:4 * h + 4], in_=QsTf[:, h, :]), writes=["Qbdf"])
                P.op("pool", lambda e, h=h: e.memset(BD[:, h, 4 * h:4 * h + 4], 1.0), writes=["BD"])
            P.op("dve", lambda e: e.tensor_copy(out=Qbd, in_=Qbdf), reads=["Qbdf"], writes=["Qbd"])
            P.op("pool", lambda e: e.memset(lacc, 0.0), writes=["lacc"])

            def gather(dst, src, lp, key):
                return P.op("pool", lambda e: e.indirect_dma_start(out=dst, out_offset=None, in_=src, in_offset=bass.IndirectOffsetOnAxis(ap=idx[:, lp:lp + 1], axis=0)), reads=["idx"], writes=[key], dma=True)

            for blk in range(64):
                for i2 in range(2):
                    gather(kpage[:, i2, :], I["cache_k"], 2 * blk + i2, "kpage")
                for hh in range(16):
                    bank = pm[hh // 8]
                    col = (hh % 8) * 64 + blk
                    for i2 in range(2):
                        P.op("pe", lambda e, bank=bank, col=col, hh=hh, i2=i2: e.matmul(bank[:, col:col + 1], lhsT=kpage[:, i2, hh * 128:(hh + 1) * 128], rhs=ones[:, 0:1], start=(i2 == 0), stop=(i2 == 1)), reads=["kpage", "ones"], writes=["pm%d" % (hh // 8)], pe_acc=True)
            for b2 in range(2):
                P.op("act", lambda e, b2=b2: e.copy(out=ksumS[:, b2 * 512:(b2 + 1) * 512], in_=pm[b2][:]), writes=["pm%d" % b2, "ksumS"])
            for hh in range(16):
                P.op("pe", lambda e, hh=hh: e.matmul(pf[0:64, 0:64], lhsT=Qbdf[:, hh, :], rhs=ksumS[:, hh * 64:(hh + 1) * 64], start=(hh == 0), stop=(hh == 15)), reads=["Qbdf", "ksumS"], writes=["pf"], pe_acc=True)
            P.op("act", lambda e: e.copy(out=gs[0:64, :], in_=pf[0:64, 0:64]), writes=["pf", "gs"])
            P.op("dve", lambda e: e.max(out=top8[0:64, :], in_=gs[0:64, :]), reads=["gs"], writes=["top8"])
            P.op("dve", lambda e: e.tensor_scalar(out=als[0:64, :], in0=gs[0:64, :], scalar1=top8[0:64, 2:3], scalar2=None, op0=ALU.is_ge), reads=["gs", "top8"], writes=["als"])

            first = [True]

            def pv(lhs_fn, rhs_fn, reads, last):
                for hh in range(16):
                    st = first[0]
                    first[0] = False
                    P.op("pe", lambda e, hh=hh, st=st: e.matmul(pyb[0:64, 0:128], lhsT=lhs_fn(hh), rhs=rhs_fn(hh), start=st, stop=(last and hh == 15)), reads=reads, writes=["pyb"], pe_acc=True)

            for blk in range(64):
                for i2 in range(2):
                    gather(kpage[:, i2, :], I["cache_k"], 2 * blk + i2, "kpage")
                    gather(vpage[:, i2, :], I["cache_v"], 2 * blk + i2, "vpage")
                for hp in range(8):
                    P.op("pe", lambda e: e.nop(), writes=["pf"])
                    for q in range(4):
                        hh, i2 = 2 * hp + q // 2, q % 2
                        P.op("pe", lambda e, q=q, hh=hh, i2=i2: e.transpose(out=pf[:, q * 128:(q + 1) * 128], in_=kpage[:, i2, hh * 128:(hh + 1) * 128], identity=ident[:]), reads=["kpage", "ident"], writes=["pf"], pe_acc=True)
                    P.op("act", lambda e, hp=hp: e.copy(out=KTs[:, 2 * hp:2 * hp + 2, :], in_=pf[:].rearrange("p (a b) -> p a b", a=2)), writes=["pf", "KTs"])
                ps_, psk_ = next_pm()
                for hh in range(16):
                    P.op("pe", lambda e, ps_=ps_, hh=hh: e.matmul(ps_[0:64, 0:256], lhsT=Qbd[:, hh, :], rhs=KTs[:, hh, :], start=(hh == 0), stop=(hh == 15)), reads=["Qbd", "KTs"], writes=[psk_], pe_acc=True)
                P.op("act", lambda e, ps_=ps_: e.activation(out=Pfs[0:64, :], in_=ps_[0:64, 0:256], func=AF.Exp, scale=SCL), writes=[psk_, "Pfs"])
                P.op("dve", lambda e, blk=blk: e.tensor_scalar(out=Pbs[0:64, :], in0=Pfs[0:64, :], scalar1=als[0:64, blk:blk + 1], scalar2=None, op0=ALU.mult, op1=ALU.add, accum_out=lacc[0:64, blk:blk + 1]), reads=["Pfs", "als"], writes=["Pbs", "lacc"])
                pt, ptk = next_pt()
                for i2 in range(2):
                    P.op("pe", lambda e, pt=pt, i2=i2: e.transpose(out=pt[:, i2 * 64:(i2 + 1) * 64], in_=Pbs[0:64, i2 * 128:(i2 + 1) * 128], identity=identb[0:64, 0:64]), reads=["Pbs", "identb"], writes=[ptk], pe_acc=True)
                P.op("act", lambda e, pt=pt: e.copy(out=PTs, in_=pt[:, 0:128].rearrange("p (a b) -> p a b", a=2)), writes=[ptk, "PTs"])
                for i2 in range(2):
                    P.op("dve", lambda e, i2=i2: e.tensor_tensor(out=PTx[:, i2, :, :], in0=bc(PTs[:, i2, :], 1, 16), in1=BD, op=ALU.mult), reads=["PTs", "BD"], writes=["PTx"])
                for i2 in range(2):
                    pv(lambda hh, i2=i2: PTx[:, i2, hh, :], lambda hh, i2=i2: vpage[:, i2, hh * 128:(hh + 1) * 128], ["PTx", "vpage"], False)
            ps_, psk_ = next_pm()
            for hh in range(16):
                P.op("pe", lambda e, ps_=ps_, hh=hh: e.matmul(ps_[0:64, 0:4], lhsT=Qbd[:, hh, :], rhs=KsT[:, hh, :], start=(hh == 0), stop=(hh == 15)), reads=["Qbd"], writes=[psk_], pe_acc=True)
            P.op("act", lambda e, ps_=ps_: e.activation(out=Pfs[0:64, 0:4], in_=ps_[0:64, 0:4], func=AF.Exp, scale=SCL), writes=[psk_, "Pfs"])
            P.op("dve", lambda e: e.tensor_tensor(out=Pfs[0:64, 0:4], in0=Pfs[0:64, 0:4], in1=cm[0:64, :], op=ALU.mult), reads=["cm"], writes=["Pfs"])
            P.op("dve", lambda e: e.tensor_scalar(out=Pbs[0:64, 0:4], in0=Pfs[0:64, 0:4], scalar1=1.0, scalar2=None, op0=ALU.mult, op1=ALU.add, accum_out=lacc[0:64, 64:65]), reads=["Pfs"], writes=["Pbs", "lacc"])
            pt, ptk = next_pt()
            P.op("pe", lambda e, pt=pt: e.transpose(out=pt[0:4, 0:64], in_=Pbs[0:64, 0:4], identity=identb[0:64, 0:64]), reads=["Pbs", "identb"], writes=[ptk])
            P.op("act", lambda e, pt=pt: e.copy(out=PTs[0:4, 0, :], in_=pt[0:4, 0:64]), writes=[ptk, "PTs"])
            P.op("dve", lambda e: e.tensor_tensor(out=PTx[0:4, 0, :, :], in0=bc(PTs[0:4, 0, :], 1, 16), in1=BD[0:4], op=ALU.mult), reads=["PTs", "BD"], writes=["PTx"])
            pv(lambda hh: PTx[0:4, 0, hh, :], lambda hh: Vs[0:4, hh * 128:(hh + 1) * 128], ["PTx"], True)
            P.op("dve", lambda e: e.reduce_sum(out=small[0:64, 24:25], in_=lacc[0:64, 0:65], axis=AX.X), reads=["lacc"], writes=["lsum"])
            P.op("dve", lambda e: e.reciprocal(out=small[0:64, 25:26], in_=small[0:64, 24:25]), reads=["lsum"], writes=["lrec"])
            P.op("dve", lambda e: e.tensor_scalar(out=o2[0:64, :], in0=pyb[0:64, 0:128], scalar1=small[0:64, 25:26], scalar2=None, op0=ALU.mult), reads=["lrec"], writes=["pyb", "o2"])
            pt, ptk = next_pt()
            P.op("pe", lambda e, pt=pt: e.transpose(out=pt[:, 0:64], in_=o2[0:64, :], identity=identb[0:64, 0:64]), reads=["o2", "identb"], writes=[ptk])
            P.op("act", lambda e, pt=pt: e.copy(out=oTs, in_=pt[:, 0:64]), writes=[ptk, "oTs"])
            for cg in range(4):
                load_w(wos, I["moba_w_o"][0], 0, 16, cg * 512, 512, "wos")
                pp, ppk = next_pm()
                for hh in range(16):
                    P.op("pe", lambda e, pp=pp, hh=hh: e.matmul(pp[0:4, :], lhsT=oTs[:, 4 * hh:4 * hh + 4], rhs=wos[:, hh, :], start=(hh == 0), stop=(hh == 15)), reads=["oTs", "wos"], writes=[ppk], pe_acc=True)
                P.op("dve", lambda e, pp=pp, cg=cg: e.tensor_tensor(out=R[0:4, 8, cg * 512:(cg + 1) * 512], in0=pp[0:4, :], in1=R[0:4, 8, cg * 512:(cg + 1) * 512], op=ALU.add), writes=[ppk, ("R", 8)])

        def moba(li, ph, NT):
            P.barrier()
            cv = Carver()
            xs_scr = cv.bf16(D)
            make_xnT(li, NT, xs_scr)
            P.barrier()
            cv = Carver()
            T = NT * 128
            NKT = (ph + 1) * 8
            SCL = 1.0 / math.sqrt(128.0)
            QsTf = cv.f32(64).rearrange("p (h t) -> p h t", h=16)
            KsT = cv.bf16(64).rearrange("p (h t) -> p h t", h=16)
            Vs = cv.bf16(D)
            cv.keep = cv.off
            wq = cv.bf16(KC * 384).rearrange("p (k n) -> p k n", k=KC)
            qkv = cv.f32(NT * 384).rearrange("p (t n) -> p t n", t=NT)
            rope = cv.f32(NT * 32).rearrange("p (t n) -> p t n", t=NT)
            validt = cv.f32(64).rearrange("p (t n) -> p t n", t=8)
            negt = cv.f32(64).rearrange("p (t n) -> p t n", t=8)
            ownt = cv.f32(64).rearrange("p (t n) -> p t n", t=8)
            rt = [cv.f32(NT * 16).rearrange("p (t n) -> p t n", t=NT) for _ in range(4)]
            ksumT = cv.f32(8)
            QTf = cv.f32(T)
            QT = cv.bf16(T)
            KT = cv.bf16(2048)
            Vb = cv.bf16(16 * 130).rearrange("p (t n) -> p t n", t=16)
            alw = cv.f32(64).rearrange("p (t n) -> p t n", t=8)
            gm = cv.f32(8)
            top8 = cv.f32(8)
            Pf = [cv.f32(512) for _ in range(2)]
            Pb = [cv.bf16(512) for _ in range(2)]
            PT = [cv.bf16(512) for _ in range(2)]
            ob = cv.bf16(128)
            oT = cv.bf16(4 * 1024).rearrange("p (k n) -> p k n", k=4)
            wo = cv.bf16(4 * 512).rearrange("p (k n) -> p k n", k=4)
            print("moba arena", cv.off)
            wqkv = I["moba_w_qkv"][0]
            P.op("pool", lambda e: e.memset(rope, 0.0), writes=["rope"])
            P.op("sp", lambda e: e.dma_start(out=rope[:, 0:8, :], in_=I["rope_p"][ph * TP:(ph + 1) * TP, :].rearrange("(t p) n -> p t n", p=128)), writes=["rope"], dma=True)
            if NT == 9:
                P.op("sp", lambda e: e.dma_start(out=rope[0:4, 8, :], in_=I["rope_s"]), writes=["rope"], dma=True)
            for (dst, nm) in ((validt, "valid_p"), (ownt, "own_p")):
                P.op("sp", lambda e, dst=dst, nm=nm: e.dma_start(out=dst, in_=I[nm][ph * TP:(ph + 1) * TP, :].rearrange("(t p) n -> p t n", p=128)), writes=[nm], dma=True)
            P.op("dve", lambda e: e.tensor_scalar(out=negt, in0=validt, scalar1=-1.0, scalar2=1e30, op0=ALU.add, op1=ALU.mult), reads=["valid_p"], writes=["negt"])
            P.op("pool", lambda e: e.memset(Vb, 1.0), writes=["Vb"])

            for h in range(16):
                for i3 in range(3):
                    load_w(wq[:, :, i3 * 128:(i3 + 1) * 128], wqkv, 0, KC, i3 * 2048 + h * 128, 128, "wq")
                for tt in range(NT):
                    pp, ppk = next_pm()
                    for kc in range(KC):
                        P.op("pe", lambda e, pp=pp, kc=kc, tt=tt: e.matmul(pp[:, 0:384], lhsT=xnT[:, kc, tt * 128:(tt + 1) * 128], rhs=wq[:, kc, :], start=(kc == 0), stop=(kc == KC - 1)), reads=["wq", ("xnT", tt)], writes=[ppk], pe_acc=True)
                    P.op("act", lambda e, pp=pp, tt=tt: e.copy(out=qkv[:, tt, :], in_=pp[:, 0:384]), writes=[ppk, "qkv"])
                for off in (0, 128):
                    x1 = qkv[:, :, off:off + 16]
                    x2 = qkv[:, :, off + 16:off + 32]
                    cs, sn = rope[:, :, 0:16], rope[:, :, 16:32]
                    P.op("dve", lambda e, x1=x1, cs=cs: e.tensor_tensor(out=rt[0], in0=x1, in1=cs, op=ALU.mult), reads=["qkv", "rope"], writes=["rt0"])
                    P.op("dve", lambda e, x2=x2, sn=sn: e.tensor_tensor(out=rt[1], in0=x2, in1=sn, op=ALU.mult), reads=["qkv", "rope"], writes=["rt1"])
                    P.op("dve", lambda e, x1=x1, sn=sn: e.tensor_tensor(out=rt[2], in0=x1, in1=sn, op=ALU.mult), reads=["qkv", "rope"], writes=["rt2"])
                    P.op("dve", lambda e, x2=x2, cs=cs: e.tensor_tensor(out=rt[3], in0=x2, in1=cs, op=ALU.mult), reads=["qkv", "rope"], writes=["rt3"])
                    P.op("dve", lambda e, x1=x1: e.tensor_tensor(out=x1, in0=rt[0], in1=rt[1], op=ALU.subtract), reads=["rt0", "rt1"], writes=["qkv"])
                    P.op("dve", lambda e, x2=x2: e.tensor_tensor(out=x2, in0=rt[2], in1=rt[3], op=ALU.add), reads=["rt2", "rt3"], writes=["qkv"])
                r0 = ph * TP
                finals.append(P.op("sp", lambda e, h=h, r0=r0: e.dma_start(out=O["k_p"][r0:r0 + TP, h * 128:(h + 1) * 128].rearrange("(t p) n -> p t n", p=128), in_=qkv[:, 0:8, 128:256]), reads=["qkv"], dma=True))
                finals.append(P.op("sp", lambda e, h=h, r0=r0: e.dma_start(out=O["v_p"][r0:r0 + TP, h * 128:(h + 1) * 128].rearrange("(t p) n -> p t n", p=128), in_=qkv[:, 0:8, 256:384]), reads=["qkv"], dma=True))
                if NT == 9:
                    finals.append(P.op("sp", lambda e, h=h: e.dma_start(out=O["k_s"][:, h * 128:(h + 1) * 128], in_=qkv[0:4, 8, 128:256]), reads=["qkv"], dma=True))
                    finals.append(P.op("sp", lambda e, h=h: e.dma_start(out=O["v_s"][:, h * 128:(h + 1) * 128], in_=qkv[0:4, 8, 256:384]), reads=["qkv"], dma=True))
                if ph == 0:
                    P.op("pool", lambda e: e.memset(ksumT, 0.0), writes=["ksumT"])
                else:
                    P.op("sp", lambda e, h=h: e.dma_start(out=ksumT[:, 0:4], in_=S["ksum"][h]), writes=["ksumT"], dma=True)
                P.op("pe", lambda e: e.nop(), writes=["pf"])
                for b4 in range(4):
                    for i2 in range(2):
                        P.op("pe", lambda e, b4=b4, i2=i2: e.matmul(pf[:, b4:b4 + 1], lhsT=qkv[:, 2 * b4 + i2, 128:256], rhs=ones[:, 0:1], start=(i2 == 0), stop=(i2 == 1)), reads=["qkv", "ones"], writes=["pf"], pe_acc=True)
                P.op("act", lambda e: e.copy(out=ksumT[:, ph * 4:ph * 4 + 4], in_=pf[:, 0:4]), writes=["pf", "ksumT"])
                if ph == 0:
                    P.op("sp", lambda e, h=h: e.dma_start(out=S["ksum"][h], in_=ksumT[:, 0:4]), reads=["ksumT"], writes=["ksum_scr"], dma=True)
                if ph == 1:
                    P.op("sp", lambda e, h=h: e.dma_start(out=KT[:, 0:1024], in_=S["kT"][h]), writes=["KT"], dma=True)
                    P.op("sp", lambda e, h=h: e.dma_start(out=Vb[:, 0:8, 0:128], in_=S["v"][h]), writes=["Vb"], dma=True)
                for (src_off, kind) in ((0, "q"), (128, "k")):
                    for t0 in range(0, NT, 4):
                        nt = min(4, NT - t0)
                        P.op("pe", lambda e: e.nop(), writes=["pf"])
                        for q in range(nt):
                            P.op("pe", lambda e, q=q, t0=t0, src_off=src_off: e.transpose(out=pf[:, q * 128:(q + 1) * 128], in_=qkv[:, t0 + q, src_off:src_off + 128], identity=ident[:]), reads=["qkv", "ident"], writes=["pf"], pe_acc=True)
                        if kind == "q":
                            P.op("act", lambda e, t0=t0, nt=nt: e.copy(out=QTf[:, t0 * 128:(t0 + nt) * 128], in_=pf[:, 0:nt * 128]), writes=["pf", "QTf"])
                            P.op("dve", lambda e, t0=t0, nt=nt: e.tensor_copy(out=QT[:, t0 * 128:(t0 + nt) * 128], in_=QTf[:, t0 * 128:(t0 + nt) * 128]), reads=["QTf"], writes=["QT"])
                            if t0 == 8:
                                P.op("pool", lambda e, h=h: e.tensor_copy(out=QsTf[:, h, :], in_=QTf[:, 1024:1028]), reads=["QTf"], writes=["QsTf"])
                        else:
                            if t0 < 8:
                                P.op("act", lambda e, t0=t0, nt=nt: e.copy(out=KT[:, ph * 1024 + t0 * 128:ph * 1024 + (t0 + nt) * 128], in_=pf[:, 0:nt * 128]), writes=["pf", "KT"])
                            else:
                                P.op("act", lambda e, h=h: e.copy(out=KsT[:, h, :], in_=pf[:, 0:4]), writes=["pf", "KsT"])
                P.op("act", lambda e: e.copy(out=Vb[:, ph * 8:ph * 8 + 8, 0:128], in_=qkv[:, 0:8, 256:384]), reads=["qkv"], writes=["Vb"])
                if NT == 9:
                    P.op("act", lambda e, h=h: e.copy(out=Vs[0:4, h * 128:(h + 1) * 128], in_=qkv[0:4, 8, 256:384]), reads=["qkv"], writes=["Vs"])
                if ph == 0:
                    P.op("sp", lambda e, h=h: e.dma_start(out=S["kT"][h], in_=KT[:, 0:1024]), reads=["KT"], writes=["kT_scr"], dma=True)
                    P.op("sp", lambda e, h=h: e.dma_start(out=S["v"][h], in_=Vb[:, 0:8, 0:128]), reads=["Vb"], writes=["v_scr"], dma=True)
                for tt in range(8):
                    P.op("pe", lambda e: e.nop(), writes=["pf"])
                    P.op("pe", lambda e, tt=tt: e.matmul(pf[:, 0:8], lhsT=QTf[:, tt * 128:(tt + 1) * 128], rhs=ksumT, start=True, stop=True), reads=["QTf", "ksumT"], writes=["pf"], pe_acc=True)
                    P.op("dve", lambda e, tt=tt: e.tensor_tensor(out=gm, in0=pf[:, 0:8], in1=validt[:, tt, :], op=ALU.mult), reads=["valid_p"], writes=["pf", "gm"])
                    P.op("dve", lambda e, tt=tt: e.tensor_tensor(out=gm, in0=gm, in1=negt[:, tt, :], op=ALU.add), reads=["negt"], writes=["gm"])
                    P.op("dve", lambda e: e.max(out=top8, in_=gm), reads=["gm"], writes=["top8"])
                    P.op("dve", lambda e, tt=tt: e.tensor_scalar(out=alw[:, tt, :], in0=gm, scalar1=top8[:, 2:3], scalar2=None, op0=ALU.is_ge), reads=["gm", "top8"], writes=["alw"])
                    P.op("dve", lambda e, tt=tt: e.tensor_tensor(out=alw[:, tt, :], in0=alw[:, tt, :], in1=validt[:, tt, :], op=ALU.mult), reads=["valid_p"], writes=["alw"])
                    P.op("dve", lambda e, tt=tt: e.tensor_tensor(out=alw[:, tt, :], in0=alw[:, tt, :], in1=ownt[:, tt, :], op=ALU.add), reads=["own_p"], writes=["alw"])
                cnt = 0
                for tt in range(8):
                    G = ph * 8 + tt
                    nk = G + 1
                    po, pok = pyb, "pyb"
                    for kg in range(0, nk, 4):
                        ng = min(4, nk - kg)
                        n = ng * 128
                        ps_, psk_ = next_pm()
                        pfb, pbb, ptb_ = Pf[cnt % 2], Pb[cnt % 2], PT[cnt % 2]
                        kf, kb_, kt_ = ("Pf", cnt % 2), ("Pb", cnt % 2), ("PT", cnt % 2)
                        cnt += 1
                        P.op("pe", lambda e, ps_=ps_, tt=tt, kg=kg, n=n: e.matmul(ps_[:, 0:n], lhsT=QT[:, tt * 128:(tt + 1) * 128], rhs=KT[:, kg * 128:kg * 128 + n], start=True, stop=True), reads=["QT", "KT"], writes=[psk_])
                        P.op("act", lambda e, ps_=ps_, pfb=pfb, n=n: e.activation(out=pfb[:, 0:n], in_=ps_[:, 0:n], func=AF.Exp, scale=SCL), writes=[psk_, kf])
                        for b2 in range(0, ng, 2):
                            nb_ = min(2, ng - b2) * 128
                            kbi = (kg + b2) // 2
                            P.op("dve", lambda e, pfb=pfb, pbb=pbb, b2=b2, nb_=nb_, tt=tt, kbi=kbi: e.tensor_scalar(out=pbb[:, b2 * 128:b2 * 128 + nb_], in0=pfb[:, b2 * 128:b2 * 128 + nb_], scalar1=alw[:, tt, kbi:kbi + 1], scalar2=None, op0=ALU.mult), reads=[kf, "alw"], writes=[kb_])
                        if kg + ng == nk:
                            dcol = (ng - 1) * 128
                            P.op("dve", lambda e, pbb=pbb, dcol=dcol: e.tensor_tensor(out=pbb[:, dcol:dcol + 128], in0=pbb[:, dcol:dcol + 128], in1=TriL[:], op=ALU.mult), reads=["TriL"], writes=[kb_])
                        pt, ptk = next_pt()
                        for q in range(ng):
                            P.op("pe", lambda e, pt=pt, pbb=pbb, q=q: e.transpose(out=pt[:, q * 128:(q + 1) * 128], in_=pbb[:, q * 128:(q + 1) * 128], identity=identb[:]), reads=[kb_, "identb"], writes=[ptk], pe_acc=True)
                        P.op("act", lambda e, pt=pt, ptb_=ptb_, n=n: e.copy(out=ptb_[:, 0:n], in_=pt[:, 0:n]), writes=[ptk, kt_])
                        for q in range(ng):
                            kt = kg + q
                            P.op("pe", lambda e, ptb_=ptb_, q=q, kt=kt, nk=nk: e.matmul(po[:, 0:129], lhsT=ptb_[:, q * 128:(q + 1) * 128], rhs=Vb[:, kt, 0:129], start=(kt == 0), stop=(kt == nk - 1)), reads=[kt_, "Vb"], writes=[pok], pe_acc=True)
                    P.op("dve", lambda e: e.reciprocal(out=small[:, 20:21], in_=po[:, 128:129]), writes=[pok, "orec"])
                    P.op("dve", lambda e: e.tensor_scalar(out=ob, in0=po[:, 0:128], scalar1=small[:, 20:21], scalar2=None, op0=ALU.mult), reads=["orec"], writes=[pok, "ob"])
                    pt, ptk = next_pt()
                    P.op("pe", lambda e, pt=pt: e.transpose(out=pt[:, 0:128], in_=ob, identity=identb[:]), reads=["ob", "identb"], writes=[ptk])
                    P.op("act", lambda e, pt=pt, h=h, tt=tt: e.copy(out=oT[:, h % 4, tt * 128:(tt + 1) * 128], in_=pt[:, 0:128]), writes=[ptk, ("oT", tt)])
                if h % 4 == 3:
                    hg = h // 4
                    for cg in range(4):
                        load_w(wo, I["moba_w_o"][0], hg * 512, 4, cg * 512, 512, "wo")
                        for tt in range(8):
                            pp, ppk = next_pm()
                            for ci in range(4):
                                P.op("pe", lambda e, pp=pp, ci=ci, tt=tt: e.matmul(pp[:], lhsT=oT[:, ci, tt * 128:(tt + 1) * 128], rhs=wo[:, ci, :], start=(ci == 0), stop=(ci == 3)), reads=["wo", ("oT", tt)], writes=[ppk], pe_acc=True)
                            P.op("dve", lambda e, pp=pp, tt=tt, cg=cg: e.tensor_tensor(out=R[:, tt, cg * 512:(cg + 1) * 512], in0=pp[:], in1=R[:, tt, cg * 512:(cg + 1) * 512], op=ALU.add), writes=[ppk, ("R", tt)])
            if NT == 9 and not cfg.get("stub_cache"):
                moba_sample(QsTf, KsT, Vs, cv)

        def gmlp(li, ph, NT):
            P.barrier()
            cv = Carver()
            xs_scr = cv.bf16(D)
            make_xnT(li, NT, xs_scr)
            P.barrier()
            cv = Carver()
            VB = cv.bf16(NT * D).rearrange("p (t n) -> p t n", t=NT)
            vs32 = cv.f32(D)
            g_bc = cv.f32(D)
            b_bc = cv.f32(D)
            woff = cv.off
            wb = [cv.bf16(KC * 256).rearrange("p (k n) -> p k n", k=KC) for _ in range(2)]
            wbig = AR[:, woff:woff + KC * 256].bitcast(BF16).rearrange("p (k n) -> p k n", k=KC)
            wsn = AR[:, woff:woff + 2048].rearrange("p (g n) -> p g n", g=16)
            WsT = cv.bf16(16 * 128).rearrange("p (g n) -> p g n", g=16)
            bsc = cv.f32(16)
            tmp = [cv.f32(512) for _ in range(2)]
            junk = cv.bf16(512)
            s1 = cv.f32(NT * 8).rearrange("p (t n) -> p t n", t=NT)
            s2 = cv.f32(NT * 8).rearrange("p (t n) -> p t n", t=NT)
            mu = cv.f32(NT)
            ex2 = cv.f32(NT)
            rstd = cv.f32(NT)
            print("gmlp arena", cv.off)
            w_in = I["gmlp_w_in"][0]
            P.op("sp", lambda e: e.dma_start(out=g_bc, in_=bass.AP(I["gmlp_ln_g"].tensor, 0, [[0, 128], [1, D]])), writes=["g_bc"], dma=True)
            P.op("sp", lambda e: e.dma_start(out=b_bc, in_=bass.AP(I["gmlp_ln_b"].tensor, 0, [[0, 128], [1, D]])), writes=["b_bc"], dma=True)
            P.op("sp", lambda e: e.dma_start(out=bsc, in_=I["gmlp_b_s"][0].rearrange("g i -> i g")), writes=["bsc"], dma=True)
            P.op("sp", lambda e: e.dma_start(out=wsn, in_=I["gmlp_w_s"][0].rearrange("g i j -> i g j")), writes=["wsn"], dma=True)
            P.op("dve", lambda e: e.tensor_tensor(out=wsn, in0=wsn, in1=bc(TriLf[:], 1, 16), op=ALU.mult), reads=["TriLf"], writes=["wsn"])
            for g4 in range(4):
                P.op("pe", lambda e: e.nop(), writes=["pf"])
                for q in range(4):
                    P.op("pe", lambda e, g4=g4, q=q: e.transpose(out=pf[:, q * 128:(q + 1) * 128], in_=wsn[:, g4 * 4 + q, :], identity=ident[:]), reads=["wsn", "ident"], writes=["pf"], pe_acc=True)
                P.op("act", lambda e, g4=g4: e.copy(out=WsT[:, g4 * 4:g4 * 4 + 4, :], in_=pf[:].rearrange("p (a b) -> p a b", a=4)), writes=["pf", "WsT"])
            P.barrier()
            cnt = 0
            for cgw in range(8):
                w = wb[cgw % 2]
                wk = ("wbk", cgw % 2)
                load_w(w, w_in, 0, KC, D + cgw * 256, 256, wk)
                for tt in range(NT):
                    pp, ppk = next_pm()
                    for kc in range(KC):
                        P.op("pe", lambda e, pp=pp, kc=kc, tt=tt, w=w: e.matmul(pp[:, 0:256], lhsT=xnT[:, kc, tt * 128:(tt + 1) * 128], rhs=w[:, kc, :], start=(kc == 0), stop=(kc == KC - 1)), reads=[wk, ("xnT", tt)], writes=[ppk], pe_acc=True)
                    t_ = tmp[cnt % 2]
                    tk = ("tmp", cnt % 2)
                    cnt += 1
                    P.op("act", lambda e, pp=pp, t_=t_, tt=tt, cgw=cgw: e.activation(out=t_[:, 0:256], in_=pp[:, 0:256], func=AF.Gelu, accum_out=s1[:, tt, cgw:cgw + 1]), writes=[ppk, tk, "s1"])
                    P.op("act", lambda e, t_=t_, tt=tt, cgw=cgw: e.activation(out=junk[:, 0:256], in_=t_[:, 0:256], func=AF.Square, accum_out=s2[:, tt, cgw:cgw + 1]), reads=[tk], writes=["junk", "s2"])
                    P.op("dve", lambda e, t_=t_, tt=tt, cgw=cgw: e.tensor_copy(out=VB[:, tt, cgw * 256:(cgw + 1) * 256], in_=t_[:, 0:256]), reads=[tk], writes=[("VB", tt)])
                    if tt == 8:
                        P.op("pool", lambda e, t_=t_, cgw=cgw: e.tensor_copy(out=vs32[:, cgw * 256:(cgw + 1) * 256], in_=t_[:, 0:256]), reads=[tk], writes=["vs32"])
            P.op("dve", lambda e: e.reduce_sum(out=mu, in_=s1, axis=AX.X), reads=["s1"], writes=["mu"])
            P.op("dve", lambda e: e.reduce_sum(out=ex2, in_=s2, axis=AX.X), reads=["s2"], writes=["ex2"])
            P.op("dve", lambda e: e.tensor_scalar(out=mu, in0=mu, scalar1=1.0 / D, scalar2=None, op0=ALU.mult), writes=["mu"])
            P.op("dve", lambda e: e.tensor_tensor(out=rstd, in0=mu, in1=mu, op=ALU.mult), reads=["mu"], writes=["rstd"])
            P.op("dve", lambda e: e.scalar_tensor_tensor(out=rstd, in0=ex2, scalar=1.0 / D, in1=rstd, op0=ALU.mult, op1=ALU.subtract), reads=["ex2"], writes=["rstd"])
            P.op("act", lambda e: e.activation(out=rstd, in_=rstd, func=AF.Sqrt, bias=small[:, 8:9]), reads=["epsc"], writes=["rstd"])
            P.op("dve", lambda e: e.reciprocal(out=rstd, in_=rstd), writes=["rstd"])
            for tt in range(NT):
                src = vs32 if tt == 8 else VB[:, tt, :]
                sk = "vs32" if tt == 8 else ("VB", tt)
                P.op("dve", lambda e, tt=tt, src=src: e.tensor_scalar(out=src, in0=src, scalar1=mu[:, tt:tt + 1], scalar2=rstd[:, tt:tt + 1], op0=ALU.subtract, op1=ALU.mult), reads=["mu", "rstd"], writes=[sk])
                P.op("dve", lambda e, src=src: e.tensor_tensor(out=src, in0=src, in1=g_bc, op=ALU.mult), reads=["g_bc"], writes=[sk])
                if tt == 8:
                    P.op("dve", lambda e: e.tensor_tensor(out=vs32, in0=vs32, in1=b_bc, op=ALU.add), reads=["b_bc"], writes=["vs32"])
                    finals.append(P.op("sp", lambda e: e.dma_start(out=O["gv_s"], in_=vs32[0:4, :]), reads=["vs32"], dma=True))
                    P.op("dve", lambda e: e.tensor_copy(out=VB[:, 8, :], in_=vs32), reads=["vs32"], writes=[("VB", 8)])
                else:
                    P.op("dve", lambda e, src=src: e.tensor_tensor(out=src, in0=src, in1=b_bc, op=ALU.add), reads=["b_bc"], writes=[sk])
            for tt in range(NT):
                for g4 in range(4):
                    pp, ppk = next_pm()
                    for q in range(4):
                        g = g4 * 4 + q
                        P.op("pe", lambda e, pp=pp, q=q, g=g, tt=tt: e.matmul(pp[:, q * 128:(q + 1) * 128], lhsT=WsT[:, g, :], rhs=VB[:, tt, g * 128:(g + 1) * 128], start=True, stop=True), reads=["WsT", ("VB", tt)], writes=[ppk], pe_acc=True)
                    P.op("dve", lambda e, pp=pp, g4=g4, tt=tt: e.tensor_tensor(out=VB[:, tt, g4 * 512:(g4 + 1) * 512].rearrange("p (a b) -> p a b", a=4), in0=pp[:].rearrange("p (a b) -> p a b", a=4), in1=bc(bsc[:, g4 * 4:g4 * 4 + 4], 2, 128), op=ALU.add), reads=["bsc"], writes=[ppk, ("VB", tt)])
            for cgw in range(8):
                w = wb[cgw % 2]
                wk = ("wbk", cgw % 2)
                load_w(w, w_in, 0, KC, cgw * 256, 256, wk)
                for tt in range(NT):
                    pp, ppk = next_pm()
                    for kc in range(KC):
                        P.op("pe", lambda e, pp=pp, kc=kc, tt=tt, w=w: e.matmul(pp[:, 0:256], lhsT=xnT[:, kc, tt * 128:(tt + 1) * 128], rhs=w[:, kc, :], start=(kc == 0), stop=(kc == KC - 1)), reads=[wk, ("xnT", tt)], writes=[ppk], pe_acc=True)
                    t_ = tmp[cnt % 2]
                    tk = ("tmp", cnt % 2)
                    cnt += 1
                    P.op("act", lambda e, pp=pp, t_=t_: e.activation(out=t_[:, 0:256], in_=pp[:, 0:256], func=AF.Gelu), writes=[ppk, tk])
                    P.op("dve", lambda e, t_=t_, tt=tt, cgw=cgw: e.tensor_tensor(out=VB[:, tt, cgw * 256:(cgw + 1) * 256], in0=t_[:, 0:256], in1=VB[:, tt, cgw * 256:(cgw + 1) * 256], op=ALU.mult), reads=[tk], writes=[("VB", tt)])
            P.barrier()
            for tt in range(NT):
                for q in range(2):
                    pt, ptk = next_pt()
                    for jj in range(8):
                        kc = q * 8 + jj
                        P.op("pe", lambda e, pt=pt, jj=jj, kc=kc, tt=tt: e.transpose(out=pt[:, jj * 128:(jj + 1) * 128], in_=VB[:, tt, kc * 128:(kc + 1) * 128], identity=identb[:]), reads=[("VB", tt), "identb"], writes=[ptk], pe_acc=True)
                    P.op("act", lambda e, pt=pt, q=q, tt=tt: e.copy(out=xnT[:, q * 8:q * 8 + 8, tt * 128:(tt + 1) * 128], in_=pt.rearrange("p (a b) -> p a b", a=8)), writes=[ptk, ("xnT", tt)])
            for cg in range(4):
                load_w(wbig, I["gmlp_w_out"][0], 0, KC, cg * 512, 512, "wbig")
                for tt in range(NT):
                    pp, ppk = next_pm()
                    for kc in range(KC):
                        P.op("pe", lambda e, pp=pp, kc=kc, tt=tt: e.matmul(pp[:], lhsT=xnT[:, kc, tt * 128:(tt + 1) * 128], rhs=wbig[:, kc, :], start=(kc == 0), stop=(kc == KC - 1)), reads=["wbig", ("xnT", tt)], writes=[ppk], pe_acc=True)
                    P.op("dve", lambda e, pp=pp, tt=tt, cg=cg: e.tensor_tensor(out=R[:, tt, cg * 512:(cg + 1) * 512], in0=pp[:], in1=R[:, tt, cg * 512:(cg + 1) * 512], op=ALU.add), writes=[ppk, ("R", tt)])

        def final_out(ph, NT):
            P.barrier()
            cv = Carver()
            xs_scr = cv.bf16(D)
            gb = cv.f32(D)
            yo = [cv.f32(D) for _ in range(2)]
            P.op("sp", lambda e: e.dma_start(out=gb, in_=bass.AP(I["norm_final"].tensor, 0, [[0, 128], [1, D]])), writes=["gb"], dma=True)
            for tt in range(NT):
                rms_stats(tt)(xs_scr)
                y = yo[tt % 2]
                yk = ("yo", tt % 2)
                P.op("dve", lambda e, tt=tt, y=y: e.scalar_tensor_tensor(out=y, in0=R[:, tt, :], scalar=small[:, 2:3], in1=gb, op0=ALU.mult, op1=ALU.mult), reads=[("R", tt), "rstd", "gb"], writes=[yk])
                if tt < 8:
                    r0 = ph * TP + tt * 128
                    finals.append(P.op("sp", lambda e, y=y, r0=r0: e.dma_start(out=O["y_p"][r0:r0 + 128, :], in_=y), reads=[yk], dma=True))
                else:
                    finals.append(P.op("sp", lambda e, y=y: e.dma_start(out=O["y_s"], in_=y[0:4, :]), reads=[yk], dma=True))

        for ph in range(2):
            NT = 9 if ph == 0 else 8
            P.barrier()
            for tt in range(8):
                r0 = ph * TP + tt * 128
                P.op("sp", lambda e, tt=tt, r0=r0: e.dma_start(out=R[:, tt, :], in_=I["xp"][r0:r0 + 128, :]), writes=[("R", tt)], dma=True)
            if ph == 0:
                P.op("dve", lambda e: e.memset(R[:, 8, :], 0.0), writes=[("R", 8)])
                P.op("sp", lambda e: e.dma_start(out=R[0:4, 8, :], in_=I["xs"]), writes=[("R", 8)], dma=True)
            for li in cfg.get("layers", list(range(n_layers))):
                if do_mixer and li % 3 == 0:
                    ssd(li, ph, NT)
                if do_mixer and li % 3 == 1:
                    moba(li, ph, NT)
                if do_mixer and li % 3 == 2:
                    gmlp(li, ph, NT)
                if do_ffn:
                    ffn(li, NT)
            final_out(ph, NT)
        P.finish(finals)
    return nc


_NC_CACHE = {}


def _get_nc(cfg_key=()):
    if cfg_key not in _NC_CACHE:
        _NC_CACHE[cfg_key] = build(dict(cfg_key))
    return _NC_CACHE[cfg_key]


def kernel(**inp):
    cfg = inp.pop("_cfg", {})
    nc = _get_nc(tuple(sorted((k, tuple(v) if isinstance(v, list) else v) for k, v in cfg.items())))

    def f(a):
        return np.ascontiguousarray(np.asarray(a, dtype=np.float32))

    half = 16
    inv_freq = (np.float32(500000.0) ** (-np.arange(half, dtype=np.float32) * np.float32(2.0) / np.float32(32.0))).astype(np.float32)

    def rope_tab(pos):
        ang = pos.astype(np.float32)[:, None] * inv_freq[None, :]
        return np.concatenate([np.cos(ang), np.sin(ang)], axis=1).astype(np.float32)

    pos_p = np.arange(SEQ)
    cur = pos_p // 256
    kb = np.arange(8)
    shared = {
        "norm_mix": f(inp["norm_mix"]), "norm_ffn": f(inp["norm_ffn"]), "norm_final": f(inp["norm_final"]).reshape(1, D),
        "ssd_w_in": f(inp["ssd_w_in"]), "ssd_conv_w": f(inp["ssd_conv_w"]), "ssd_conv_b": f(inp["ssd_conv_b"]),
        "ssd_dt_bias": f(inp["ssd_dt_bias"]), "ssd_a_log": f(inp["ssd_a_log"]), "ssd_d": f(inp["ssd_d"]),
        "ssd_gate_norm": f(inp["ssd_gate_norm"]), "ssd_w_out": f(inp["ssd_w_out"]),
        "moba_w_qkv": f(inp["moba_w_qkv"]), "moba_w_o": f(inp["moba_w_o"]),
        "gmlp_w_in": f(inp["gmlp_w_in"]), "gmlp_ln_g": f(inp["gmlp_ln_g"]), "gmlp_ln_b": f(inp["gmlp_ln_b"]),
        "gmlp_w_s": f(inp["gmlp_w_s"]), "gmlp_b_s": f(inp["gmlp_b_s"]), "gmlp_w_out": f(inp["gmlp_w_out"]),
        "ffn_w_gate": f(inp["ffn_w_gate"]), "ffn_w_up": f(inp["ffn_w_up"]), "ffn_w_down": f(inp["ffn_w_down"]),
        "cache_k": None if cfg.get("stub_cache") else f(inp["cache_k"]).reshape(-1, D),
        "cache_v": None if cfg.get("stub_cache") else f(inp["cache_v"]).reshape(-1, D),
        "rope_p": rope_tab(pos_p), "rope_s": rope_tab(16384 + np.arange(4)),
        "valid_p": (kb[None, :] < cur[:, None]).astype(np.float32), "own_p": (kb[None, :] == cur[:, None]).astype(np.float32),
        "own_s": np.tile((np.arange(4)[None, :] <= np.arange(4)[:, None]).astype(np.float32), (16, 1)),
    }
    if cfg.get("stub_cache"):
        shared["cache_k"] = np.zeros((1, 128, 128), np.float32)
        shared["cache_v"] = np.zeros((1, 128, 128), np.float32)
    if cfg.get("stub_w"):
        for k in list(shared):
            if shared[k] is not None and k.split("_")[0] in ("ssd", "ffn", "moba", "gmlp", "cache") and shared[k].size > (1 << 22):
                shared[k] = np.zeros((1, 128, 128), np.float32)
    xp = f(inp["x_prompt"])
    xs = f(inp["x_sample"])
    sssm = f(inp["state_ssm"])
    sconv = f(inp["state_conv"])
    ptab = np.ascontiguousarray(np.asarray(inp["page_table"], dtype=np.int32))
    in_maps = []
    for c in range(NCORES):
        m = dict(shared)
        m["xp"] = xp[c % 4]
        m["xs"] = xs[c]
        m["state_ssm"] = np.ascontiguousarray(sssm[:, c].reshape(2, 4096, 128))
        m["state_conv"] = np.ascontiguousarray(sconv[:, c])
        m["page_table"] = ptab[c:c + 1]
        in_maps.append(m)
    res = run_bass_kernel_spmd(nc, in_maps, core_ids=list(range(NCORES))).results
    y_p = np.stack([res[c]["y_p"] for c in range(4)])
    y_s = np.stack([res[c]["y_s"] for c in range(8)])
    ssm_p = np.stack([res[c]["ssm_p"] for c in range(4)], axis=1).reshape(2, 4, 64, 64, 128)
    ssm_s = np.stack([res[c]["ssm_s"] for c in range(8)], axis=1).reshape(2, 8, 64, 64, 128)
    conv_p = np.stack([res[c]["conv_p"] for c in range(4)], axis=1)
    conv_s = np.stack([res[c]["conv_s"] for c in range(8)], axis=1)
    k_p = np.stack([res[c]["k_p"] for c in range(4)]).reshape(1, 4, SEQ, 16, 128)
    v_p = np.stack([res[c]["v_p"] for c in range(4)]).reshape(1, 4, SEQ, 16, 128)
    k_s = np.stack([res[c]["k_s"] for c in range(8)]).reshape(1, 8, 4, 16, 128)
    v_s = np.stack([res[c]["v_s"] for c in range(8)]).reshape(1, 8, 4, 16, 128)
    gv_s = np.stack([res[c]["gv_s"] for c in range(8)]).reshape(1, 8, 4, D)
    return (y_p, y_s, ssm_p, ssm_s, conv_p, conv_s, k_p, v_p, k_s, v_s, gv_s)
```

```python
import contextlib
import math
import numpy as np
import concourse.bass as bass
import concourse.mybir as mybir
from concourse.bass_utils import run_bass_kernel_spmd

F32 = mybir.dt.float32
BF16 = mybir.dt.bfloat16
I32 = mybir.dt.int32
AF = mybir.ActivationFunctionType
ALU = mybir.AluOpType
AX = mybir.AxisListType

ENGS = ("pe", "dve", "act", "pool", "sp")
SEM_CAP = 20000
DMA_CAP = 1000
DMA_POOL = 8

D = 2048
KC = 16
SEQ = 2048
TP = 1024
DEPTH = 4
SSD_INNER = 4096
SSD_CONV_DIM = 6144
SSD_IN = 10304
FFN_H = 5632
EPS = 1e-6
NCORES = 8
NPOOL = 1280


class Op:
    __slots__ = ("eng", "fn", "deps", "needed", "is_dma", "event")

    def __init__(self, eng, fn, is_dma):
        self.eng = eng
        self.fn = fn
        self.deps = []
        self.needed = False
        self.is_dma = is_dma
        self.event = None


class Prog:
    def __init__(self, nc):
        self.nc = nc
        self.ops = {e: [] for e in ENGS}
        self.last_w = {}
        self.readers = {}
        self.stack = contextlib.ExitStack()
        self.nsem = 0
        self.dmas_since = []

    def new_sem(self):
        self.nsem += 1
        return self.stack.enter_context(self.nc.semaphore("s%d" % self.nsem))

    def op(self, eng, fn, reads=(), writes=(), dma=False, pe_acc=False, extra=()):
        o = Op(eng, fn, dma)
        deps = list(extra)
        for k in reads:
            w = self.last_w.get(k)
            if w is not None:
                deps.append(w)
        for k in writes:
            w = self.last_w.get(k)
            if w is not None and not (pe_acc and w.eng == "pe" and eng == "pe" and not w.is_dma):
                deps.append(w)
            deps.extend(self.readers.get(k, ()))
        seen = set()
        for d in deps:
            if d is o or id(d) in seen:
                continue
            seen.add(id(d))
            d.needed = True
            o.deps.append(d)
        for k in reads:
            self.readers.setdefault(k, []).append(o)
        for k in writes:
            self.last_w[k] = o
            self.readers[k] = []
        self.ops[eng].append(o)
        if dma:
            self.dmas_since.append(o)
        return o

    def barrier(self):
        prev = [self.ops[e][-1] for e in ENGS if self.ops[e]] + self.dmas_since
        self.dmas_since = []
        for e in ENGS:
            self.op(e, lambda eh: eh.nop(), extra=prev)
        self.last_w = {}
        self.readers = {}

    def finish(self, final_ops):
        nc = self.nc
        for o in final_ops:
            o.needed = True
        for e in ENGS:
            sem = None
            cnt = 0
            for o in self.ops[e]:
                if o.is_dma or not o.needed:
                    continue
                if sem is None or cnt >= SEM_CAP:
                    sem = self.new_sem()
                    cnt = 0
                cnt += 1
                o.event = (sem, cnt)
        dma_prev = {}
        for e in ENGS:
            pool = []
            i = 0
            for o in self.ops[e]:
                if not o.is_dma:
                    continue
                if len(pool) < DMA_POOL:
                    pool.append([self.new_sem(), 0])
                slot = pool[i % len(pool)]
                if slot[1] >= DMA_CAP:
                    slot[0] = self.new_sem()
                    slot[1] = 0
                dma_prev[id(o)] = (slot[0], slot[1] * 16) if slot[1] > 0 else None
                slot[1] += 1
                o.event = (slot[0], slot[1] * 16)
                i += 1
        handles = {"pe": "tensor", "dve": "vector", "act": "scalar", "pool": "gpsimd", "sp": "sync"}
        with nc.Block() as block:
            for e in ENGS:

                def body(eh, e=e):
                    known = {}

                    def wait(ev):
                        if ev is None:
                            return
                        s, v = ev
                        if known.get(id(s), 0) >= v:
                            return
                        known[id(s)] = v
                        eh.wait_ge(s, v)

                    for o in self.ops[e]:
                        for d in o.deps:
                            wait(d.event)
                        if o.is_dma:
                            wait(dma_prev[id(o)])
                        ins = o.fn(eh)
                        if o.is_dma:
                            ins.then_inc(o.event[0], 16)
                        elif o.needed:
                            ins.then_inc(o.event[0], 1)
                    if e == "sp":
                        for o in final_ops:
                            wait(o.event)

                getattr(block, handles[e])(body)


def bc(ap, axis, n):
    l = [list(x) for x in ap.ap]
    l.insert(axis, [0, n])
    return bass.AP(ap.tensor, ap.offset, l)


def build(cfg):
    nc = bass.Bass("TRN2", target_bir_lowering=False)

    def din(name, shape, dt=F32):
        if cfg.get("stub_cache") and name.startswith("cache"):
            shape = [1, 128, 128]
        if cfg.get("stub_w") and name.split("_")[0] in ("ssd", "ffn", "moba", "gmlp", "cache") and int(np.prod(shape)) > (1 << 22):
            shape = [1, 128, 128]
        return nc.dram_tensor(name, list(shape), dt, kind="ExternalInput").ap()

    def dout(name, shape, dt=F32):
        return nc.dram_tensor(name, list(shape), dt, kind="ExternalOutput").ap()

    def dscr(name, shape, dt=F32):
        return nc.dram_tensor(name, list(shape), dt).ap()

    n_layers = cfg.get("n_layers", DEPTH)
    do_mixer = cfg.get("mixer", True)
    do_ffn = cfg.get("ffn", True)

    I = {}
    I["xp"] = din("xp", [SEQ, D])
    I["xs"] = din("xs", [4, D])
    I["state_ssm"] = din("state_ssm", [2, 4096, 128])
    I["state_conv"] = din("state_conv", [2, 3, SSD_CONV_DIM])
    I["norm_mix"] = din("norm_mix", [DEPTH, D])
    I["norm_ffn"] = din("norm_ffn", [DEPTH, D])
    I["norm_final"] = din("norm_final", [1, D])
    I["ssd_w_in"] = din("ssd_w_in", [2, D, SSD_IN])
    I["ssd_conv_w"] = din("ssd_conv_w", [2, 4, SSD_CONV_DIM])
    I["ssd_conv_b"] = din("ssd_conv_b", [2, SSD_CONV_DIM])
    I["ssd_dt_bias"] = din("ssd_dt_bias", [2, 64])
    I["ssd_a_log"] = din("ssd_a_log", [2, 64])
    I["ssd_d"] = din("ssd_d", [2, 64])
    I["ssd_gate_norm"] = din("ssd_gate_norm", [2, SSD_INNER])
    I["ssd_w_out"] = din("ssd_w_out", [2, SSD_INNER, D])
    I["moba_w_qkv"] = din("moba_w_qkv", [1, D, 3 * D])
    I["moba_w_o"] = din("moba_w_o", [1, D, D])
    I["rope_p"] = din("rope_p", [SEQ, 32])
    I["rope_s"] = din("rope_s", [4, 32])
    I["valid_p"] = din("valid_p", [SEQ, 8])
    I["own_p"] = din("own_p", [SEQ, 8])
    npool = cfg.get("npool", NPOOL)
    I["cache_k"] = din("cache_k", [npool * 128, D])
    I["cache_v"] = din("cache_v", [npool * 128, D])
    I["page_table"] = din("page_table", [1, 128], I32)
    I["own_s"] = din("own_s", [64, 4])
    I["gmlp_w_in"] = din("gmlp_w_in", [1, D, 2 * D])
    I["gmlp_ln_g"] = din("gmlp_ln_g", [1, D])
    I["gmlp_ln_b"] = din("gmlp_ln_b", [1, D])
    I["gmlp_w_s"] = din("gmlp_w_s", [1, 16, 128, 128])
    I["gmlp_b_s"] = din("gmlp_b_s", [1, 16, 128])
    I["gmlp_w_out"] = din("gmlp_w_out", [1, D, D])
    I["ffn_w_gate"] = din("ffn_w_gate", [DEPTH, D, FFN_H])
    I["ffn_w_up"] = din("ffn_w_up", [DEPTH, D, FFN_H])
    I["ffn_w_down"] = din("ffn_w_down", [DEPTH, FFN_H, D])
    O = {}
    O["y_p"] = dout("y_p", [SEQ, D])
    O["y_s"] = dout("y_s", [4, D])
    O["ssm_p"] = dout("ssm_p", [2, 4096, 128])
    O["ssm_s"] = dout("ssm_s", [2, 4096, 128])
    O["conv_p"] = dout("conv_p", [2, 3, SSD_CONV_DIM])
    O["conv_s"] = dout("conv_s", [2, 3, SSD_CONV_DIM])
    O["k_p"] = dout("k_p", [SEQ, D])
    O["v_p"] = dout("v_p", [SEQ, D])
    O["k_s"] = dout("k_s", [4, D])
    O["v_s"] = dout("v_s", [4, D])
    O["gv_s"] = dout("gv_s", [4, D])
    S = {}
    S["kT"] = dscr("kT_scr", [16, 128, 1024], BF16)
    S["v"] = dscr("v_scr", [16, 128, 8, 128], BF16)
    S["ksum"] = dscr("ksum_scr", [16, 128, 4])
    S["hsave"] = dscr("hsave", [2, 128, 4096])

    P = Prog(nc)
    finals = []
    with contextlib.ExitStack() as es:
        es.enter_context(P.stack)
        es.enter_context(nc.allow_non_contiguous_dma(reason="small strided constant/boundary loads"))

        def sb(name, shape, dt):
            return es.enter_context(nc.sbuf_tensor(name, list(shape), dt))

        def psum(name, shape, dt):
            return es.enter_context(nc.psum_tensor(name, list(shape), dt))

        R = sb("R", [128, 9, D], F32)
        xnT = sb("xnT", [128, KC, 1152], BF16)
        ident = sb("ident", [128, 128], F32)
        identb = sb("identb", [128, 128], BF16)
        gcol = sb("gcol", [128, 9, KC], F32)
        small = sb("small", [128, 64], F32)
        ARN = 23000
        AR = sb("AR", [128, ARN], F32)
        pm = [psum("pm%d" % i, [128, 512], F32) for i in range(4)]
        pyb = psum("pyb", [128, 512], F32)
        ptb = [psum("ptb%d" % i, [128, 1024], BF16) for i in range(2)]
        pf = psum("pf", [128, 512], F32)

        rot = {"pm": 0, "pt": 0}

        def next_pm():
            i = rot["pm"] % 4
            rot["pm"] += 1
            return pm[i], "pm%d" % i

        def next_pt():
            i = rot["pt"] % 2
            rot["pt"] += 1
            return ptb[i], "pt%d" % i

        class Carver:
            def __init__(self):
                self.off = 0

            def f32(self, n):
                a = AR[:, self.off:self.off + n]
                self.off += n
                assert self.off <= ARN, self.off
                return a

            def bf16(self, n):
                m = (n + 1) // 2
                a = AR[:, self.off:self.off + m].bitcast(BF16)
                self.off += m
                assert self.off <= ARN, self.off
                return a

        P.op("pool", lambda e: e.memset(ident[:], 0.0), writes=["ident"])
        P.op("pool", lambda e: e.affine_select(out=ident[:], in_=ident[:], pattern=[[-1, 128]], compare_op=ALU.not_equal, fill=1.0, base=0, channel_multiplier=1), reads=["ident"], writes=["ident"])
        P.op("dve", lambda e: e.tensor_copy(out=identb[:], in_=ident[:]), reads=["ident"], writes=["identb"])
        P.op("sp", lambda e: e.dma_start(out=gcol[:, 0:4, :], in_=I["norm_mix"].rearrange("v (kc p) -> p v kc", p=128)), writes=["gcol0"], dma=True)
        P.op("sp", lambda e: e.dma_start(out=gcol[:, 4:8, :], in_=I["norm_ffn"].rearrange("v (kc p) -> p v kc", p=128)), writes=["gcol1"], dma=True)
        P.op("sp", lambda e: e.dma_start(out=gcol[:, 8:9, :], in_=I["norm_final"].rearrange("v (kc p) -> p v kc", p=128)), writes=["gcol2"], dma=True)

        def rms_stats(tt):
            def f(xs_scr):
                P.op("act", lambda e: e.activation(out=xs_scr, in_=R[:, tt, :], func=AF.Square, accum_out=small[:, 0:1]), reads=[("R", tt)], writes=["xs_scr", "ss"])
                P.op("act", lambda e: e.activation(out=small[:, 1:2], in_=small[:, 0:1], func=AF.Sqrt, scale=1.0 / D, bias=small[:, 8:9]), reads=["ss", "epsc"], writes=["std"])
                P.op("dve", lambda e: e.reciprocal(out=small[:, 2:3], in_=small[:, 1:2]), reads=["std"], writes=["rstd"])
            return f

        P.op("dve", lambda e: e.memset(small[:, 8:9], EPS), writes=["epsc"])
        Tri = sb("Tri", [128, 128], F32)
        SL = sb("SL", [128, 128], F32)
        NEG = sb("NEG", [128, 128], F32)
        ones = sb("ones", [128, 128], F32)
        vmask = sb("vmask", [128, 1], F32)
        ctail = sb("ctail", [128, 2, 48, 3], F32)
        P.op("pool", lambda e: e.memset(ones[:], 1.0), writes=["ones"])
        P.op("pool", lambda e: e.memset(Tri[:], 1.0), writes=["Tri"])
        P.op("pool", lambda e: e.affine_select(out=Tri[:], in_=Tri[:], pattern=[[1, 128]], compare_op=ALU.is_ge, fill=0.0, base=0, channel_multiplier=-1), writes=["Tri"])
        P.op("pool", lambda e: e.memset(SL[:], 1.0), writes=["SL"])
        P.op("pool", lambda e: e.affine_select(out=SL[:], in_=SL[:], pattern=[[-1, 128]], compare_op=ALU.is_gt, fill=0.0, base=0, channel_multiplier=1), writes=["SL"])
        P.op("pool", lambda e: e.memset(NEG[:], 0.0), writes=["NEG"])
        P.op("pool", lambda e: e.affine_select(out=NEG[:], in_=NEG[:], pattern=[[1, 128]], compare_op=ALU.is_ge, fill=-30000.0, base=0, channel_multiplier=-1), writes=["NEG"])
        TriLf = sb("TriLf", [128, 128], F32)
        P.op("pool", lambda e: e.memset(TriLf[:], 1.0), writes=["TriLf"])
        P.op("pool", lambda e: e.affine_select(out=TriLf[:], in_=TriLf[:], pattern=[[-1, 128]], compare_op=ALU.is_ge, fill=0.0, base=0, channel_multiplier=1), writes=["TriLf"])
        TriL = sb("TriL", [128, 128], BF16)
        P.op("pool", lambda e: e.memset(TriL[:], 1.0), writes=["TriL"])
        P.op("pool", lambda e: e.affine_select(out=TriL[:], in_=TriL[:], pattern=[[-1, 128]], compare_op=ALU.is_ge, fill=0.0, base=0, channel_multiplier=1), writes=["TriL"])
        P.op("pool", lambda e: e.memset(vmask[:], 1.0), writes=["vmask"])
        P.op("pool", lambda e: e.affine_select(out=vmask[:], in_=vmask[:], pattern=[[0, 1]], compare_op=ALU.is_ge, fill=0.0, base=3, channel_multiplier=-1), writes=["vmask"])

        def make_xnT(gidx, NT, xs_scr):
            for tt in range(NT):
                rms_stats(tt)(xs_scr)
                P.op("act", lambda e, tt=tt: e.activation(out=xs_scr, in_=R[:, tt, :], func=AF.Identity, scale=small[:, 2:3]), reads=[("R", tt), "rstd"], writes=["xs_scr"])
                for q in range(2):
                    pt, ptk = next_pt()
                    for j in range(8):
                        kc = q * 8 + j
                        P.op("pe", lambda e, pt=pt, j=j, kc=kc: e.transpose(out=pt[:, j * 128:(j + 1) * 128], in_=xs_scr[:, kc * 128:(kc + 1) * 128], identity=identb[:]), reads=["xs_scr", "identb"], writes=[ptk], pe_acc=True)
                    P.op("dve", lambda e, pt=pt, q=q, tt=tt: e.tensor_tensor(out=xnT[:, q * 8:q * 8 + 8, tt * 128:(tt + 1) * 128], in0=pt.rearrange("p (a b) -> p a b", a=8), in1=bc(gcol[:, gidx, q * 8:q * 8 + 8], 2, 128), op=ALU.mult), reads=["gcol0", "gcol1", "gcol2"], writes=[ptk, ("xnT", tt)])

        def load_w(dst, w2d, row0, nkc, col0, ncols, key):
            src = w2d[row0:row0 + nkc * 128, col0:col0 + ncols].rearrange("(kc p) n -> p kc n", p=128)
            return P.op("pool", lambda e: e.dma_start(out=dst, in_=src), writes=[key], dma=True)

        def mts(NT):
            T = NT * 128
            return [(c, min(512, T - c)) for c in range(0, T, 512)]

        def ffn(li, NT):
            P.barrier()
            cv = Carver()
            xs_scr = cv.bf16(D)
            wg = [cv.bf16(KC * 256).rearrange("p (k n) -> p k n", k=KC) for _ in range(2)]
            wu = [cv.bf16(KC * 256).rearrange("p (k n) -> p k n", k=KC) for _ in range(2)]
            wd = [cv.bf16(2 * D).rearrange("p (k n) -> p k n", k=2) for _ in range(2)]
            hid = [cv.bf16(2 * 1152).rearrange("p (k n) -> p k n", k=2) for _ in range(2)]
            sg = [cv.f32(512) for _ in range(2)]
            make_xnT(4 + li, NT, xs_scr)
            T = NT * 128
            wgd, wud, wdd = I["ffn_w_gate"][li], I["ffn_w_up"][li], I["ffn_w_down"][li]
            cnt = 0
            if cfg.get('ffn_stage', 3) < 2:
                return
            for grp in range(cfg.get('ffn_groups', FFN_H // 256)):
                b = grp % 2
                load_w(wg[b], wgd, 0, KC, grp * 256, 256, ("wg", b))
                load_w(wu[b], wud, 0, KC, grp * 256, 256, ("wu", b))
                load_w(wd[b], wdd, grp * 256, 2, 0, D, ("wd", b))
                for hc in range(2):
                    for (c0, n) in mts(NT):
                        pg, pgk = next_pm()
                        pu, puk = next_pm()
                        rk = [("xnT", t) for t in range(c0 // 128, (c0 + n) // 128)]
                        for kc in range(KC):
                            P.op("pe", lambda e, pg=pg, b=b, kc=kc, hc=hc, c0=c0, n=n: e.matmul(pg[:, 0:n], lhsT=wg[b][:, kc, hc * 128:(hc + 1) * 128], rhs=xnT[:, kc, c0:c0 + n], start=(kc == 0), stop=(kc == KC - 1)), reads=[("wg", b)] + rk, writes=[pgk], pe_acc=True)
                        for kc in range(KC):
                            P.op("pe", lambda e, pu=pu, b=b, kc=kc, hc=hc, c0=c0, n=n: e.matmul(pu[:, 0:n], lhsT=wu[b][:, kc, hc * 128:(hc + 1) * 128], rhs=xnT[:, kc, c0:c0 + n], start=(kc == 0), stop=(kc == KC - 1)), reads=[("wu", b)] + rk, writes=[puk], pe_acc=True)
                        s = sg[cnt % 2]
                        sk = ("sg", cnt % 2)
                        cnt += 1
                        P.op("act", lambda e, pg=pg, s=s, n=n: e.activation(out=s[:, 0:n], in_=pg[:, 0:n], func=AF.Silu), writes=[pgk, sk])
                        P.op("dve", lambda e, pu=pu, s=s, n=n, b=b, hc=hc, c0=c0: e.tensor_tensor(out=hid[b][:, hc, c0:c0 + n], in0=pu[:, 0:n], in1=s[:, 0:n], op=ALU.mult), reads=[sk], writes=[puk, ("hid", b, hc, c0)])
                for tt in range(NT if cfg.get('ffn_stage', 3) >= 3 else 0):
                    for cg in range(4):
                        po, pok = next_pm()
                        for hc in range(2):
                            P.op("pe", lambda e, po=po, b=b, hc=hc, tt=tt, cg=cg: e.matmul(po[:], lhsT=hid[b][:, hc, tt * 128:(tt + 1) * 128], rhs=wd[b][:, hc, cg * 512:(cg + 1) * 512], start=(hc == 0), stop=(hc == 1)), reads=[("wd", b), ("hid", b, hc, (tt // 4) * 512)], writes=[pok], pe_acc=True)
                        P.op("dve", lambda e, po=po, tt=tt, cg=cg: e.tensor_tensor(out=R[:, tt, cg * 512:(cg + 1) * 512], in0=po[:], in1=R[:, tt, cg * 512:(cg + 1) * 512], op=ALU.add), writes=[pok, ("R", tt)])

        def ssd(li, ph, NT):
            j = li // 3
            P.barrier()
            cv = Carver()
            xs_scr = cv.bf16(D)
            make_xnT(li, NT, xs_scr)
            P.barrier()
            cv = Carver()
            T = NT * 128
            XL = 3 + 1024 + (3 + 128 if NT == 9 else 0)
            SOFF = 1027
            wb = [cv.bf16(KC * 264).rearrange("p (k n) -> p k n", k=KC) for _ in range(2)]
            zs = cv.bf16(NT * 512).rearrange("p (t n) -> p t n", t=NT)
            xtm = cv.f32(NT * 512).rearrange("p (t n) -> p t n", t=NT)
            btm = cv.bf16(NT * 128).rearrange("p (t n) -> p t n", t=NT)
            bT = cv.bf16(T)
            cT = cv.bf16(T)
            dtt = cv.f32(NT * 8).rearrange("p (t n) -> p t n", t=NT)
            att = cv.f32(NT * 8).rearrange("p (t n) -> p t n", t=NT)
            xpre = cv.f32(XL)
            xcv = cv.f32(T)
            xfT = xcv
            gnT = cv.bf16(4 * T).rearrange("p (k n) -> p k n", k=4)
            wo1 = cv.bf16(4 * 512).rearrange("p (k n) -> p k n", k=4)
            wo = [wo1, wo1]
            hT = cv.f32(512)
            hTb = cv.bf16(512)
            hS = hT
            cw = cv.f32(48 * 4).rearrange("p (c k) -> p c k", c=48)
            cbias = cv.f32(48)
            dtb_bc = cv.f32(64)
            A_bc = cv.f32(64)
            D_bc = cv.f32(64)
            gnw_bc = cv.f32(512)
            sc = cv.f32(64)
            lTs = [cv.f32(128) for _ in range(2)]
            exs = [cv.f32(128) for _ in range(2)]
            cbs = cv.f32(128)
            mTs = [cv.bf16(128) for _ in range(2)]
            xdt = cv.bf16(512)
            xw = cv.bf16(512)
            yy = cv.f32(512)
            gg = cv.f32(512)
            gnb = cv.bf16(512)
            st_in = xpre[:, 0:512]
            print('ssd arena', cv.off)
            w_in = I["ssd_w_in"][j]
            for k in range(4):
                P.op("sp", lambda e, k=k: e.dma_start(out=cw[:, :, k], in_=I["ssd_conv_w"][j][k:k + 1, :].rearrange("o (c p) -> p (o c)", p=128)), writes=["cw%d" % k], dma=True)
            P.op("sp", lambda e: e.dma_start(out=cbias, in_=I["ssd_conv_b"][j:j + 1, :].rearrange("o (c p) -> p (o c)", p=128)), writes=["cbias"], dma=True)
            for (dst, src, k) in ((dtb_bc, "ssd_dt_bias", "dtb"), (A_bc, "ssd_a_log", "Abc"), (D_bc, "ssd_d", "Dbc")):
                P.op("sp", lambda e, dst=dst, src=src: e.dma_start(out=dst, in_=bass.AP(I[src].tensor, j * 64, [[0, 128], [1, 64]])), writes=[k], dma=True)
            P.op("act", lambda e: e.activation(out=A_bc, in_=A_bc, func=AF.Exp), writes=["Abc"])
            P.op("dve", lambda e: e.tensor_scalar(out=A_bc, in0=A_bc, scalar1=-1.0, scalar2=None, op0=ALU.mult), writes=["Abc"])

            def fm_proj(col0, dst_fn, wt, wcol):
                for (c0, n) in mts(NT):
                    pp, ppk = next_pm()
                    rk = [("xnT", t) for t in range(c0 // 128, (c0 + n) // 128)]
                    for kc in range(KC):
                        P.op("pe", lambda e, pp=pp, kc=kc, c0=c0, n=n: e.matmul(pp[:, 0:n], lhsT=wt[:, kc, wcol:wcol + 128], rhs=xnT[:, kc, c0:c0 + n], start=(kc == 0), stop=(kc == KC - 1)), reads=["wb"] + rk, writes=[ppk], pe_acc=True)
                    dst_fn(pp, ppk, c0, n)

            def conv_chunk(ch, out_ap, out_key, zero_tail):
                segs = [(0, 1024, 0)] + ([(SOFF, 128, 1024)] if NT == 9 else [])
                for (off, L, o0) in segs:
                    P.op("dve", lambda e, off=off, L=L, o0=o0: e.tensor_scalar(out=xcv[:, o0:o0 + L], in0=xpre[:, off:off + L], scalar1=cw[:, ch, 0:1], scalar2=None, op0=ALU.mult), reads=["xpre", "cw0", "cw1", "cw2", "cw3"], writes=["xcv"])
                    for k in range(1, 4):
                        P.op("dve", lambda e, off=off, L=L, o0=o0, k=k: e.scalar_tensor_tensor(out=xcv[:, o0:o0 + L], in0=xpre[:, off + k:off + k + L], scalar=cw[:, ch, k:k + 1], in1=xcv[:, o0:o0 + L], op0=ALU.mult, op1=ALU.add), reads=["xpre", "cw0", "cw1", "cw2", "cw3"], writes=["xcv"])
                P.op("act", lambda e: e.activation(out=out_ap[:, 0:T], in_=xcv[:, 0:T], func=AF.Silu, bias=cbias[:, ch:ch + 1]), reads=["xcv", "cbias"], writes=[out_key])
                if zero_tail and NT == 9:
                    P.op("pool", lambda e: e.memset(out_ap[:, 1024 + 4:T], 0.0), writes=[out_key])

            for g in range(8):
                P.op("sp", lambda e, g=g: e.dma_start(out=gnw_bc, in_=bass.AP(I["ssd_gate_norm"].tensor, j * SSD_INNER + g * 512, [[0, 128], [1, 512]])), writes=["gnw"], dma=True)
                for half in range(2):
                    w = wb[half]
                    load_w(w[:, :, 0:256], w_in, 0, KC, g * 512 + half * 256, 256, "wb")
                    for tt in range(NT):
                        pp, ppk = next_pm()
                        for kc in range(KC):
                            P.op("pe", lambda e, pp=pp, kc=kc, tt=tt, w=w: e.matmul(pp[:, 0:256], lhsT=xnT[:, kc, tt * 128:(tt + 1) * 128], rhs=w[:, kc, 0:256], start=(kc == 0), stop=(kc == KC - 1)), reads=["wb", ("xnT", tt)], writes=[ppk], pe_acc=True)
                        P.op("act", lambda e, pp=pp, tt=tt, half=half: e.activation(out=zs[:, tt, half * 256:(half + 1) * 256], in_=pp[:, 0:256], func=AF.Silu), writes=[ppk, ("zs", tt)])
                w = wb[0]
                load_w(w[:, :, 0:128], w_in, 0, KC, 8192 + g * 128, 128, "wb")
                load_w(w[:, :, 128:256], w_in, 0, KC, 9216 + g * 128, 128, "wb")
                load_w(w[:, :, 256:264], w_in, 0, KC, 10240 + g * 8, 8, "wb")
                for tt in range(NT):
                    pp, ppk = next_pm()
                    for kc in range(KC):
                        P.op("pe", lambda e, pp=pp, kc=kc, tt=tt, w=w: e.matmul(pp[:, 0:8], lhsT=xnT[:, kc, tt * 128:(tt + 1) * 128], rhs=w[:, kc, 256:264], start=(kc == 0), stop=(kc == KC - 1)), reads=["wb", ("xnT", tt)], writes=[ppk], pe_acc=True)
                    P.op("dve", lambda e, pp=pp, tt=tt, g=g: e.tensor_tensor(out=dtt[:, tt, :], in0=pp[:, 0:8], in1=dtb_bc[:, g * 8:(g + 1) * 8], op=ALU.add), reads=["dtb"], writes=[ppk, "dtt"])
                P.op("act", lambda e: e.activation(out=dtt, in_=dtt, func=AF.Exp), writes=["dtt"])
                P.op("act", lambda e: e.activation(out=dtt, in_=dtt, func=AF.Ln, bias=1.0), writes=["dtt"])
                if NT == 9:
                    P.op("dve", lambda e: e.tensor_scalar(out=dtt[:, 8, :], in0=dtt[:, 8, :], scalar1=vmask[:, 0:1], scalar2=None, op0=ALU.mult), reads=["vmask"], writes=["dtt"])
                P.op("dve", lambda e, g=g: e.tensor_tensor(out=att, in0=dtt, in1=bc(A_bc[:, g * 8:(g + 1) * 8], 1, NT), op=ALU.mult), reads=["dtt", "Abc"], writes=["att"])

                def run_chunk(ch, wt, wcol, kind, ci):
                    if ph == 0:
                        P.op("pool", lambda e: e.memset(xpre[:, 0:3], 0.0), writes=["xpre"])
                    else:
                        P.op("pool", lambda e: e.tensor_copy(out=xpre[:, 0:3], in_=ctail[:, j, ch, :]), reads=["ctail"], writes=["xpre"])
                    if NT == 9:
                        P.op("sp", lambda e: e.dma_start(out=xpre[:, SOFF:SOFF + 3], in_=I["state_conv"][j][:, ch * 128:(ch + 1) * 128].rearrange("r p -> p r")), writes=["xpre"], dma=True)

                    def dst_fn(pp, ppk, c0, n):
                        o = 3 + c0 if c0 < 1024 else SOFF + 3
                        P.op("act", lambda e: e.copy(out=xpre[:, o:o + n], in_=pp[:, 0:n]), writes=[ppk, "xpre"])
                    fm_proj(0, dst_fn, wt, wcol)
                    if ph == 0:
                        P.op("pool", lambda e: e.tensor_copy(out=ctail[:, j, ch, :], in_=xpre[:, 1024:1027]), reads=["xpre"], writes=["ctail"])
                        finals.append(P.op("sp", lambda e: e.dma_start(out=O["conv_s"][j][:, ch * 128:(ch + 1) * 128].rearrange("r p -> p r"), in_=xpre[:, SOFF + 4:SOFF + 7]), reads=["xpre"], dma=True))
                    else:
                        finals.append(P.op("sp", lambda e: e.dma_start(out=O["conv_p"][j][:, ch * 128:(ch + 1) * 128].rearrange("r p -> p r"), in_=xpre[:, 1024:1027]), reads=["xpre"], dma=True))
                    if kind == "C":
                        conv_chunk(ch, cT, "cT", False)
                        return
                    conv_chunk(ch, xfT, "xcv", True)
                    if kind == "B":
                        P.op("dve", lambda e: e.tensor_copy(out=bT, in_=xfT), reads=["xcv"], writes=["bT"])
                    for t0 in range(0, NT, 4):
                        nt = min(4, NT - t0)
                        P.op("pe", lambda e: e.nop(), writes=["pf"])
                        for q in range(nt):
                            P.op("pe", lambda e, q=q, t0=t0: e.transpose(out=pf[:, q * 128:(q + 1) * 128], in_=xfT[:, (t0 + q) * 128:(t0 + q + 1) * 128], identity=ident[:]), reads=["xcv", "ident"], writes=["pf"], pe_acc=True)
                        if kind == "B":
                            P.op("act", lambda e, t0=t0, nt=nt: e.copy(out=btm[:, t0:t0 + nt, :], in_=pf[:, 0:nt * 128].rearrange("p (a b) -> p a b", a=nt)), writes=["pf", "btm"])
                        else:
                            P.op("act", lambda e, t0=t0, nt=nt, ci=ci: e.copy(out=xtm[:, t0:t0 + nt, ci * 128:(ci + 1) * 128], in_=pf[:, 0:nt * 128].rearrange("p (a b) -> p a b", a=nt)), writes=["pf", "xtm"])

                run_chunk(32 + g, w, 0, "B", 0)
                run_chunk(40 + g, w, 128, "C", 0)
                for half in range(2):
                    w = wb[1 - half]
                    load_w(w[:, :, 0:256], w_in, 0, KC, 4096 + g * 512 + half * 256, 256, "wb")
                    for q in range(2):
                        run_chunk(g * 4 + half * 2 + q, w, q * 128, "x", half * 2 + q)

                def chunk(tt, st, stb, stk):
                    a_t = att[:, tt, :]
                    d_t = dtt[:, tt, :]
                    P.op("pe", lambda e: e.nop(), writes=["pf"])
                    P.op("pe", lambda e: e.matmul(pf[:, 0:8], lhsT=Tri[:], rhs=a_t, start=True, stop=True), reads=["att", "Tri"], writes=["pf"], pe_acc=True)
                    P.op("pe", lambda e: e.matmul(pf[:, 8:16], lhsT=ones[:], rhs=a_t, start=True, stop=True), reads=["att", "ones"], writes=["pf"], pe_acc=True)
                    P.op("act", lambda e: e.activation(out=sc[:, 0:16], in_=pf[:, 0:16], func=AF.Exp), writes=["pf", "sc"])
                    P.op("dve", lambda e: e.tensor_tensor(out=sc[:, 24:32], in0=pf[:, 8:16], in1=pf[:, 0:8], op=ALU.subtract) if False else e.tensor_copy(out=sc[:, 32:48], in_=pf[:, 0:16]), writes=["pf", "sc2"])
                    P.op("dve", lambda e: e.tensor_tensor(out=sc[:, 24:32], in0=sc[:, 40:48], in1=sc[:, 32:40], op=ALU.subtract), reads=["sc2"], writes=["sc3"])
                    P.op("act", lambda e: e.activation(out=sc[:, 24:32], in_=sc[:, 24:32], func=AF.Exp), writes=["sc3"])
                    P.op("dve", lambda e: e.tensor_tensor(out=sc[:, 16:24], in0=sc[:, 24:32], in1=d_t, op=ALU.mult), reads=["sc3", "dtt"], writes=["sc4"])
                    P.op("dve", lambda e: e.tensor_tensor(out=xdt.rearrange("p (h d) -> p h d", h=8), in0=xtm[:, tt, :].rearrange("p (h d) -> p h d", h=8), in1=bc(d_t, 2, 64), op=ALU.mult), reads=["xtm", "dtt"], writes=["xdt"])
                    P.op("dve", lambda e: e.tensor_tensor(out=xw.rearrange("p (h d) -> p h d", h=8), in0=xtm[:, tt, :].rearrange("p (h d) -> p h d", h=8), in1=bc(sc[:, 16:24], 2, 64), op=ALU.mult), reads=["xtm", "sc4"], writes=["xw"])
                    pc, pck = next_pm()
                    P.op("pe", lambda e: e.matmul(pc[:, 0:128], lhsT=bT[:, tt * 128:(tt + 1) * 128], rhs=cT[:, tt * 128:(tt + 1) * 128], start=True, stop=True), reads=["bT", "cT"], writes=[pck])
                    P.op("act", lambda e: e.copy(out=cbs, in_=pc[:, 0:128]), writes=[pck, "cbs"])
                    py, pyk = pyb, "pyb"
                    for r in range(8):
                        lT, ex, mT = lTs[r % 2], exs[r % 2], mTs[r % 2]
                        lk, ek, mk = ("lT", r % 2), ("ex", r % 2), ("mT", r % 2)
                        P.op("dve", lambda e, r=r, lT=lT: e.tensor_scalar(out=lT, in0=SL[:], scalar1=att[:, tt, r:r + 1], scalar2=None, op0=ALU.mult), reads=["SL", "att"], writes=[lk])
                        pg_, pgk_ = next_pm()
                        P.op("pe", lambda e, pg_=pg_, lT=lT: e.matmul(pg_[:, 0:128], lhsT=lT, rhs=Tri[:], start=True, stop=False), reads=[lk, "Tri"], writes=[pgk_], pe_acc=True)
                        P.op("pe", lambda e, pg_=pg_: e.matmul(pg_[:, 0:128], lhsT=ident[:], rhs=NEG[:], start=False, stop=True), reads=["ident", "NEG"], writes=[pgk_], pe_acc=True)
                        P.op("act", lambda e, pg_=pg_, ex=ex: e.activation(out=ex, in_=pg_[:, 0:128], func=AF.Exp), writes=[pgk_, ek])
                        P.op("dve", lambda e, ex=ex, mT=mT: e.tensor_tensor(out=mT, in0=ex, in1=cbs, op=ALU.mult), reads=[ek, "cbs"], writes=[mk])
                        P.op("pe", lambda e, r=r, py=py, mT=mT: e.matmul(py[:, r * 64:(r + 1) * 64], lhsT=mT, rhs=xdt[:, r * 64:(r + 1) * 64], start=True, stop=True), reads=[mk, "xdt"], writes=[pyk], pe_acc=True)
                    po_, pok_ = next_pm()
                    P.op("pe", lambda e, po_=po_: e.matmul(po_[:], lhsT=cT[:, tt * 128:(tt + 1) * 128], rhs=stb, start=True, stop=True), reads=["cT", stk + "b"], writes=[pok_])
                    P.op("dve", lambda e, po_=po_: e.tensor_tensor(out=yy.rearrange("p (h d) -> p h d", h=8), in0=po_[:].rearrange("p (h d) -> p h d", h=8), in1=bc(sc[:, 0:8], 2, 64), op=ALU.mult), reads=["sc"], writes=[pok_, "yy"])
                    P.op("dve", lambda e, py=py: e.tensor_tensor(out=yy, in0=py[:], in1=yy, op=ALU.add), writes=[pyk, "yy"])
                    P.op("dve", lambda e, g=g: e.tensor_tensor(out=gg.rearrange("p (h d) -> p h d", h=8), in0=xtm[:, tt, :].rearrange("p (h d) -> p h d", h=8), in1=bc(D_bc[:, g * 8:(g + 1) * 8], 2, 64), op=ALU.mult), reads=["xtm", "Dbc"], writes=["gg"])
                    P.op("dve", lambda e: e.tensor_tensor(out=yy, in0=yy, in1=gg, op=ALU.add), reads=["gg"], writes=["yy"])
                    P.op("dve", lambda e: e.tensor_tensor(out=gg, in0=yy, in1=zs[:, tt, :], op=ALU.mult), reads=["yy", ("zs", tt)], writes=["gg"])
                    P.op("act", lambda e: e.activation(out=yy, in_=gg, func=AF.Square, accum_out=small[:, 16:17]), reads=["gg"], writes=["yy", "gss"])
                    P.op("act", lambda e: e.activation(out=small[:, 17:18], in_=small[:, 16:17], func=AF.Sqrt, scale=1.0 / 512, bias=small[:, 8:9]), reads=["gss", "epsc"], writes=["gstd"])
                    P.op("dve", lambda e: e.reciprocal(out=small[:, 18:19], in_=small[:, 17:18]), reads=["gstd"], writes=["grstd"])
                    P.op("dve", lambda e: e.scalar_tensor_tensor(out=gnb, in0=gg, scalar=small[:, 18:19], in1=gnw_bc, op0=ALU.mult, op1=ALU.mult), reads=["gg", "grstd", "gnw"], writes=["gnb"])
                    pt, ptk = next_pt()
                    for q in range(4):
                        P.op("pe", lambda e, q=q, pt=pt: e.transpose(out=pt[:, q * 128:(q + 1) * 128], in_=gnb[:, q * 128:(q + 1) * 128], identity=identb[:]), reads=["gnb", "identb"], writes=[ptk], pe_acc=True)
                    P.op("act", lambda e, pt=pt: e.copy(out=gnT[:, :, tt * 128:(tt + 1) * 128], in_=pt[:, 0:512].rearrange("p (a b) -> p a b", a=4)), writes=[ptk, ("gnT", tt)])
                    ps_, psk_ = next_pm()
                    P.op("pe", lambda e, ps_=ps_: e.matmul(ps_[:], lhsT=btm[:, tt, :], rhs=xw, start=True, stop=True), reads=["btm", "xw"], writes=[psk_])
                    P.op("dve", lambda e: e.tensor_tensor(out=st.rearrange("p (h d) -> p h d", h=8), in0=st.rearrange("p (h d) -> p h d", h=8), in1=bc(sc[:, 8:16], 2, 64), op=ALU.mult), reads=["sc"], writes=[stk])
                    P.op("dve", lambda e, ps_=ps_: e.tensor_tensor(out=st, in0=ps_[:], in1=st, op=ALU.add), writes=[psk_, stk])
                    P.op("act", lambda e: e.copy(out=stb, in_=st), reads=[stk], writes=[stk + "b"])

                def store_state(st, stk, dst):
                    P.op("pe", lambda e: e.nop(), writes=["pf"])
                    for q in range(4):
                        P.op("pe", lambda e, q=q: e.transpose(out=pf[:, q * 128:(q + 1) * 128], in_=st[:, q * 128:(q + 1) * 128], identity=ident[:]), reads=[stk, "ident"], writes=["pf"], pe_acc=True)
                    P.op("act", lambda e: e.copy(out=st_in, in_=pf[:]), writes=["pf", "xpre"])
                    finals.append(P.op("sp", lambda e, g=g: e.dma_start(out=dst[g * 512:(g + 1) * 512, :].rearrange("(q p) n -> p q n", p=128), in_=st_in.rearrange("p (q n) -> p q n", q=4)), reads=["xpre"], dma=True))

                if ph == 0:
                    P.op("pool", lambda e: e.memset(hT, 0.0), writes=["hT"])
                else:
                    P.op("sp", lambda e, g=g: e.dma_start(out=hT, in_=S["hsave"][j][:, g * 512:(g + 1) * 512]), writes=["hT"], dma=True)
                P.op("act", lambda e: e.copy(out=hTb, in_=hT), reads=["hT"], writes=["hTb"])
                for tt in range(8):
                    chunk(tt, hT, hTb, "hT")
                if ph == 0:
                    P.op("sp", lambda e, g=g: e.dma_start(out=S["hsave"][j][:, g * 512:(g + 1) * 512], in_=hT), reads=["hT"], writes=["hsave"], dma=True)
                else:
                    store_state(hT, "hT", O["ssm_p"][j])
                if NT == 9:
                    P.op("sp", lambda e, g=g: e.dma_start(out=st_in.rearrange("p (q n) -> p q n", q=4), in_=I["state_ssm"][j][g * 512:(g + 1) * 512, :].rearrange("(q p) n -> p q n", p=128)), writes=["xpre"], dma=True)
                    P.op("pe", lambda e: e.nop(), writes=["pf"])
                    for q in range(4):
                        P.op("pe", lambda e, q=q: e.transpose(out=pf[:, q * 128:(q + 1) * 128], in_=st_in[:, q * 128:(q + 1) * 128], identity=ident[:]), reads=["xpre", "ident"], writes=["pf"], pe_acc=True)
                    P.op("act", lambda e: e.copy(out=hS, in_=pf[:]), writes=["pf", "hT"])
                    P.op("act", lambda e: e.copy(out=hTb, in_=hS), reads=["hT"], writes=["hTb"])
                    chunk(8, hS, hTb, "hT")
                    store_state(hS, "hT", O["ssm_s"][j])

                for cg in range(4):
                    w_o = wo[cg % 2]
                    load_w(w_o, I["ssd_w_out"][j], g * 512, 4, cg * 512, 512, ("wo", 0))
                    for tt in range(NT):
                        pp, ppk = next_pm()
                        for ci in range(4):
                            P.op("pe", lambda e, pp=pp, ci=ci, tt=tt, w_o=w_o: e.matmul(pp[:], lhsT=gnT[:, ci, tt * 128:(tt + 1) * 128], rhs=w_o[:, ci, :], start=(ci == 0), stop=(ci == 3)), reads=[("wo", 0), ("gnT", tt)], writes=[ppk], pe_acc=True)
                        P.op("dve", lambda e, pp=pp, tt=tt, cg=cg: e.tensor_tensor(out=R[:, tt, cg * 512:(cg + 1) * 512], in0=pp[:], in1=R[:, tt, cg * 512:(cg + 1) * 512], op=ALU.add), writes=[ppk, ("R", tt)])

        def moba_sample(QsTf, KsT, Vs, cv0):
            P.barrier()
            cv = Carver()
            cv.off = cv0.keep
            SCL = 1.0 / math.sqrt(128.0)
            pti = cv.f32(128).bitcast(I32)
            ptf = cv.f32(128)
            idx = cv.f32(128).bitcast(I32)
            pio = cv.f32(1)
            Qbdf = cv.f32(16 * 64).rearrange("p (h n) -> p h n", h=16)
            Qbd = cv.bf16(16 * 64).rearrange("p (h n) -> p h n", h=16)
            BD = cv.bf16(16 * 64).rearrange("p (h n) -> p h n", h=16)
            cm = cv.f32(4)
            kpages = [cv.f32(2 * D).rearrange("p (i n) -> p i n", i=2) for _ in range(2)]
            vpage = cv.bf16(2 * D).rearrange("p (i n) -> p i n", i=2)
            KTs = cv.bf16(16 * 256).rearrange("p (h n) -> p h n", h=16)
            ksumS = cv.f32(1024)
            gs = cv.f32(64)
            top8 = cv.f32(8)
            als = cv.f32(64)
            lacc = cv.f32(66)
            Pfs = cv.f32(256)
            Pbs = cv.bf16(256)
            PTs = cv.bf16(2 * 64).rearrange("p (i n) -> p i n", i=2)
            PTx = cv.bf16(2 * 16 * 64).rearrange("p (i h n) -> p i h n", i=2, h=16)
            o2 = cv.bf16(128)
            oTs = cv.bf16(64)
            wos = cv.bf16(16 * 512).rearrange("p (k n) -> p k n", k=16)
            P.op("sp", lambda e: e.dma_start(out=pti, in_=bass.AP(I["page_table"].tensor, 0, [[0, 128], [1, 128]])), writes=["pti"], dma=True)
            P.op("pool", lambda e: e.iota(pio.bitcast(I32), [[0, 1]], base=0, channel_multiplier=1), writes=["pio"])
            P.op("dve", lambda e: e.tensor_copy(out=ptf, in_=pti), reads=["pti"], writes=["ptf"])
            P.op("dve", lambda e: e.tensor_copy(out=pio, in_=pio.bitcast(I32)), writes=["pio"])
            P.op("dve", lambda e: e.tensor_scalar(out=ptf, in0=ptf, scalar1=128.0, scalar2=pio[:, 0:1], op0=ALU.mult, op1=ALU.add), reads=["pio"], writes=["ptf"])
            P.op("dve", lambda e: e.tensor_copy(out=idx, in_=ptf), reads=["ptf"], writes=["idx"])
            P.op("sp", lambda e: e.dma_start(out=cm[0:64, :], in_=I["own_s"]), writes=["cm"], dma=True)
            P.op("pool", lambda e: e.memset(Qbdf, 0.0), writes=["Qbdf"])
            P.op("pool", lambda e: e.memset(BD, 0.0), writes=["BD"])
            for h in range(16):
                P.op("pool", lambda e, h=h: e.tensor_copy(out=Qbdf[:, h, 4 * h:4 * h + 4], in_=QsTf[:, h, :]), writes=["Qbdf"])
                P.op("pool", lambda e, h=h: e.memset(BD[:, h, 4 * h:4 * h + 4], 1.0), writes=["BD"])
            P.op("dve", lambda e: e.tensor_copy(out=Qbd, in_=Qbdf), reads=["Qbdf"], writes=["Qbd"])
            P.op("pool", lambda e: e.memset(lacc, 0.0), writes=["lacc"])

            def gather(dst, src, lp, key):
                return P.op("pool", lambda e: e.indirect_dma_start(out=dst, out_offset=None, in_=src, in_offset=bass.IndirectOffsetOnAxis(ap=idx[:, lp:lp + 1], axis=0)), reads=["idx"], writes=[key], dma=True)

            for blk in range(64):
                kpage, kpk = kpages[blk % 2], ("kpage", blk % 2)
                for i2 in range(2):
                    gather(kpage[:, i2, :], I["cache_k"], 2 * blk + i2, kpk)
                for hh in range(16):
                    bank = pm[hh // 8]
                    col = (hh % 8) * 64 + blk
                    for i2 in range(2):
                        P.op("pe", lambda e, bank=bank, col=col, hh=hh, i2=i2, kpage=kpage: e.matmul(bank[:, col:col + 1], lhsT=kpage[:, i2, hh * 128:(hh + 1) * 128], rhs=ones[:, 0:1], start=(i2 == 0), stop=(i2 == 1)), reads=[kpk, "ones"], writes=["pm%d" % (hh // 8)], pe_acc=True)
            for b2 in range(2):
                P.op("act", lambda e, b2=b2: e.copy(out=ksumS[:, b2 * 512:(b2 + 1) * 512], in_=pm[b2][:]), writes=["pm%d" % b2, "ksumS"])
            for hh in range(16):
                P.op("pe", lambda e, hh=hh: e.matmul(pf[0:64, 0:64], lhsT=Qbdf[:, hh, :], rhs=ksumS[:, hh * 64:(hh + 1) * 64], start=(hh == 0), stop=(hh == 15)), reads=["Qbdf", "ksumS"], writes=["pf"], pe_acc=True)
            P.op("act", lambda e: e.copy(out=gs[0:64, :], in_=pf[0:64, 0:64]), writes=["pf", "gs"])
            P.op("dve", lambda e: e.max(out=top8[0:64, :], in_=gs[0:64, :]), reads=["gs"], writes=["top8"])
            P.op("dve", lambda e: e.tensor_scalar(out=als[0:64, :], in0=gs[0:64, :], scalar1=top8[0:64, 2:3], scalar2=None, op0=ALU.is_ge), reads=["gs", "top8"], writes=["als"])

            first = [True]

            def pv(lhs_fn, rhs_fn, reads, last):
                for hh in range(16):
                    st = first[0]
                    first[0] = False
                    P.op("pe", lambda e, hh=hh, st=st: e.matmul(pyb[0:64, 0:128], lhsT=lhs_fn(hh), rhs=rhs_fn(hh), start=st, stop=(last and hh == 15)), reads=reads, writes=["pyb"], pe_acc=True)

            for blk in range(64):
                kpage, kpk = kpages[blk % 2], ("kpage", blk % 2)
                for i2 in range(2):
                    gather(kpage[:, i2, :], I["cache_k"], 2 * blk + i2, kpk)
                    gather(vpage[:, i2, :], I["cache_v"], 2 * blk + i2, "vpage")
                for hp in range(8):
                    P.op("pe", lambda e: e.nop(), writes=["pf"])
                    for q in range(4):
                        hh, i2 = 2 * hp + q // 2, q % 2
                        P.op("pe", lambda e, q=q, hh=hh, i2=i2, kpage=kpage: e.transpose(out=pf[:, q * 128:(q + 1) * 128], in_=kpage[:, i2, hh * 128:(hh + 1) * 128], identity=ident[:]), reads=[kpk, "ident"], writes=["pf"], pe_acc=True)
                    P.op("act", lambda e, hp=hp: e.copy(out=KTs[:, 2 * hp:2 * hp + 2, :], in_=pf[:].rearrange("p (a b) -> p a b", a=2)), writes=["pf", "KTs"])
                ps_, psk_ = next_pm()
                for hh in range(16):
                    P.op("pe", lambda e, ps_=ps_, hh=hh: e.matmul(ps_[0:64, 0:256], lhsT=Qbd[:, hh, :], rhs=KTs[:, hh, :], start=(hh == 0), stop=(hh == 15)), reads=["Qbd", "KTs"], writes=[psk_], pe_acc=True)
                P.op("act", lambda e, ps_=ps_: e.activation(out=Pfs[0:64, :], in_=ps_[0:64, 0:256], func=AF.Exp, scale=SCL), writes=[psk_, "Pfs"])
                P.op("dve", lambda e, blk=blk: e.tensor_scalar(out=Pbs[0:64, :], in0=Pfs[0:64, :], scalar1=als[0:64, blk:blk + 1], scalar2=None, op0=ALU.mult, op1=ALU.add, accum_out=lacc[0:64, blk:blk + 1]), reads=["Pfs", "als"], writes=["Pbs", "lacc"])
                pt, ptk = next_pt()
                for i2 in range(2):
                    P.op("pe", lambda e, pt=pt, i2=i2: e.transpose(out=pt[:, i2 * 64:(i2 + 1) * 64], in_=Pbs[0:64, i2 * 128:(i2 + 1) * 128], identity=identb[0:64, 0:64]), reads=["Pbs", "identb"], writes=[ptk], pe_acc=True)
                P.op("act", lambda e, pt=pt: e.copy(out=PTs, in_=pt[:, 0:128].rearrange("p (a b) -> p a b", a=2)), writes=[ptk, "PTs"])
                for i2 in range(2):
                    P.op("dve", lambda e, i2=i2: e.tensor_tensor(out=PTx[:, i2, :, :], in0=bc(PTs[:, i2, :], 1, 16), in1=BD, op=ALU.mult), reads=["PTs", "BD"], writes=["PTx"])
                for i2 in range(2):
                    pv(lambda hh, i2=i2: PTx[:, i2, hh, :], lambda hh, i2=i2: vpage[:, i2, hh * 128:(hh + 1) * 128], ["PTx", "vpage"], False)
            ps_, psk_ = next_pm()
            for hh in range(16):
                P.op("pe", lambda e, ps_=ps_, hh=hh: e.matmul(ps_[0:64, 0:4], lhsT=Qbd[:, hh, :], rhs=KsT[:, hh, :], start=(hh == 0), stop=(hh == 15)), reads=["Qbd"], writes=[psk_], pe_acc=True)
            P.op("act", lambda e, ps_=ps_: e.activation(out=Pfs[0:64, 0:4], in_=ps_[0:64, 0:4], func=AF.Exp, scale=SCL), writes=[psk_, "Pfs"])
            P.op("dve", lambda e: e.tensor_tensor(out=Pfs[0:64, 0:4], in0=Pfs[0:64, 0:4], in1=cm[0:64, :], op=ALU.mult), reads=["cm"], writes=["Pfs"])
            P.op("dve", lambda e: e.tensor_scalar(out=Pbs[0:64, 0:4], in0=Pfs[0:64, 0:4], scalar1=1.0, scalar2=None, op0=ALU.mult, op1=ALU.add, accum_out=lacc[0:64, 64:65]), reads=["Pfs"], writes=["Pbs", "lacc"])
            pt, ptk = next_pt()
            P.op("pe", lambda e, pt=pt: e.transpose(out=pt[0:4, 0:64], in_=Pbs[0:64, 0:4], identity=identb[0:64, 0:64]), reads=["Pbs", "identb"], writes=[ptk])
            P.op("act", lambda e, pt=pt: e.copy(out=PTs[0:4, 0, :], in_=pt[0:4, 0:64]), writes=[ptk, "PTs"])
            P.op("dve", lambda e: e.tensor_tensor(out=PTx[0:4, 0, :, :], in0=bc(PTs[0:4, 0, :], 1, 16), in1=BD[0:4], op=ALU.mult), reads=["PTs", "BD"], writes=["PTx"])
            pv(lambda hh: PTx[0:4, 0, hh, :], lambda hh: Vs[0:4, hh * 128:(hh + 1) * 128], ["PTx"], True)
            P.op("dve", lambda e: e.reduce_sum(out=small[0:64, 24:25], in_=lacc[0:64, 0:65], axis=AX.X), reads=["lacc"], writes=["lsum"])
            P.op("dve", lambda e: e.reciprocal(out=small[0:64, 25:26], in_=small[0:64, 24:25]), reads=["lsum"], writes=["lrec"])
            P.op("dve", lambda e: e.tensor_scalar(out=o2[0:64, :], in0=pyb[0:64, 0:128], scalar1=small[0:64, 25:26], scalar2=None, op0=ALU.mult), reads=["lrec"], writes=["pyb", "o2"])
            pt, ptk = next_pt()
            P.op("pe", lambda e, pt=pt: e.transpose(out=pt[:, 0:64], in_=o2[0:64, :], identity=identb[0:64, 0:64]), reads=["o2", "identb"], writes=[ptk])
            P.op("act", lambda e, pt=pt: e.copy(out=oTs, in_=pt[:, 0:64]), writes=[ptk, "oTs"])
            for cg in range(4):
                load_w(wos, I["moba_w_o"][0], 0, 16, cg * 512, 512, "wos")
                pp, ppk = next_pm()
                for hh in range(16):
                    P.op("pe", lambda e, pp=pp, hh=hh: e.matmul(pp[0:4, :], lhsT=oTs[:, 4 * hh:4 * hh + 4], rhs=wos[:, hh, :], start=(hh == 0), stop=(hh == 15)), reads=["oTs", "wos"], writes=[ppk], pe_acc=True)
                P.op("dve", lambda e, pp=pp, cg=cg: e.tensor_tensor(out=R[0:4, 8, cg * 512:(cg + 1) * 512], in0=pp[0:4, :], in1=R[0:4, 8, cg * 512:(cg + 1) * 512], op=ALU.add), writes=[ppk, ("R", 8)])

        def moba(li, ph, NT):
            P.barrier()
            cv = Carver()
            xs_scr = cv.bf16(D)
            make_xnT(li, NT, xs_scr)
            P.barrier()
            cv = Carver()
            T = NT * 128
            NKT = (ph + 1) * 8
            SCL = 1.0 / math.sqrt(128.0)
            QsTf = cv.f32(64).rearrange("p (h t) -> p h t", h=16)
            KsT = cv.bf16(64).rearrange("p (h t) -> p h t", h=16)
            Vs = cv.bf16(D)
            cv.keep = cv.off
            wq = cv.bf16(KC * 384).rearrange("p (k n) -> p k n", k=KC)
            qkv = cv.f32(NT * 384).rearrange("p (t n) -> p t n", t=NT)
            rope = cv.f32(NT * 32).rearrange("p (t n) -> p t n", t=NT)
            validt = cv.f32(64).rearrange("p (t n) -> p t n", t=8)
            negt = cv.f32(64).rearrange("p (t n) -> p t n", t=8)
            ownt = cv.f32(64).rearrange("p (t n) -> p t n", t=8)
            rt = [cv.f32(NT * 16).rearrange("p (t n) -> p t n", t=NT) for _ in range(4)]
            ksumT = cv.f32(8)
            QTf = cv.f32(T)
            QT = cv.bf16(T)
            KT = cv.bf16(2048)
            Vb = cv.bf16(16 * 130).rearrange("p (t n) -> p t n", t=16)
            alw = cv.f32(64).rearrange("p (t n) -> p t n", t=8)
            gm = cv.f32(8)
            top8 = cv.f32(8)
            Pf = [cv.f32(512) for _ in range(2)]
            Pb = [cv.bf16(512) for _ in range(2)]
            PT = [cv.bf16(512) for _ in range(2)]
            ob = cv.bf16(128)
            oT = cv.bf16(4 * 1024).rearrange("p (k n) -> p k n", k=4)
            wo = cv.bf16(4 * 512).rearrange("p (k n) -> p k n", k=4)
            print("moba arena", cv.off)
            wqkv = I["moba_w_qkv"][0]
            P.op("pool", lambda e: e.memset(rope, 0.0), writes=["rope"])
            P.op("sp", lambda e: e.dma_start(out=rope[:, 0:8, :], in_=I["rope_p"][ph * TP:(ph + 1) * TP, :].rearrange("(t p) n -> p t n", p=128)), writes=["rope"], dma=True)
            if NT == 9:
                P.op("sp", lambda e: e.dma_start(out=rope[0:4, 8, :], in_=I["rope_s"]), writes=["rope"], dma=True)
            for (dst, nm) in ((validt, "valid_p"), (ownt, "own_p")):
                P.op("sp", lambda e, dst=dst, nm=nm: e.dma_start(out=dst, in_=I[nm][ph * TP:(ph + 1) * TP, :].rearrange("(t p) n -> p t n", p=128)), writes=[nm], dma=True)
            P.op("dve", lambda e: e.tensor_scalar(out=negt, in0=validt, scalar1=-1.0, scalar2=1e30, op0=ALU.add, op1=ALU.mult), reads=["valid_p"], writes=["negt"])
            P.op("pool", lambda e: e.memset(Vb, 1.0), writes=["Vb"])

            for h in range(16):
                for i3 in range(3):
                    load_w(wq[:, :, i3 * 128:(i3 + 1) * 128], wqkv, 0, KC, i3 * 2048 + h * 128, 128, "wq")
                for tt in range(NT):
                    pp, ppk = next_pm()
                    for kc in range(KC):
                        P.op("pe", lambda e, pp=pp, kc=kc, tt=tt: e.matmul(pp[:, 0:384], lhsT=xnT[:, kc, tt * 128:(tt + 1) * 128], rhs=wq[:, kc, :], start=(kc == 0), stop=(kc == KC - 1)), reads=["wq", ("xnT", tt)], writes=[ppk], pe_acc=True)
                    P.op("act", lambda e, pp=pp, tt=tt: e.copy(out=qkv[:, tt, :], in_=pp[:, 0:384]), writes=[ppk, "qkv"])
                for off in (0, 128):
                    x1 = qkv[:, :, off:off + 16]
                    x2 = qkv[:, :, off + 16:off + 32]
                    cs, sn = rope[:, :, 0:16], rope[:, :, 16:32]
                    P.op("dve", lambda e, x1=x1, cs=cs: e.tensor_tensor(out=rt[0], in0=x1, in1=cs, op=ALU.mult), reads=["qkv", "rope"], writes=["rt0"])
                    P.op("dve", lambda e, x2=x2, sn=sn: e.tensor_tensor(out=rt[1], in0=x2, in1=sn, op=ALU.mult), reads=["qkv", "rope"], writes=["rt1"])
                    P.op("dve", lambda e, x1=x1, sn=sn: e.tensor_tensor(out=rt[2], in0=x1, in1=sn, op=ALU.mult), reads=["qkv", "rope"], writes=["rt2"])
                    P.op("dve", lambda e, x2=x2, cs=cs: e.tensor_tensor(out=rt[3], in0=x2, in1=cs, op=ALU.mult), reads=["qkv", "rope"], writes=["rt3"])
                    P.op("dve", lambda e, x1=x1: e.tensor_tensor(out=x1, in0=rt[0], in1=rt[1], op=ALU.subtract), reads=["rt0", "rt1"], writes=["qkv"])
                    P.op("dve", lambda e, x2=x2: e.tensor_tensor(out=x2, in0=rt[2], in1=rt[3], op=ALU.add), reads=["rt2", "rt3"], writes=["qkv"])
                r0 = ph * TP
                finals.append(P.op("sp", lambda e, h=h, r0=r0: e.dma_start(out=O["k_p"][r0:r0 + TP, h * 128:(h + 1) * 128].rearrange("(t p) n -> p t n", p=128), in_=qkv[:, 0:8, 128:256]), reads=["qkv"], dma=True))
                finals.append(P.op("sp", lambda e, h=h, r0=r0: e.dma_start(out=O["v_p"][r0:r0 + TP, h * 128:(h + 1) * 128].rearrange("(t p) n -> p t n", p=128), in_=qkv[:, 0:8, 256:384]), reads=["qkv"], dma=True))
                if NT == 9:
                    finals.append(P.op("sp", lambda e, h=h: e.dma_start(out=O["k_s"][:, h * 128:(h + 1) * 128], in_=qkv[0:4, 8, 128:256]), reads=["qkv"], dma=True))
                    finals.append(P.op("sp", lambda e, h=h: e.dma_start(out=O["v_s"][:, h * 128:(h + 1) * 128], in_=qkv[0:4, 8, 256:384]), reads=["qkv"], dma=True))
                if ph == 0:
                    P.op("pool", lambda e: e.memset(ksumT, 0.0), writes=["ksumT"])
                else:
                    P.op("sp", lambda e, h=h: e.dma_start(out=ksumT[:, 0:4], in_=S["ksum"][h]), writes=["ksumT"], dma=True)
                P.op("pe", lambda e: e.nop(), writes=["pf"])
                for b4 in range(4):
                    for i2 in range(2):
                        P.op("pe", lambda e, b4=b4, i2=i2: e.matmul(pf[:, b4:b4 + 1], lhsT=qkv[:, 2 * b4 + i2, 128:256], rhs=ones[:, 0:1], start=(i2 == 0), stop=(i2 == 1)), reads=["qkv", "ones"], writes=["pf"], pe_acc=True)
                P.op("act", lambda e: e.copy(out=ksumT[:, ph * 4:ph * 4 + 4], in_=pf[:, 0:4]), writes=["pf", "ksumT"])
                if ph == 0:
                    P.op("sp", lambda e, h=h: e.dma_start(out=S["ksum"][h], in_=ksumT[:, 0:4]), reads=["ksumT"], writes=["ksum_scr"], dma=True)
                if ph == 1:
                    P.op("sp", lambda e, h=h: e.dma_start(out=KT[:, 0:1024], in_=S["kT"][h]), writes=["KT"], dma=True)
                    P.op("sp", lambda e, h=h: e.dma_start(out=Vb[:, 0:8, 0:128], in_=S["v"][h]), writes=["Vb"], dma=True)
                for (src_off, kind) in ((0, "q"), (128, "k")):
                    for t0 in range(0, NT, 4):
                        nt = min(4, NT - t0)
                        P.op("pe", lambda e: e.nop(), writes=["pf"])
                        for q in range(nt):
                            P.op("pe", lambda e, q=q, t0=t0, src_off=src_off: e.transpose(out=pf[:, q * 128:(q + 1) * 128], in_=qkv[:, t0 + q, src_off:src_off + 128], identity=ident[:]), reads=["qkv", "ident"], writes=["pf"], pe_acc=True)
                        if kind == "q":
                            P.op("act", lambda e, t0=t0, nt=nt: e.copy(out=QTf[:, t0 * 128:(t0 + nt) * 128], in_=pf[:, 0:nt * 128]), writes=["pf", "QTf"])
                            P.op("dve", lambda e, t0=t0, nt=nt: e.tensor_copy(out=QT[:, t0 * 128:(t0 + nt) * 128], in_=QTf[:, t0 * 128:(t0 + nt) * 128]), reads=["QTf"], writes=["QT"])
                            if t0 == 8:
                                P.op("pool", lambda e, h=h: e.tensor_copy(out=QsTf[:, h, :], in_=QTf[:, 1024:1028]), reads=["QTf"], writes=["QsTf"])
                        else:
                            if t0 < 8:
                                P.op("act", lambda e, t0=t0, nt=nt: e.copy(out=KT[:, ph * 1024 + t0 * 128:ph * 1024 + (t0 + nt) * 128], in_=pf[:, 0:nt * 128]), writes=["pf", "KT"])
                            else:
                                P.op("act", lambda e, h=h: e.copy(out=KsT[:, h, :], in_=pf[:, 0:4]), writes=["pf", "KsT"])
                P.op("act", lambda e: e.copy(out=Vb[:, ph * 8:ph * 8 + 8, 0:128], in_=qkv[:, 0:8, 256:384]), reads=["qkv"], writes=["Vb"])
                if NT == 9:
                    P.op("act", lambda e, h=h: e.copy(out=Vs[0:4, h * 128:(h + 1) * 128], in_=qkv[0:4, 8, 256:384]), reads=["qkv"], writes=["Vs"])
                if ph == 0:
                    P.op("sp", lambda e, h=h: e.dma_start(out=S["kT"][h], in_=KT[:, 0:1024]), reads=["KT"], writes=["kT_scr"], dma=True)
                    P.op("sp", lambda e, h=h: e.dma_start(out=S["v"][h], in_=Vb[:, 0:8, 0:128]), reads=["Vb"], writes=["v_scr"], dma=True)
                for tt in range(8):
                    P.op("pe", lambda e: e.nop(), writes=["pf"])
                    P.op("pe", lambda e, tt=tt: e.matmul(pf[:, 0:8], lhsT=QTf[:, tt * 128:(tt + 1) * 128], rhs=ksumT, start=True, stop=True), reads=["QTf", "ksumT"], writes=["pf"], pe_acc=True)
                    P.op("dve", lambda e, tt=tt: e.tensor_tensor(out=gm, in0=pf[:, 0:8], in1=validt[:, tt, :], op=ALU.mult), reads=["valid_p"], writes=["pf", "gm"])
                    P.op("dve", lambda e, tt=tt: e.tensor_tensor(out=gm, in0=gm, in1=negt[:, tt, :], op=ALU.add), reads=["negt"], writes=["gm"])
                    P.op("dve", lambda e: e.max(out=top8, in_=gm), reads=["gm"], writes=["top8"])
                    P.op("dve", lambda e, tt=tt: e.tensor_scalar(out=alw[:, tt, :], in0=gm, scalar1=top8[:, 2:3], scalar2=None, op0=ALU.is_ge), reads=["gm", "top8"], writes=["alw"])
                    P.op("dve", lambda e, tt=tt: e.tensor_tensor(out=alw[:, tt, :], in0=alw[:, tt, :], in1=validt[:, tt, :], op=ALU.mult), reads=["valid_p"], writes=["alw"])
                    P.op("dve", lambda e, tt=tt: e.tensor_tensor(out=alw[:, tt, :], in0=alw[:, tt, :], in1=ownt[:, tt, :], op=ALU.add), reads=["own_p"], writes=["alw"])
                cnt = 0
                for tt in range(8):
                    G = ph * 8 + tt
                    nk = G + 1
                    po, pok = pyb, "pyb"
                    for kg in range(0, nk, 4):
                        ng = min(4, nk - kg)
                        n = ng * 128
                        ps_, psk_ = next_pm()
                        pfb, pbb, ptb_ = Pf[cnt % 2], Pb[cnt % 2], PT[cnt % 2]
                        kf, kb_, kt_ = ("Pf", cnt % 2), ("Pb", cnt % 2), ("PT", cnt % 2)
                        cnt += 1
                        P.op("pe", lambda e, ps_=ps_, tt=tt, kg=kg, n=n: e.matmul(ps_[:, 0:n], lhsT=QT[:, tt * 128:(tt + 1) * 128], rhs=KT[:, kg * 128:kg * 128 + n], start=True, stop=True), reads=["QT", "KT"], writes=[psk_])
                        P.op("act", lambda e, ps_=ps_, pfb=pfb, n=n: e.activation(out=pfb[:, 0:n], in_=ps_[:, 0:n], func=AF.Exp, scale=SCL), writes=[psk_, kf])
                        for b2 in range(0, ng, 2):
                            nb_ = min(2, ng - b2) * 128
                            kbi = (kg + b2) // 2
                            P.op("dve", lambda e, pfb=pfb, pbb=pbb, b2=b2, nb_=nb_, tt=tt, kbi=kbi: e.tensor_scalar(out=pbb[:, b2 * 128:b2 * 128 + nb_], in0=pfb[:, b2 * 128:b2 * 128 + nb_], scalar1=alw[:, tt, kbi:kbi + 1], scalar2=None, op0=ALU.mult), reads=[kf, "alw"], writes=[kb_])
                        if kg + ng == nk:
                            dcol = (ng - 1) * 128
                            P.op("dve", lambda e, pbb=pbb, dcol=dcol: e.tensor_tensor(out=pbb[:, dcol:dcol + 128], in0=pbb[:, dcol:dcol + 128], in1=TriL[:], op=ALU.mult), reads=["TriL"], writes=[kb_])
                        pt, ptk = next_pt()
                        for q in range(ng):
                            P.op("pe", lambda e, pt=pt, pbb=pbb, q=q: e.transpose(out=pt[:, q * 128:(q + 1) * 128], in_=pbb[:, q * 128:(q + 1) * 128], identity=identb[:]), reads=[kb_, "identb"], writes=[ptk], pe_acc=True)
                        P.op("act", lambda e, pt=pt, ptb_=ptb_, n=n: e.copy(out=ptb_[:, 0:n], in_=pt[:, 0:n]), writes=[ptk, kt_])
                        for q in range(ng):
                            kt = kg + q
                            P.op("pe", lambda e, ptb_=ptb_, q=q, kt=kt, nk=nk: e.matmul(po[:, 0:129], lhsT=ptb_[:, q * 128:(q + 1) * 128], rhs=Vb[:, kt, 0:129], start=(kt == 0), stop=(kt == nk - 1)), reads=[kt_, "Vb"], writes=[pok], pe_acc=True)
                    P.op("dve", lambda e: e.reciprocal(out=small[:, 20:21], in_=po[:, 128:129]), writes=[pok, "orec"])
                    P.op("dve", lambda e: e.tensor_scalar(out=ob, in0=po[:, 0:128], scalar1=small[:, 20:21], scalar2=None, op0=ALU.mult), reads=["orec"], writes=[pok, "ob"])
                    pt, ptk = next_pt()
                    P.op("pe", lambda e, pt=pt: e.transpose(out=pt[:, 0:128], in_=ob, identity=identb[:]), reads=["ob", "identb"], writes=[ptk])
                    P.op("act", lambda e, pt=pt, h=h, tt=tt: e.copy(out=oT[:, h % 4, tt * 128:(tt + 1) * 128], in_=pt[:, 0:128]), writes=[ptk, ("oT", tt)])
                if h % 4 == 3:
                    hg = h // 4
                    for cg in range(4):
                        load_w(wo, I["moba_w_o"][0], hg * 512, 4, cg * 512, 512, "wo")
                        for tt in range(8):
                            pp, ppk = next_pm()
                            for ci in range(4):
                                P.op("pe", lambda e, pp=pp, ci=ci, tt=tt: e.matmul(pp[:], lhsT=oT[:, ci, tt * 128:(tt + 1) * 128], rhs=wo[:, ci, :], start=(ci == 0), stop=(ci == 3)), reads=["wo", ("oT", tt)], writes=[ppk], pe_acc=True)
                            P.op("dve", lambda e, pp=pp, tt=tt, cg=cg: e.tensor_tensor(out=R[:, tt, cg * 512:(cg + 1) * 512], in0=pp[:], in1=R[:, tt, cg * 512:(cg + 1) * 512], op=ALU.add), writes=[ppk, ("R", tt)])
            if NT == 9 and not cfg.get("stub_cache"):
                moba_sample(QsTf, KsT, Vs, cv)

        def gmlp(li, ph, NT):
            P.barrier()
            cv = Carver()
            xs_scr = cv.bf16(D)
            make_xnT(li, NT, xs_scr)
            P.barrier()
            cv = Carver()
            VB = cv.bf16(NT * D).rearrange("p (t n) -> p t n", t=NT)
            vs32 = cv.f32(D)
            g_bc = cv.f32(D)
            b_bc = cv.f32(D)
            woff = cv.off
            wb = [cv.bf16(KC * 256).rearrange("p (k n) -> p k n", k=KC) for _ in range(2)]
            wbig = AR[:, woff:woff + KC * 256].bitcast(BF16).rearrange("p (k n) -> p k n", k=KC)
            wsn = AR[:, woff:woff + 2048].rearrange("p (g n) -> p g n", g=16)
            WsT = cv.bf16(16 * 128).rearrange("p (g n) -> p g n", g=16)
            bsc = cv.f32(16)
            tmp = [cv.f32(512) for _ in range(2)]
            junk = cv.bf16(512)
            s1 = cv.f32(NT * 8).rearrange("p (t n) -> p t n", t=NT)
            s2 = cv.f32(NT * 8).rearrange("p (t n) -> p t n", t=NT)
            mu = cv.f32(NT)
            ex2 = cv.f32(NT)
            rstd = cv.f32(NT)
            print("gmlp arena", cv.off)
            w_in = I["gmlp_w_in"][0]
            P.op("sp", lambda e: e.dma_start(out=g_bc, in_=bass.AP(I["gmlp_ln_g"].tensor, 0, [[0, 128], [1, D]])), writes=["g_bc"], dma=True)
            P.op("sp", lambda e: e.dma_start(out=b_bc, in_=bass.AP(I["gmlp_ln_b"].tensor, 0, [[0, 128], [1, D]])), writes=["b_bc"], dma=True)
            P.op("sp", lambda e: e.dma_start(out=bsc, in_=I["gmlp_b_s"][0].rearrange("g i -> i g")), writes=["bsc"], dma=True)
            P.op("sp", lambda e: e.dma_start(out=wsn, in_=I["gmlp_w_s"][0].rearrange("g i j -> i g j")), writes=["wsn"], dma=True)
            P.op("dve", lambda e: e.tensor_tensor(out=wsn, in0=wsn, in1=bc(TriLf[:], 1, 16), op=ALU.mult), reads=["TriLf"], writes=["wsn"])
            for g4 in range(4):
                P.op("pe", lambda e: e.nop(), writes=["pf"])
                for q in range(4):
                    P.op("pe", lambda e, g4=g4, q=q: e.transpose(out=pf[:, q * 128:(q + 1) * 128], in_=wsn[:, g4 * 4 + q, :], identity=ident[:]), reads=["wsn", "ident"], writes=["pf"], pe_acc=True)
                P.op("act", lambda e, g4=g4: e.copy(out=WsT[:, g4 * 4:g4 * 4 + 4, :], in_=pf[:].rearrange("p (a b) -> p a b", a=4)), writes=["pf", "WsT"])
            P.barrier()
            cnt = 0
            for cgw in range(8):
                w = wb[cgw % 2]
                wk = ("wbk", cgw % 2)
                load_w(w, w_in, 0, KC, D + cgw * 256, 256, wk)
                for tt in range(NT):
                    pp, ppk = next_pm()
                    for kc in range(KC):
                        P.op("pe", lambda e, pp=pp, kc=kc, tt=tt, w=w: e.matmul(pp[:, 0:256], lhsT=xnT[:, kc, tt * 128:(tt + 1) * 128], rhs=w[:, kc, :], start=(kc == 0), stop=(kc == KC - 1)), reads=[wk, ("xnT", tt)], writes=[ppk], pe_acc=True)
                    t_ = tmp[cnt % 2]
                    tk = ("tmp", cnt % 2)
                    cnt += 1
                    P.op("act", lambda e, pp=pp, t_=t_, tt=tt, cgw=cgw: e.activation(out=t_[:, 0:256], in_=pp[:, 0:256], func=AF.Gelu, accum_out=s1[:, tt, cgw:cgw + 1]), writes=[ppk, tk, "s1"])
                    P.op("act", lambda e, t_=t_, tt=tt, cgw=cgw: e.activation(out=junk[:, 0:256], in_=t_[:, 0:256], func=AF.Square, accum_out=s2[:, tt, cgw:cgw + 1]), reads=[tk], writes=["junk", "s2"])
                    P.op("dve", lambda e, t_=t_, tt=tt, cgw=cgw: e.tensor_copy(out=VB[:, tt, cgw * 256:(cgw + 1) * 256], in_=t_[:, 0:256]), reads=[tk], writes=[("VB", tt)])
                    if tt == 8:
                        P.op("pool", lambda e, t_=t_, cgw=cgw: e.tensor_copy(out=vs32[:, cgw * 256:(cgw + 1) * 256], in_=t_[:, 0:256]), reads=[tk], writes=["vs32"])
            P.op("dve", lambda e: e.reduce_sum(out=mu, in_=s1, axis=AX.X), reads=["s1"], writes=["mu"])
            P.op("dve", lambda e: e.reduce_sum(out=ex2, in_=s2, axis=AX.X), reads=["s2"], writes=["ex2"])
            P.op("dve", lambda e: e.tensor_scalar(out=mu, in0=mu, scalar1=1.0 / D, scalar2=None, op0=ALU.mult), writes=["mu"])
            P.op("dve", lambda e: e.tensor_tensor(out=rstd, in0=mu, in1=mu, op=ALU.mult), reads=["mu"], writes=["rstd"])
            P.op("dve", lambda e: e.scalar_tensor_tensor(out=rstd, in0=ex2, scalar=1.0 / D, in1=rstd, op0=ALU.mult, op1=ALU.subtract), reads=["ex2"], writes=["rstd"])
            P.op("act", lambda e: e.activation(out=rstd, in_=rstd, func=AF.Sqrt, bias=small[:, 8:9]), reads=["epsc"], writes=["rstd"])
            P.op("dve", lambda e: e.reciprocal(out=rstd, in_=rstd), writes=["rstd"])
            for tt in range(NT):
                src = vs32 if tt == 8 else VB[:, tt, :]
                sk = "vs32" if tt == 8 else ("VB", tt)
                P.op("dve", lambda e, tt=tt, src=src: e.tensor_scalar(out=src, in0=src, scalar1=mu[:, tt:tt + 1], scalar2=rstd[:, tt:tt + 1], op0=ALU.subtract, op1=ALU.mult), reads=["mu", "rstd"], writes=[sk])
                P.op("dve", lambda e, src=src: e.tensor_tensor(out=src, in0=src, in1=g_bc, op=ALU.mult), reads=["g_bc"], writes=[sk])
                if tt == 8:
                    P.op("dve", lambda e: e.tensor_tensor(out=vs32, in0=vs32, in1=b_bc, op=ALU.add), reads=["b_bc"], writes=["vs32"])
                    finals.append(P.op("sp", lambda e: e.dma_start(out=O["gv_s"], in_=vs32[0:4, :]), reads=["vs32"], dma=True))
                    P.op("dve", lambda e: e.tensor_copy(out=VB[:, 8, :], in_=vs32), reads=["vs32"], writes=[("VB", 8)])
                else:
                    P.op("dve", lambda e, src=src: e.tensor_tensor(out=src, in0=src, in1=b_bc, op=ALU.add), reads=["b_bc"], writes=[sk])
            for tt in range(NT):
                for g4 in range(4):
                    pp, ppk = next_pm()
                    for q in range(4):
                        g = g4 * 4 + q
                        P.op("pe", lambda e, pp=pp, q=q, g=g, tt=tt: e.matmul(pp[:, q * 128:(q + 1) * 128], lhsT=WsT[:, g, :], rhs=VB[:, tt, g * 128:(g + 1) * 128], start=True, stop=True), reads=["WsT", ("VB", tt)], writes=[ppk], pe_acc=True)
                    P.op("dve", lambda e, pp=pp, g4=g4, tt=tt: e.tensor_tensor(out=VB[:, tt, g4 * 512:(g4 + 1) * 512].rearrange("p (a b) -> p a b", a=4), in0=pp[:].rearrange("p (a b) -> p a b", a=4), in1=bc(bsc[:, g4 * 4:g4 * 4 + 4], 2, 128), op=ALU.add), reads=["bsc"], writes=[ppk, ("VB", tt)])
            for cgw in range(8):
                w = wb[cgw % 2]
                wk = ("wbk", cgw % 2)
                load_w(w, w_in, 0, KC, cgw * 256, 256, wk)
                for tt in range(NT):
                    pp, ppk = next_pm()
                    for kc in range(KC):
                        P.op("pe", lambda e, pp=pp, kc=kc, tt=tt, w=w: e.matmul(pp[:, 0:256], lhsT=xnT[:, kc, tt * 128:(tt + 1) * 128], rhs=w[:, kc, :], start=(kc == 0), stop=(kc == KC - 1)), reads=[wk, ("xnT", tt)], writes=[ppk], pe_acc=True)
                    t_ = tmp[cnt % 2]
                    tk = ("tmp", cnt % 2)
                    cnt += 1
                    P.op("act", lambda e, pp=pp, t_=t_: e.activation(out=t_[:, 0:256], in_=pp[:, 0:256], func=AF.Gelu), writes=[ppk, tk])
                    P.op("dve", lambda e, t_=t_, tt=tt, cgw=cgw: e.tensor_tensor(out=VB[:, tt, cgw * 256:(cgw + 1) * 256], in0=t_[:, 0:256], in1=VB[:, tt, cgw * 256:(cgw + 1) * 256], op=ALU.mult), reads=[tk], writes=[("VB", tt)])
            P.barrier()
            for tt in range(NT):
                for q in range(2):
                    pt, ptk = next_pt()
                    for jj in range(8):
                        kc = q * 8 + jj
                        P.op("pe", lambda e, pt=pt, jj=jj, kc=kc, tt=tt: e.transpose(out=pt[:, jj * 128:(jj + 1) * 128], in_=VB[:, tt, kc * 128:(kc + 1) * 128], identity=identb[:]), reads=[("VB", tt), "identb"], writes=[ptk], pe_acc=True)
                    P.op("act", lambda e, pt=pt, q=q, tt=tt: e.copy(out=xnT[:, q * 8:q * 8 + 8, tt * 128:(tt + 1) * 128], in_=pt.rearrange("p (a b) -> p a b", a=8)), writes=[ptk, ("xnT", tt)])
            for cg in range(4):
                load_w(wbig, I["gmlp_w_out"][0], 0, KC, cg * 512, 512, "wbig")
                for tt in range(NT):
                    pp, ppk = next_pm()
                    for kc in range(KC):
                        P.op("pe", lambda e, pp=pp, kc=kc, tt=tt: e.matmul(pp[:], lhsT=xnT[:, kc, tt * 128:(tt + 1) * 128], rhs=wbig[:, kc, :], start=(kc == 0), stop=(kc == KC - 1)), reads=["wbig", ("xnT", tt)], writes=[ppk], pe_acc=True)
                    P.op("dve", lambda e, pp=pp, tt=tt, cg=cg: e.tensor_tensor(out=R[:, tt, cg * 512:(cg + 1) * 512], in0=pp[:], in1=R[:, tt, cg * 512:(cg + 1) * 512], op=ALU.add), writes=[ppk, ("R", tt)])

        def final_out(ph, NT):
            P.barrier()
            cv = Carver()
            xs_scr = cv.bf16(D)
            gb = cv.f32(D)
            yo = [cv.f32(D) for _ in range(2)]
            P.op("sp", lambda e: e.dma_start(out=gb, in_=bass.AP(I["norm_final"].tensor, 0, [[0, 128], [1, D]])), writes=["gb"], dma=True)
            for tt in range(NT):
                rms_stats(tt)(xs_scr)
                y = yo[tt % 2]
                yk = ("yo", tt % 2)
                P.op("dve", lambda e, tt=tt, y=y: e.scalar_tensor_tensor(out=y, in0=R[:, tt, :], scalar=small[:, 2:3], in1=gb, op0=ALU.mult, op1=ALU.mult), reads=[("R", tt), "rstd", "gb"], writes=[yk])
                if tt < 8:
                    r0 = ph * TP + tt * 128
                    finals.append(P.op("sp", lambda e, y=y, r0=r0: e.dma_start(out=O["y_p"][r0:r0 + 128, :], in_=y), reads=[yk], dma=True))
                else:
                    finals.append(P.op("sp", lambda e, y=y: e.dma_start(out=O["y_s"], in_=y[0:4, :]), reads=[yk], dma=True))

        for ph in range(2):
            NT = 9 if ph == 0 else 8
            P.barrier()
            for tt in range(8):
                r0 = ph * TP + tt * 128
                P.op("sp", lambda e, tt=tt, r0=r0: e.dma_start(out=R[:, tt, :], in_=I["xp"][r0:r0 + 128, :]), writes=[("R", tt)], dma=True)
            if ph == 0:
                P.op("dve", lambda e: e.memset(R[:, 8, :], 0.0), writes=[("R", 8)])
                P.op("sp", lambda e: e.dma_start(out=R[0:4, 8, :], in_=I["xs"]), writes=[("R", 8)], dma=True)
            for li in cfg.get("layers", list(range(n_layers))):
                if do_mixer and li % 3 == 0:
                    ssd(li, ph, NT)
                if do_mixer and li % 3 == 1:
                    moba(li, ph, NT)
                if do_mixer and li % 3 == 2:
                    gmlp(li, ph, NT)
                if do_ffn:
                    ffn(li, NT)
            final_out(ph, NT)
        P.finish(finals)
    return nc


_NC_CACHE = {}


def _get_nc(cfg_key=()):
    if cfg_key not in _NC_CACHE:
        _NC_CACHE[cfg_key] = build(dict(cfg_key))
    return _NC_CACHE[cfg_key]


def kernel(**inp):
    cfg = inp.pop("_cfg", {})
    nc = _get_nc(tuple(sorted((k, tuple(v) if isinstance(v, list) else v) for k, v in cfg.items())))

    def f(a):
        return np.ascontiguousarray(np.asarray(a, dtype=np.float32))

    half = 16
    inv_freq = (np.float32(500000.0) ** (-np.arange(half, dtype=np.float32) * np.float32(2.0) / np.float32(32.0))).astype(np.float32)

    def rope_tab(pos):
        ang = pos.astype(np.float32)[:, None] * inv_freq[None, :]
        return np.concatenate([np.cos(ang), np.sin(ang)], axis=1).astype(np.float32)

    pos_p = np.arange(SEQ)
    cur = pos_p // 256
    kb = np.arange(8)
    shared = {
        "norm_mix": f(inp["norm_mix"]), "norm_ffn": f(inp["norm_ffn"]), "norm_final": f(inp["norm_final"]).reshape(1, D),
        "ssd_w_in": f(inp["ssd_w_in"]), "ssd_conv_w": f(inp["ssd_conv_w"]), "ssd_conv_b": f(inp["ssd_conv_b"]),
        "ssd_dt_bias": f(inp["ssd_dt_bias"]), "ssd_a_log": f(inp["ssd_a_log"]), "ssd_d": f(inp["ssd_d"]),
        "ssd_gate_norm": f(inp["ssd_gate_norm"]), "ssd_w_out": f(inp["ssd_w_out"]),
        "moba_w_qkv": f(inp["moba_w_qkv"]), "moba_w_o": f(inp["moba_w_o"]),
        "gmlp_w_in": f(inp["gmlp_w_in"]), "gmlp_ln_g": f(inp["gmlp_ln_g"]), "gmlp_ln_b": f(inp["gmlp_ln_b"]),
        "gmlp_w_s": f(inp["gmlp_w_s"]), "gmlp_b_s": f(inp["gmlp_b_s"]), "gmlp_w_out": f(inp["gmlp_w_out"]),
        "ffn_w_gate": f(inp["ffn_w_gate"]), "ffn_w_up": f(inp["ffn_w_up"]), "ffn_w_down": f(inp["ffn_w_down"]),
        "cache_k": None if cfg.get("stub_cache") else f(inp["cache_k"]).reshape(-1, D),
        "cache_v": None if cfg.get("stub_cache") else f(inp["cache_v"]).reshape(-1, D),
        "rope_p": rope_tab(pos_p), "rope_s": rope_tab(16384 + np.arange(4)),
        "valid_p": (kb[None, :] < cur[:, None]).astype(np.float32), "own_p": (kb[None, :] == cur[:, None]).astype(np.float32),
        "own_s": np.tile((np.arange(4)[None, :] <= np.arange(4)[:, None]).astype(np.float32), (16, 1)),
    }
    if cfg.get("stub_cache"):
        shared["cache_k"] = np.zeros((1, 128, 128), np.float32)
        shared["cache_v"] = np.zeros((1, 128, 128), np.float32)
    if cfg.get("stub_w"):
        for k in list(shared):
            if shared[k] is not None and k.split("_")[0] in ("ssd", "ffn", "moba", "gmlp", "cache") and shared[k].size > (1 << 22):
                shared[k] = np.zeros((1, 128, 128), np.float32)
    xp = f(inp["x_prompt"])
    xs = f(inp["x_sample"])
    sssm = f(inp["state_ssm"])
    sconv = f(inp["state_conv"])
    ptab = np.ascontiguousarray(np.asarray(inp["page_table"], dtype=np.int32))
    in_maps = []
    for c in range(NCORES):
        m = dict(shared)
        m["xp"] = xp[c % 4]
        m["xs"] = xs[c]
        m["state_ssm"] = np.ascontiguousarray(sssm[:, c].reshape(2, 4096, 128))
        m["state_conv"] = np.ascontiguousarray(sconv[:, c])
        m["page_table"] = ptab[c:c + 1]
        in_maps.append(m)
    res = run_bass_kernel_spmd(nc, in_maps, core_ids=list(range(NCORES))).results
    y_p = np.stack([res[c]["y_p"] for c in range(4)])
    y_s = np.stack([res[c]["y_s"] for c in range(8)])
    ssm_p = np.stack([res[c]["ssm_p"] for c in range(4)], axis=1).reshape(2, 4, 64, 64, 128)
    ssm_s = np.stack([res[c]["ssm_s"] for c in range(8)], axis=1).reshape(2, 8, 64, 64, 128)
    conv_p = np.stack([res[c]["conv_p"] for c in range(4)], axis=1)
    conv_s = np.stack([res[c]["conv_s"] for c in range(8)], axis=1)
    k_p = np.stack([res[c]["k_p"] for c in range(4)]).reshape(1, 4, SEQ, 16, 128)
    v_p = np.stack([res[c]["v_p"] for c in range(4)]).reshape(1, 4, SEQ, 16, 128)
    k_s = np.stack([res[c]["k_s"] for c in range(8)]).reshape(1, 8, 4, 16, 128)
    v_s = np.stack([res[c]["v_s"] for c in range(8)]).reshape(1, 8, 4, 16, 128)
    gv_s = np.stack([res[c]["gv_s"] for c in range(8)]).reshape(1, 8, 4, D)
    return (y_p, y_s, ssm_p, ssm_s, conv_p, conv_s, k_p, v_p, k_s, v_s, gv_s)
```
